# Optimizing a Trainium2 kernel written in Bass

```python
import math
import jax, jax.numpy as jnp
from jax import lax
import numpy as np

D_MODEL = 1024
BATCH = 16
SEQ = 2048
DEPTH = 1

D_MIX = D_MODEL
D_SSM = D_MIX // 2
D_DN = D_MIX - D_SSM
SSM_GROUP = 16
SSM_GROUPS = D_SSM // SSM_GROUP
SSM_STATE = 64
DN_HEAD_DIM = 128
DN_HEADS = D_DN // DN_HEAD_DIM
DN_CONV = 4
DN_CHUNK = 64
PLE_DIM = 256
NORM_EPS = 1e-6

SPLIT_1 = D_SSM
SPLIT_2 = SPLIT_1 + D_SSM
SPLIT_3 = SPLIT_2 + D_DN
SPLIT_4 = SPLIT_3 + D_DN
SPLIT_5 = SPLIT_4 + D_DN
SPLIT_6 = SPLIT_5 + D_DN
SPLIT_7 = SPLIT_6 + DN_HEADS
D_IN_PROJ = SPLIT_7 + DN_HEADS

kernel_name = "hymba_s5_gated_deltanet_block"


def rms_norm(x, g):
    xf = x.astype(jnp.float32)
    y = xf * lax.rsqrt(jnp.mean(xf * xf, axis=-1, keepdims=True) + NORM_EPS)
    return (y * g.astype(jnp.float32)).astype(x.dtype)


def l2_normalize(x):
    return x * lax.rsqrt(jnp.sum(x * x, axis=-1, keepdims=True) + NORM_EPS)


def _complex_linear_combine(earlier, later):
    a1r, a1i, b1r, b1i = earlier
    a2r, a2i, b2r, b2i = later
    ar = a2r * a1r - a2i * a1i
    ai = a2r * a1i + a2i * a1r
    br = a2r * b1r - a2i * b1i + b2r
    bi = a2r * b1i + a2i * b1r + b2i
    return (ar, ai, br, bi)


def s5_branch(u, A_re, A_im, B_re, B_im, C_re, C_im, D, log_dt, w_glu, b_glu):
    f32 = jnp.float32
    bsz, L, _ = u.shape
    uf = u.astype(f32)
    ug = uf.reshape(bsz, L, SSM_GROUPS, SSM_GROUP)
    dt = jnp.exp(log_dt.astype(f32))[:, None]
    lr = A_re.astype(f32)
    li = A_im.astype(f32)
    mag = jnp.exp(lr * dt)
    ang = li * dt
    ab_r = mag * jnp.cos(ang)
    ab_i = mag * jnp.sin(ang)
    den = lr * lr + li * li
    nr = ab_r - 1.0
    ni = ab_i
    cr = (nr * lr + ni * li) / den
    ci = (ni * lr - nr * li) / den
    Br = B_re.astype(f32)
    Bi = B_im.astype(f32)
    bb_r = cr[..., None] * Br - ci[..., None] * Bi
    bb_i = cr[..., None] * Bi + ci[..., None] * Br
    bu_r = jnp.einsum('gph,blgh->blgp', bb_r, ug)
    bu_i = jnp.einsum('gph,blgh->blgp', bb_i, ug)
    a_r = jnp.broadcast_to(ab_r, (1, L) + ab_r.shape)
    a_i = jnp.broadcast_to(ab_i, (1, L) + ab_i.shape)
    _, _, s_r, s_i = lax.associative_scan(_complex_linear_combine, (a_r, a_i, bu_r, bu_i), axis=1)
    y = (jnp.einsum('ghp,blgp->blgh', C_re.astype(f32), s_r)
         - jnp.einsum('ghp,blgp->blgh', C_im.astype(f32), s_i))
    y = y.reshape(bsz, L, D_SSM) + D.astype(f32) * uf
    y = jax.nn.gelu(y)
    y = y * jax.nn.sigmoid(y @ w_glu.astype(f32) + b_glu.astype(f32))
    return y


def causal_depthwise_conv(x, w):
    K, C = w.shape
    return lax.conv_general_dilated(x, w[:, None, :], window_strides=(1,), padding=((K - 1, 0),),
                                    dimension_numbers=('NWC', 'WIO', 'NWC'), feature_group_count=C)


def chunk_gated_delta_rule(q, k, v, g, beta):
    f32 = jnp.float32
    bsz, L, H, Dk = q.shape
    Dv = v.shape[-1]
    C = DN_CHUNK
    N = L // C

    def to_chunks(t):
        return t.reshape(bsz, N, C, H, -1).transpose(0, 1, 3, 2, 4)

    q, k, v = to_chunks(q), to_chunks(k), to_chunks(v)
    g = g.reshape(bsz, N, C, H).transpose(0, 1, 3, 2)
    beta = beta.reshape(bsz, N, C, H).transpose(0, 1, 3, 2)
    g = jnp.cumsum(g, axis=-1)
    causal = jnp.tril(jnp.ones((C, C), dtype=bool))
    strict = jnp.tril(jnp.ones((C, C), dtype=bool), k=-1)
    diff = g[..., :, None] - g[..., None, :]
    decay = jnp.where(causal, jnp.exp(jnp.where(causal, diff, 0.0)), 0.0)
    k_beta = k * beta[..., None]
    v_beta = v * beta[..., None]
    lower = jnp.where(strict, jnp.einsum('bnhid,bnhjd->bnhij', k_beta, k) * decay, 0.0)
    tri = lower + jnp.eye(C, dtype=f32)
    rhs = jnp.concatenate([v_beta, k_beta * jnp.exp(g)[..., None]], axis=-1)
    sol = lax.linalg.triangular_solve(tri, rhs, left_side=True, lower=True, unit_diagonal=True)
    u = sol[..., :Dv]
    w = sol[..., Dv:]
    attn_intra = jnp.where(causal, jnp.einsum('bnhid,bnhjd->bnhij', q, k) * decay, 0.0)

    def step(S, inp):
        q_c, k_c, u_c, w_c, g_c, a_c = inp
        v_new = u_c - jnp.einsum('bhck,bhkv->bhcv', w_c, S)
        o = (jnp.einsum('bhck,bhkv->bhcv', q_c * jnp.exp(g_c)[..., None], S)
             + jnp.einsum('bhij,bhjv->bhiv', a_c, v_new))
        g_last = g_c[..., -1]
        k_dec = k_c * jnp.exp(g_last[..., None] - g_c)[..., None]
        S = S * jnp.exp(g_last)[..., None, None] + jnp.einsum('bhck,bhcv->bhkv', k_dec, v_new)
        return S, o

    xs = tuple(jnp.moveaxis(t, 1, 0) for t in (q, k, u, w, g, attn_intra))
    S0 = jnp.zeros((bsz, H, Dk, Dv), dtype=f32)
    _, o = lax.scan(step, S0, xs)
    o = jnp.moveaxis(o, 0, 1).transpose(0, 1, 3, 2, 4).reshape(bsz, L, H, Dv)
    return o


def gated_deltanet_branch(q, k, v, z, b_raw, a_raw, conv_w, A_log, dt_bias, norm_g):
    f32 = jnp.float32
    bsz, L, _ = q.shape
    qkv = jnp.concatenate([q, k, v], axis=-1).astype(f32)
    qkv = jax.nn.silu(causal_depthwise_conv(qkv, conv_w.astype(f32)))
    q, k, v = jnp.split(qkv, [D_DN, 2 * D_DN], axis=-1)
    q = l2_normalize(q.reshape(bsz, L, DN_HEADS, DN_HEAD_DIM)) * (DN_HEAD_DIM ** -0.5)
    k = l2_normalize(k.reshape(bsz, L, DN_HEADS, DN_HEAD_DIM))
    v = v.reshape(bsz, L, DN_HEADS, DN_HEAD_DIM)
    beta = jax.nn.sigmoid(b_raw.astype(f32))
    g = -jnp.exp(A_log.astype(f32)) * jax.nn.softplus(a_raw.astype(f32) + dt_bias.astype(f32))
    o = chunk_gated_delta_rule(q, k, v, g, beta)
    zf = z.astype(f32).reshape(bsz, L, DN_HEADS, DN_HEAD_DIM)
    o = rms_norm(o, norm_g) * jax.nn.silu(zf)
    return o.reshape(bsz, L, D_DN)


def setup_inputs(seed: int = 0) -> dict:
    key = jax.random.key(seed)
    ks = jax.random.split(key, 24)
    f32 = jnp.float32
    nrm = lambda k, s, sc: jax.random.normal(k, s, f32) * sc
    x = jax.random.normal(ks[0], (BATCH, SEQ, D_MODEL), f32)
    p = jax.random.normal(ks[1], (DEPTH, BATCH, SEQ, PLE_DIM), f32)
    norm_mix_g = 1.0 + nrm(ks[2], (DEPTH, D_MODEL), 0.02)
    w_in = nrm(ks[3], (DEPTH, D_MODEL, D_IN_PROJ), D_MODEL ** -0.5)
    n_idx = jnp.arange(SSM_STATE, dtype=f32)
    ssm_A_re = -0.5 + nrm(ks[4], (DEPTH, SSM_GROUPS, SSM_STATE), 0.01)
    ssm_A_im = math.pi * n_idx + nrm(ks[5], (DEPTH, SSM_GROUPS, SSM_STATE), 0.01)
    ssm_B_re = nrm(ks[6], (DEPTH, SSM_GROUPS, SSM_STATE, SSM_GROUP), (2 * SSM_GROUP) ** -0.5)
    ssm_B_im = nrm(ks[7], (DEPTH, SSM_GROUPS, SSM_STATE, SSM_GROUP), (2 * SSM_GROUP) ** -0.5)
    ssm_C_re = nrm(ks[8], (DEPTH, SSM_GROUPS, SSM_GROUP, SSM_STATE), SSM_STATE ** -0.5)
    ssm_C_im = nrm(ks[9], (DEPTH, SSM_GROUPS, SSM_GROUP, SSM_STATE), SSM_STATE ** -0.5)
    ssm_D = nrm(ks[10], (DEPTH, D_SSM), 1.0)
    ssm_log_dt = jax.random.uniform(ks[11], (DEPTH, SSM_GROUPS), f32, math.log(1e-3), math.log(1e-1))
    ssm_w_glu = nrm(ks[12], (DEPTH, D_SSM, D_SSM), D_SSM ** -0.5)
    ssm_b_glu = nrm(ks[13], (DEPTH, D_SSM), 0.01)
    dn_conv_w = nrm(ks[14], (DEPTH, DN_CONV, 3 * D_DN), DN_CONV ** -0.5)
    dn_A_log = jnp.log(jax.random.uniform(ks[15], (DEPTH, DN_HEADS), f32, 1.0, 16.0))
    dt0 = jnp.exp(jax.random.uniform(ks[16], (DEPTH, DN_HEADS), f32, math.log(1e-3), math.log(1e-1)))
    dn_dt_bias = dt0 + jnp.log(-jnp.expm1(-dt0))
    dn_norm_g = 1.0 + nrm(ks[17], (DEPTH, DN_HEAD_DIM), 0.02)
    w_out = nrm(ks[18], (DEPTH, D_MIX, D_MODEL), D_MIX ** -0.5)
    w_ple_proj = nrm(ks[19], (DEPTH, PLE_DIM, D_MODEL), PLE_DIM ** -0.5)
    ple_norm_g = 1.0 + nrm(ks[20], (DEPTH, D_MODEL), 0.02)
    w_ple_gate = nrm(ks[21], (DEPTH, D_MODEL, D_MODEL), D_MODEL ** -0.5)
    final_norm_g = 1.0 + nrm(ks[22], (D_MODEL,), 0.02)
    return {"x": x, "p": p, "norm_mix_g": norm_mix_g, "w_in": w_in,
            "ssm_A_re": ssm_A_re, "ssm_A_im": ssm_A_im, "ssm_B_re": ssm_B_re, "ssm_B_im": ssm_B_im,
            "ssm_C_re": ssm_C_re, "ssm_C_im": ssm_C_im, "ssm_D": ssm_D, "ssm_log_dt": ssm_log_dt,
            "ssm_w_glu": ssm_w_glu, "ssm_b_glu": ssm_b_glu,
            "dn_conv_w": dn_conv_w, "dn_A_log": dn_A_log, "dn_dt_bias": dn_dt_bias, "dn_norm_g": dn_norm_g,
            "w_out": w_out, "w_ple_proj": w_ple_proj, "ple_norm_g": ple_norm_g, "w_ple_gate": w_ple_gate,
            "final_norm_g": final_norm_g}


def reference(x, p, norm_mix_g, w_in, ssm_A_re, ssm_A_im, ssm_B_re, ssm_B_im, ssm_C_re, ssm_C_im,
              ssm_D, ssm_log_dt, ssm_w_glu, ssm_b_glu, dn_conv_w, dn_A_log, dn_dt_bias, dn_norm_g,
              w_out, w_ple_proj, ple_norm_g, w_ple_gate, final_norm_g):
    h = x
    for i in range(DEPTH):
        a = rms_norm(h, norm_mix_g[i])
        proj = a @ w_in[i]
        u_s, z_s, q, k, v, z_d, b_raw, a_raw = jnp.split(
            proj, [SPLIT_1, SPLIT_2, SPLIT_3, SPLIT_4, SPLIT_5, SPLIT_6, SPLIT_7], axis=-1)
        y_s = s5_branch(u_s, ssm_A_re[i], ssm_A_im[i], ssm_B_re[i], ssm_B_im[i], ssm_C_re[i],
                        ssm_C_im[i], ssm_D[i], ssm_log_dt[i], ssm_w_glu[i], ssm_b_glu[i])
        y_s = y_s * jax.nn.silu(z_s.astype(jnp.float32))
        y_d = gated_deltanet_branch(q, k, v, z_d, b_raw, a_raw, dn_conv_w[i], dn_A_log[i],
                                    dn_dt_bias[i], dn_norm_g[i])
        mix = jnp.concatenate([y_s, y_d], axis=-1).astype(h.dtype) @ w_out[i]
        h = h + mix
        e = rms_norm(p[i] @ w_ple_proj[i], ple_norm_g[i])
        h = h + jax.nn.sigmoid(h @ w_ple_gate[i]) * e
    return rms_norm(h, final_norm_g)
```

```python
import math
from contextlib import ExitStack
import numpy as np
import ml_dtypes
import concourse.bass as bass
import concourse.mybir as mybir
from concourse.bass_utils import run_bass_kernel_spmd

F32 = mybir.dt.float32
BF16 = mybir.dt.bfloat16
I32 = mybir.dt.int32
AF = mybir.ActivationFunctionType
ALU = mybir.AluOpType
ENG = ['pe', 'act', 'dve', 'pool', 'sp']
NCORES = 8
SEQ = 2048
NT_SEQ = SEQ // 128
EPS = 1e-6
TWO_PI = 6.283185


class Sched:
    def __init__(self, nc, sems, dma_sems):
        self.nc = nc
        self.sem = sems
        self.cnt = {e: 0 for e in ENG}
        self.waited = {e: {} for e in ENG}
        self.prog = {e: [] for e in ENG}
        self.lastw = {}
        self.readers = {}
        self.dma_sems = dma_sems
        self.dma_cnt = [0] * len(dma_sems)
        self.dma_rr = 0
        self.group_keys = {}
        self.cur_group = None
        self.phase_tokens = {}
        self.phase_seen = set()
        self.capture = None

    def begin_capture(self):
        self.capture = []
        return self.capture

    def end_capture(self):
        self.capture = None

    def issue_merged(self, streams, grain=24, lead=None):
        if lead is None:
            lead = [0.0] * len(streams)
        lead = [l for l, s in zip(lead, streams) if s]
        streams = [s for s in streams if s]
        idx = [0] * len(streams)
        while True:
            best, bf = None, None
            for i, s in enumerate(streams):
                if idx[i] < len(s):
                    f = idx[i] / len(s) - lead[i]
                    if best is None or f < bf:
                        best, bf = i, f
            if best is None:
                break
            for _ in range(grain):
                if idx[best] >= len(streams[best]):
                    break
                kind, a, b, c, d = streams[best][idx[best]]
                idx[best] += 1
                if kind == 'op':
                    self.opn(a, b, c, d)
                else:
                    self.dma(a, b, c, d)

    def barrier(self):
        for E in ENG:
            self.drain_all(E)

    def begin_phase(self, group, aliases):
        toks = {}
        for g in aliases:
            for k in self.group_keys.get(g, ()):
                for tok in [self.lastw.get(k)] + list(self.readers.get(k, ())):
                    if tok is not None and toks.get(tok[0], 0) < tok[1]:
                        toks[tok[0]] = tok[1]
        self.cur_group = group
        self.phase_tokens = toks
        self.phase_seen = set()
        self.group_keys.setdefault(group, set())

    def _semh(self, key):
        if isinstance(key, tuple):
            return self.dma_sems[key[1]]
        return self.sem[key]

    def _deps(self, E, reads, writes):
        deps = {}

        def add(tok, raw):
            if tok is None:
                return
            k, v = tok
            if k == E and E == 'pe':
                return
            if deps.get(k, 0) < v:
                deps[k] = v
        for r in reads:
            add(self.lastw.get(r), True)
            if isinstance(r, str) and r.startswith('ps') and r[2:].isdigit():
                for t in self.readers.get(r, ()):
                    if t[0] != E:
                        add(t, False)
        for w in writes:
            add(self.lastw.get(w), False)
            for t in self.readers.get(w, ()):
                add(t, False)
            if w not in self.phase_seen:
                self.phase_seen.add(w)
                for k, v in self.phase_tokens.items():
                    if not (k == E and E == 'pe'):
                        if deps.get(k, 0) < v:
                            deps[k] = v
        if self.cur_group is not None:
            gk = self.group_keys[self.cur_group]
            gk.update(reads)
            gk.update(writes)
        return deps

    def _emit_waits(self, E, deps):
        for k, v in deps.items():
            if self.waited[E].get(k, 0) >= v:
                continue
            self.waited[E][k] = v
            h = self._semh(k)
            self.prog[E].append(lambda eng, h=h, v=v: eng.wait_ge(h, v))

    def _commit(self, tok, reads, writes):
        for r in reads:
            self.readers.setdefault(r, []).append(tok)
        for w in writes:
            self.lastw[w] = tok
            self.readers[w] = []

    def op(self, E, fn, reads=(), writes=()):
        self.opn(E, [fn], reads, writes)

    def opn(self, E, fns, reads=(), writes=()):
        if self.capture is not None:
            self.capture.append(('op', E, fns, tuple(reads), tuple(writes)))
            return
        deps = self._deps(E, reads, writes)
        self._emit_waits(E, deps)
        self.cnt[E] += 1
        v = self.cnt[E]
        h = self.sem[E]
        for f in fns[:-1]:
            self.prog[E].append(lambda eng, f=f: f(eng))
        self.prog[E].append(lambda eng, fn=fns[-1], h=h: fn(eng).then_inc(h, 1))
        self._commit((E, v), reads, writes)

    def dma(self, Q, fn, reads=(), writes=()):
        if self.capture is not None:
            self.capture.append(('dma', Q, fn, tuple(reads), tuple(writes)))
            return
        i = self.dma_rr
        self.dma_rr = (self.dma_rr + 1) % len(self.dma_sems)
        deps = self._deps(Q, reads, writes)
        if self.dma_cnt[i] > 0:
            k = ('dma', i)
            deps[k] = max(deps.get(k, 0), 16 * self.dma_cnt[i])
        self._emit_waits(Q, deps)
        self.dma_cnt[i] += 1
        v = 16 * self.dma_cnt[i]
        h = self.dma_sems[i]
        self.prog[Q].append(lambda eng, fn=fn, h=h: fn(eng).then_inc(h, 16))
        self._commit((('dma', i), v), reads, writes)

    def finish(self, E, res):
        deps = {}
        for r in res:
            tok = self.lastw.get(r)
            if tok is not None:
                k, v = tok
                if deps.get(k, 0) < v:
                    deps[k] = v
        self._emit_waits(E, deps)

    def drain_all(self, E):
        deps = {}
        for i, c in enumerate(self.dma_cnt):
            if c:
                deps[('dma', i)] = 16 * c
        for e in ENG:
            if e != E and self.cnt[e]:
                deps[e] = self.cnt[e]
        self._emit_waits(E, deps)

    def replay(self, block):
        prog = self.prog

        @block.tensor
        def _(e):
            for f in prog['pe']:
                f(e)

        @block.scalar
        def _(e):
            for f in prog['act']:
                f(e)

        @block.vector
        def _(e):
            for f in prog['dve']:
                f(e)

        @block.gpsimd
        def _(e):
            for f in prog['pool']:
                f(e)

        @block.sync
        def _(e):
            for f in prog['sp']:
                f(e)


def host_consts():
    c = {}
    c['identb'] = np.eye(128, dtype=ml_dtypes.bfloat16)
    c['identf'] = np.eye(128, dtype=np.float32)
    t = np.arange(128)
    same = (t[:, None] // 64) == (t[None, :] // 64)
    c['tri'] = (same & (t[:, None] <= t[None, :])).astype(np.float32)
    c['blk'] = same.astype(np.float32)
    c['onesA'] = np.repeat((t[:, None] < 64), 128, axis=1).astype(np.float32)
    c['onesB'] = np.repeat((t[:, None] >= 64), 128, axis=1).astype(np.float32)
    i = np.arange(64)
    c['i2'] = ((t[:, None] % 64) == i[None, :]).astype(np.float32)
    pj = (t % 64)[:, None]
    m = np.zeros((128, 3, 64), np.float32)
    m[:, 0, :] = np.where(i[None, :] >= pj, 0.0, -30000.0)
    m[:, 1, :] = np.where(i[None, :] > pj, 0.0, -30000.0)
    m[:, 2, :] = np.where(i[None, :] < pj, 0.0, -30000.0)
    c['maskneg'] = m
    jj = t // 16
    c['toepmask'] = (jj[None, :] >= jj[:, None]).astype(np.float32)
    c['tau8'] = np.ascontiguousarray(np.broadcast_to(np.arange(1, 9, dtype=np.float32)[None, None, :], (128, 16, 8)))
    c['cidx'] = np.ascontiguousarray(np.broadcast_to(np.arange(0, 17, dtype=np.float32)[None, None, :], (128, 16, 17)))
    r0 = np.ones((128, 16, 16), np.float32)
    r0[:, :, 0] = 0.0
    c['rho0'] = r0
    sel = np.zeros((128, 8, 16), np.float32)
    for tok in range(128):
        sel[tok, tok % 8, tok // 8] = 1.0
    c['selj'] = sel.astype(ml_dtypes.bfloat16)
    c['onesb'] = np.ones((128, 2), dtype=ml_dtypes.bfloat16)
    return c


CONST_SHAPES = {'identb': ([128, 128], BF16), 'identf': ([128, 128], F32), 'tri': ([128, 128], F32),
                'blk': ([128, 128], F32), 'onesA': ([128, 128], F32), 'onesB': ([128, 128], F32),
                'i2': ([128, 64], F32), 'maskneg': ([128, 3, 64], F32), 'toepmask': ([128, 128], F32),
                'tau8': ([128, 16, 8], F32), 'cidx': ([128, 16, 17], F32), 'rho0': ([128, 16, 16], F32),
                'selj': ([128, 8, 16], BF16), 'onesb': ([128, 2], BF16)}


def build_nc(ntiles=2 * NT_SEQ, dbg=False, stage=99):
    nc = bass.Bass("TRN2", target_bir_lowering=False)
    DI = lambda n, s, d=F32: nc.dram_tensor(n, s, d, kind="ExternalInput").ap()
    ntok = 2 * SEQ
    x = DI("x", [ntok, 1024]); p = DI("p", [ntok, 256])
    norm_mix_g = DI("norm_mix_g", [1024]); w_in = DI("w_in", [1024, 3080])
    A_re = DI("ssm_A_re", [32, 64]); A_im = DI("ssm_A_im", [32, 64])
    B_re = DI("ssm_B_re", [32, 64, 16]); B_im = DI("ssm_B_im", [32, 64, 16])
    C_re = DI("ssm_C_re", [32, 16, 64]); C_im = DI("ssm_C_im", [32, 16, 64])
    ssm_D = DI("ssm_D", [512]); log_dt = DI("ssm_log_dt", [32])
    w_glu = DI("ssm_w_glu", [512, 512]); b_glu = DI("ssm_b_glu", [512])
    conv_w = DI("dn_conv_w", [4, 1536]); A_log = DI("dn_A_log", [4]); dt_bias = DI("dn_dt_bias", [4])
    dn_norm_g = DI("dn_norm_g", [128]); w_out = DI("w_out", [1024, 1024])
    w_pp = DI("w_ple_proj", [256, 1024]); ple_g = DI("ple_norm_g", [1024])
    w_pg = DI("w_ple_gate", [1024, 1024]); fin_g = DI("final_norm_g", [1024])
    cd = {k: DI("c_" + k, s, d) for k, (s, d) in CONST_SHAPES.items()}
    out = nc.dram_tensor("out", [ntok, 1024], F32, kind="ExternalOutput").ap()

    es = ExitStack()
    with es:
        T = lambda n, s, d=F32: es.enter_context(nc.sbuf_tensor(n, s, d))
        wglu = T("wglu", [128, 4, 512], BF16)
        wba = T("wba", [128, 8, 8], BF16)
        NSLOT = 4
        wslot = [T("wslot%d" % i, [128, 8, 512], BF16) for i in range(NSLOT)]
        wscr = nc.dram_tensor("wscr", [11, 8, 128, 512], BF16, kind="Internal").ap()
        C = {k: T("k_" + k, s, d) for k, (s, d) in CONST_SHAPES.items()}
        toepT = T("toepT", [128, 32, 128], BF16)
        ptre = T("ptre", [128, 32, 64], BF16); ptim = T("ptim", [128, 32, 64], BF16)
        gqre = T("gqre", [128, 16, 256], BF16); gqimn = T("gqimn", [128, 16, 256], BF16)
        cosT = T("cosT", [128, 16, 17]); sinT = T("sinT", [128, 16, 17])
        rhoS = T("rhoS", [128, 16, 16]); rho = T("rho", [128, 16])
        diag = T("diag", [128, 4, 12, 128], BF16)
        gcol = T("gcol", [128, 8]); Dcol = T("Dcol", [128, 4]); bglu = T("bglu", [128, 4])
        cwcol = T("cwcol", [128, 4, 12])
        fing_b = T("fing_b", [128, 1024]); pleg_b = T("pleg_b", [128, 1024]); dng_b = T("dng_b", [128, 128])
        negA_b = T("negA_b", [128, 4]); dtb_b = T("dtb_b", [128, 4])
        carr = T("carr", [128, 2, 16])
        qkvp = T("qkvp", [128, 12, 131], BF16)
        S32 = T("S32", [128, 4, 128]); Sbf = T("Sbf", [128, 4, 128], BF16)
        pt = T("pt", [128, 256])
        st = T("st", [128, 8]); ssd = T("ssd", [128, 4]); rsd = T("rsd", [128, 4])
        zsf = T("zsf", [128, 4, 128], BF16); ysf = T("ysf", [128, 4, 128], BF16)
        qkv = T("qkv", [128, 12, 128], BF16); zd = T("zd", [128, 4, 128], BF16)
        ydf = T("ydf", [128, 4, 128], BF16)
        RA, RB_ = 64 * 1024, 26 * 1024
        arena = T("arena", [128, (RA + RB_) // 4])

        class Region:
            def __init__(self, base, size):
                self.base, self.size, self.off = base, size, 0

            def reset(self):
                self.off = 0
                return self

            def alloc(self, shape, dtype=F32):
                esz = 2 if dtype == BF16 else 4
                n = int(np.prod(shape[1:]))
                nb = (n * esz + 31) // 32 * 32
                assert self.off + nb <= self.size, (self.off, nb, self.size)
                b0 = self.base + self.off
                self.off += nb
                a = arena[0:shape[0], b0 // 4:(b0 + nb) // 4]
                if dtype != F32:
                    a = a.bitcast(dtype)
                a = a[:, 0:n]
                fs = list(shape[1:])
                if len(fs) == 2:
                    a = a.rearrange("p (a b) -> p a b", a=fs[0])
                elif len(fs) == 3:
                    a = a.rearrange("p (a b c) -> p a b c", a=fs[0], b=fs[1])
                elif len(fs) == 4:
                    a = a.rearrange("p (a b c d) -> p a b c d", a=fs[0], b=fs[1], c=fs[2])
                return a
        regA = Region(0, RA); regB = Region(RA, RB_); regS = Region(0, RA + RB_)
        regB.reset()
        xt = regB.alloc([128, 1024]); xn = regB.alloc([128, 1024], BF16); aT = regB.alloc([128, 8, 128], BF16)
        h1 = regB.alloc([128, 1024]); h1b = regB.alloc([128, 1024], BF16); h1T = regB.alloc([128, 8, 128], BF16)
        ee = regB.alloc([128, 1024]); sg = regB.alloc([128, 1024]); ot = ee
        ptb = regB.alloc([128, 256], BF16); pT = regB.alloc([128, 2, 128], BF16)
        regA.reset()
        u32 = regA.alloc([128, 4, 128]); ubf = regA.alloc([128, 4, 128], BF16)
        utok = regA.alloc([128, 512], BF16); cmf = regA.alloc([16, 4096], BF16); ucm = cmf.rearrange('p (g j i) -> p g j i', g=32, j=8); ycm = cmf.rearrange('p (j c) -> p j c', j=8)
        ug = regA.alloc([128, 32, 16], BF16)
        zin = regA.alloc([128, 2, 16, 16]); ztmp = regA.alloc([128, 4, 16, 16]); zs_ = regA.alloc([128, 2, 16, 16])
        snx = regA.alloc([128, 2, 16, 16]); sbf5 = regA.alloc([128, 2, 16, 16], BF16); ctmp = regA.alloc([128, 2, 16])
        yfm = regA.alloc([128, 4, 128]); y1 = regA.alloc([128, 4, 128], BF16)
        gate = regA.alloc([128, 4, 128]); y2 = gate
        sq = regA.alloc([128, 8, 128], BF16); kvt = regA.alloc([128, 8, 128], BF16)
        ba = regA.alloc([128, 16]); cum = regA.alloc([128, 16])
        sc_ = {n: regA.alloc([128, 4]) for n in ['sigb', 'lnb', 'ex', 'sp', 'g', 'c1', 'r1', 'r2',
                                                  'kbgs', 'kds', 'so', 'eglA', 'eglB', 'tmp']}
        lnr = regA.alloc([128, 8])
        RB = regA.alloc([128, 4, 3, 64]); Mx = regA.alloc([128, 4, 3, 64]); Bm = regA.alloc([128, 4, 3, 64])
        attnT = regA.alloc([128, 4, 64], BF16)
        NP = [regA.alloc([128, 4, 2, 64]) for i in range(2)]
        NTt = [regA.alloc([128, 4, 64]) for i in range(2)]
        Pf = regA.alloc([128, 4, 64], BF16)
        vb = regA.alloc([128, 4, 128], BF16); kbg = regA.alloc([128, 4, 128], BF16); kd = regA.alloc([128, 4, 128], BF16)
        ug32 = regA.alloc([128, 4, 128]); wT = regA.alloc([128, 4, 2, 64], BF16)
        vnew = regA.alloc([128, 4, 128], BF16); otmp = regA.alloc([128, 4, 128]); otok = regA.alloc([128, 4, 128])
        t1 = regA.alloc([128, 4, 128]); ydt = regA.alloc([128, 4, 128], BF16)
        regS.reset()
        NSTG = 8
        stg = [regS.alloc([128, 512]) for i in range(NSTG)]
        stgb = [regS.alloc([128, 512], BF16) for i in range(NSTG)]
        cst = regS.alloc([128, 128])
        s5 = {n: regS.alloc([128, 16]) for n in ['lr', 'li', 'ldt', 'dt', 'lrdt', 'lidt', 'nr', 'ni', 'den',
                                                  'cr', 'ci', 't0', 't1', 'phr']}
        s8 = {n: regS.alloc([128, 16, 8]) for n in ['magl', 'ang', 'mag', 'imag', 'sn', 'cs', 'pwr', 'pwi',
                                                     'ipr', 'ipi', 'a', 'b', 'c', 'd']}
        s8i = regS.alloc([128, 16, 8], I32)
        s17 = {n: regS.alloc([128, 16, 17]) for n in ['ang', 'a', 'b', 'c', 'd']}
        s17i = regS.alloc([128, 16, 17], I32)
        Bt = {n: regS.alloc([128, 16, 16]) for n in ['bre', 'bim', 'cre', 'cim', 'bbre', 'bbim', 'x', 'y']}
        H = {n: regS.alloc([128, 4, 8, 16]) for n in ['hre', 'him', 'gre', 'gim', 'h2re', 'h2im', 'x', 'y']}
        PS = [es.enter_context(nc.psum_tensor("ps%d" % i, [128, 512], F32)) for i in range(8)]
        sems = {e: es.enter_context(nc.semaphore("s_" + e)) for e in ENG}
        dsems = [es.enter_context(nc.semaphore("d%d" % i)) for i in range(16)]
        block = es.enter_context(nc.Block())
        S = Sched(nc, sems, dsems)
        psrr = [0]
        dbg_outs = []

        def DBG(name, ap, key, shape, dtype):
            if not dbg:
                return
            dt_ = nc.dram_tensor("dbg_" + name, shape, dtype, kind="ExternalOutput").ap()
            S.dma('sp', lambda e: e.dma_start(out=dt_, in_=ap), reads=[key], writes=['dbg_' + name])
            dbg_outs.append('dbg_' + name)

        bankset = {'cur': [0, 1, 2, 3, 4, 5, 6], 'pos': {}}

        def set_banks(lst):
            bankset['cur'] = lst

        def bank():
            lst = bankset['cur']
            k = tuple(lst)
            p_ = bankset['pos'].get(k, 0)
            bankset['pos'][k] = (p_ + 1) % len(lst)
            i = lst[p_]
            return PS[i], 'ps%d' % i

        wrr = {'A': 0, 'D': 0}
        wset = {'A': [0, 1], 'D': [2, 3]}

        def wget(chunk, who):
            i = wset[who][wrr[who]]
            wrr[who] = (wrr[who] + 1) % len(wset[who])
            sl = wslot[i]
            nkb = 4 if chunk == 10 else 8
            S.dma('sp', lambda e, sl=sl, chunk=chunk, nkb=nkb: e.dma_start(out=sl[:, 0:nkb, :], in_=wscr[chunk, 0:nkb].rearrange("kb p n -> p kb n")), reads=['wscr%d_%d' % (chunk, kq) for kq in range(nkb)], writes=['wslot%d' % i])
            return sl, 'wslot%d' % i

        def psf(b, *shape):
            n = int(np.prod(shape))
            a = b[:, 0:n]
            if len(shape) == 2:
                return a.rearrange("p (a b) -> p a b", a=shape[0])
            if len(shape) == 3:
                return a.rearrange("p (a b c) -> p a b c", a=shape[0], b=shape[1])
            return a

        def psb(b, *shape):
            n = int(np.prod(shape))
            a = b[:].bitcast(BF16)[:, 0:n]
            if len(shape) == 2:
                return a.rearrange("p (a b) -> p a b", a=shape[0])
            if len(shape) == 3:
                return a.rearrange("p (a b c) -> p a b c", a=shape[0], b=shape[1])
            return a

        TT = lambda E, o, a, b, op, r, w: S.op(E, lambda e: e.tensor_tensor(out=o, in0=a, in1=b, op=op), reads=r, writes=w)
        TS = lambda E, o, a, s1, s2, o0, o1, r, w: S.op(E, lambda e: e.tensor_scalar(out=o, in0=a, scalar1=s1, scalar2=s2, op0=o0, op1=o1) if o1 is not None else e.tensor_scalar(out=o, in0=a, scalar1=s1, scalar2=None, op0=o0), reads=r, writes=w)
        STT = lambda o, a, s, b, o0, o1, r, w: S.op('dve', lambda e: e.scalar_tensor_tensor(out=o, in0=a, scalar=s, in1=b, op0=o0, op1=o1), reads=r, writes=w)
        ACT = lambda o, a, f, r, w, bias=None, scale=None: S.op('act', lambda e: e.activation(out=o, in_=a, func=f, **({'bias': bias} if bias is not None else {}), **({'scale': scale} if scale is not None else {})), reads=r, writes=w)
        CP = lambda E, o, a, r, w: S.op(E, lambda e: e.tensor_copy(out=o, in_=a), reads=r, writes=w)
        MM = lambda o, l, rh, st_, sp_, r, w: S.op('pe', lambda e: e.matmul(o, lhsT=l, rhs=rh, start=st_, stop=sp_), reads=r, writes=w)
        MMS = lambda o, l, rh, st_, sp_, r, w: S.op('pe', lambda e: e.matmul(o, lhsT=l, rhs=rh, start=st_, stop=sp_, skip_group_check=True), reads=r, writes=w)
        TR = lambda o, a, idt, r, w: S.op('pe', lambda e: e.transpose(out=o, in_=a, identity=idt), reads=r, writes=w)
        LD = lambda o, a, w, q='sp': S.dma(q, lambda e: e.dma_start(out=o, in_=a), writes=w)
        LDS = lambda o, a, w, q='sp': S.dma(q, lambda e: e.dma_start(out=o, in_=a, allow_slow_non_contiguous=True), writes=w)

        caps_setup = {}

        def do_setup():
            S.begin_phase('SETUP', [])
            for k in CONST_SHAPES:
                LD(C[k][:], cd[k], ['C'])
            LDS(gcol[:], norm_mix_g.rearrange("(kb p) -> p kb", p=128), ['gcol'])
            LDS(Dcol[:], ssm_D.rearrange("(kb p) -> p kb", p=128), ['Dcol'])
            LDS(bglu[:], b_glu.rearrange("(kb p) -> p kb", p=128), ['bglu'])
            TS('dve', bglu[:], bglu[:], 0.5, None, ALU.mult, None, ['bglu'], ['bglu'])
            LDS(cwcol[:], conv_w.rearrange("t (b p) -> p t b", p=128), ['cwcol'])
            LD(fing_b[:], fin_g.rearrange("(o n) -> o n", o=1).to_broadcast([128, 1024]), ['fing'])
            LD(pleg_b[:], ple_g.rearrange("(o n) -> o n", o=1).to_broadcast([128, 1024]), ['pleg'])
            LD(dng_b[:], dn_norm_g.rearrange("(o n) -> o n", o=1).to_broadcast([128, 128]), ['dng'])
            LD(negA_b[:], A_log.rearrange("(o n) -> o n", o=1).to_broadcast([128, 4]), ['negA'])
            LD(dtb_b[:], dt_bias.rearrange("(o n) -> o n", o=1).to_broadcast([128, 4]), ['dtb'])
            if stage < 1:
                return
            ACT(negA_b[:], negA_b[:], AF.Exp, ['negA'], ['negA'])
            TS('dve', negA_b[:], negA_b[:], -1.0, None, ALU.mult, None, ['negA'], ['negA'])
            for par in range(2):
                ps_ = slice(64 * par, 64 * par + 64)
                LDS(s5['lr'][ps_, :], A_re.rearrange("(gp par) n -> par n gp", par=2)[par], ['lr'])
                LDS(s5['li'][ps_, :], A_im.rearrange("(gp par) n -> par n gp", par=2)[par], ['li'])
                LDS(s5['ldt'][ps_, :], log_dt.rearrange("(o gp par) -> par o gp", par=2, o=1)[par].to_broadcast([64, 16]), ['ldt'])
                LDS(Bt['bre'][ps_], B_re.rearrange("(gp par) n i -> par n gp i", par=2)[par], ['bre'])
                LDS(Bt['bim'][ps_], B_im.rearrange("(gp par) n i -> par n gp i", par=2)[par], ['bim'])
            caps_setup['S'] = S.begin_capture()
            for (Cs, dst, dk) in ((C_re, Bt['cre'], 'cre'), (C_im, Bt['cim'], 'cim')):
                for hf in range(2):
                    for gl in range(8):
                        gp = 8 * hf + gl
                        LD(cst[16 * gl:16 * gl + 16, :].rearrange("p (par n) -> p par n", par=2), Cs[2 * gp:2 * gp + 2].rearrange("par o n -> o par n"), ['cst'])
                    b, bk = bank()
                    S.op('pe', lambda e, b=b: e.transpose(out=b[:, 0:128], in_=cst[:], identity=C['identf'][:]), reads=['cst', 'C'], writes=[bk])
                    CP('dve', dst[:, 8 * hf:8 * hf + 8, :].rearrange("p g o -> p (g o)"), b[:, 0:128], [bk], [dk])
            if stage < 2:
                return
            ACT(s5['dt'][:], s5['ldt'][:], AF.Exp, ['ldt'], ['dt'])
            TT('dve', s5['lrdt'][:], s5['lr'][:], s5['dt'][:], ALU.mult, ['lr', 'dt'], ['lrdt'])
            TT('dve', s5['lidt'][:], s5['li'][:], s5['dt'][:], ALU.mult, ['li', 'dt'], ['lidt'])
            b8 = lambda t_: t_[:].rearrange("p (g o) -> p g o", o=1).to_broadcast([128, 16, 8])
            TT('dve', s8['magl'][:], C['tau8'][:], b8(s5['lrdt']), ALU.mult, ['C', 'lrdt'], ['magl'])
            TT('dve', s8['ang'][:], C['tau8'][:], b8(s5['lidt']), ALU.mult, ['C', 'lidt'], ['s8ang'])

            def sincos(ang, sn, cs, sc, sci, tag):
                for (dst, off) in ((sn, 32.0), (cs, 32.25)):
                    TS('dve', sc['a'][:], ang[:], 1.0 / (2 * math.pi), off, ALU.mult, ALU.add, [tag + 'ang'], [tag + 'a'])
                    CP('dve', sci[:], sc['a'][:], [tag + 'a'], [tag + 'i'])
                    CP('dve', sc['b'][:], sci[:], [tag + 'i'], [tag + 'b'])
                    TT('dve', sc['c'][:], sc['a'][:], sc['b'][:], ALU.subtract, [tag + 'a', tag + 'b'], [tag + 'c'])
                    TS('dve', sc['d'][:], sc['c'][:], 0.5, None, ALU.is_gt, None, [tag + 'c'], [tag + 'd'])
                    TT('dve', sc['c'][:], sc['c'][:], sc['d'][:], ALU.subtract, [tag + 'c', tag + 'd'], [tag + 'c'])
                    ACT(dst[:], sc['c'][:], AF.Sin, [tag + 'c'], [tag + ('sn' if dst is sn else 'cs')], scale=TWO_PI)

            if stage < 2.1:
                return
            sincos(s8['ang'], s8['sn'], s8['cs'], s8, s8i, 's8')
            if stage < 2.2:
                return
            ACT(s8['mag'][:], s8['magl'][:], AF.Exp, ['magl'], ['mag'])
            ACT(s8['imag'][:], s8['magl'][:], AF.Exp, ['magl'], ['imag'], scale=-1.0)
            TT('dve', s8['pwr'][:], s8['mag'][:], s8['cs'][:], ALU.mult, ['mag', 's8cs'], ['pwr'])
            TT('dve', s8['pwi'][:], s8['mag'][:], s8['sn'][:], ALU.mult, ['mag', 's8sn'], ['pwi'])
            TT('dve', s8['ipr'][:], s8['imag'][:], s8['cs'][:], ALU.mult, ['imag', 's8cs'], ['ipr'])
            STT(s8['ipi'][:], s8['imag'][:], -1.0, s8['sn'][:], ALU.mult, ALU.mult, ['imag', 's8sn'], ['ipi'])
            if stage < 2.3:
                return
            TS('dve', s5['nr'][:], s8['pwr'][:, :, 0], -1.0, None, ALU.add, None, ['pwr'], ['nr'])
            CP('dve', s5['ni'][:], s8['pwi'][:, :, 0], ['pwi'], ['ni'])
            TT('dve', s5['t0'][:], s5['lr'][:], s5['lr'][:], ALU.mult, ['lr'], ['t0'])
            TT('dve', s5['t1'][:], s5['li'][:], s5['li'][:], ALU.mult, ['li'], ['t1'])
            TT('dve', s5['den'][:], s5['t0'][:], s5['t1'][:], ALU.add, ['t0', 't1'], ['den'])
            S.op('dve', lambda e: e.reciprocal(out=s5['den'][:], in_=s5['den'][:]), reads=['den'], writes=['den'])
            TT('dve', s5['t0'][:], s5['nr'][:], s5['lr'][:], ALU.mult, ['nr', 'lr'], ['t0'])
            TT('dve', s5['t1'][:], s5['ni'][:], s5['li'][:], ALU.mult, ['ni', 'li'], ['t1'])
            TT('dve', s5['cr'][:], s5['t0'][:], s5['t1'][:], ALU.add, ['t0', 't1'], ['cr'])
            TT('dve', s5['cr'][:], s5['cr'][:], s5['den'][:], ALU.mult, ['cr', 'den'], ['cr'])
            TT('dve', s5['t0'][:], s5['ni'][:], s5['lr'][:], ALU.mult, ['ni', 'lr'], ['t0'])
            TT('dve', s5['t1'][:], s5['nr'][:], s5['li'][:], ALU.mult, ['nr', 'li'], ['t1'])
            TT('dve', s5['ci'][:], s5['t0'][:], s5['t1'][:], ALU.subtract, ['t0', 't1'], ['ci'])
            TT('dve', s5['ci'][:], s5['ci'][:], s5['den'][:], ALU.mult, ['ci', 'den'], ['ci'])
            b16 = lambda t_: t_[:].rearrange("p (g o) -> p g o", o=1).to_broadcast([128, 16, 16])
            if stage < 2.4:
                return
            TT('dve', Bt['x'][:], Bt['bre'][:], b16(s5['cr']), ALU.mult, ['bre', 'cr'], ['Bx'])
            if stage < 2.5:
                return
            TT('dve', Bt['y'][:], Bt['bim'][:], b16(s5['ci']), ALU.mult, ['bim', 'ci'], ['By'])
            if stage < 2.6:
                return
            TT('dve', Bt['bbre'][:], Bt['x'][:], Bt['y'][:], ALU.subtract, ['Bx', 'By'], ['bbre'])
            if stage < 2.7:
                return
            TT('dve', Bt['x'][:], Bt['bim'][:], b16(s5['cr']), ALU.mult, ['bim', 'cr'], ['Bx'])
            if stage < 2.8:
                return
            TT('dve', Bt['y'][:], Bt['bre'][:], b16(s5['ci']), ALU.mult, ['bre', 'ci'], ['By'])
            if stage < 2.9:
                return
            TT('dve', Bt['bbim'][:], Bt['x'][:], Bt['y'][:], ALU.add, ['Bx', 'By'], ['bbim'])
            if stage < 3:
                return
            fl = lambda t_: t_[:].rearrange("p g j i -> p g (j i)")
            for qt in range(4):
                for gl in range(4):
                    gp = qt * 4 + gl
                    pw_b = lambda t_: t_[:, gp, :].rearrange("p (j o) -> p j o", o=1).to_broadcast([128, 8, 16])
                    bb_b = lambda t_: t_[:, gp, :].rearrange("p (o i) -> p o i", o=1).to_broadcast([128, 8, 16])
                    hx, hy = H['x'][:, gl], H['y'][:, gl]
                    TT('dve', hx, pw_b(s8['ipr']), bb_b(Bt['bbre']), ALU.mult, ['ipr', 'bbre'], ['Hx'])
                    TT('dve', hy, pw_b(s8['ipi']), bb_b(Bt['bbim']), ALU.mult, ['ipi', 'bbim'], ['Hy'])
                    TT('dve', H['hre'][:, gl], hx, hy, ALU.subtract, ['Hx', 'Hy'], ['hre'])
                    TT('dve', hx, pw_b(s8['ipr']), bb_b(Bt['bbim']), ALU.mult, ['ipr', 'bbim'], ['Hx'])
                    TT('dve', hy, pw_b(s8['ipi']), bb_b(Bt['bbre']), ALU.mult, ['ipi', 'bbre'], ['Hy'])
                    TT('dve', H['him'][:, gl], hx, hy, ALU.add, ['Hx', 'Hy'], ['him'])
                    TT('dve', hx, pw_b(s8['pwr']), bb_b(Bt['cre']), ALU.mult, ['pwr', 'cre'], ['Hx'])
                    TT('dve', hy, pw_b(s8['pwi']), bb_b(Bt['cim']), ALU.mult, ['pwi', 'cim'], ['Hy'])
                    TT('dve', H['gre'][:, gl], hx, hy, ALU.subtract, ['Hx', 'Hy'], ['gre'])
                    TT('dve', hx, pw_b(s8['pwi']), bb_b(Bt['cre']), ALU.mult, ['pwi', 'cre'], ['Hx'])
                    TT('dve', hy, pw_b(s8['pwr']), bb_b(Bt['cim']), ALU.mult, ['pwr', 'cim'], ['Hy'])
                    TT('dve', H['gim'][:, gl], hx, hy, ALU.add, ['Hx', 'Hy'], ['gim'])
                if stage < 3.1:
                    return
                p8 = lambda t_: t_[:, qt * 4:qt * 4 + 4, 7:8].to_broadcast([128, 4, 128])
                TT('dve', fl(H['x']), fl(H['hre']), p8(s8['pwr']), ALU.mult, ['hre', 'pwr'], ['Hx'])
                TT('dve', fl(H['y']), fl(H['him']), p8(s8['pwi']), ALU.mult, ['him', 'pwi'], ['Hy'])
                TT('dve', fl(H['h2re']), fl(H['x']), fl(H['y']), ALU.subtract, ['Hx', 'Hy'], ['h2re'])
                TT('dve', fl(H['x']), fl(H['him']), p8(s8['pwr']), ALU.mult, ['him', 'pwr'], ['Hx'])
                TT('dve', fl(H['y']), fl(H['hre']), p8(s8['pwi']), ALU.mult, ['hre', 'pwi'], ['Hy'])
                TT('dve', fl(H['h2im']), fl(H['x']), fl(H['y']), ALU.add, ['Hx', 'Hy'], ['h2im'])
                if stage < 3.2:
                    return
                TS('dve', fl(H['him']), fl(H['him']), -1.0, None, ALU.mult, None, ['him'], ['him'])
                if qt == 0:
                    S.op('dve', lambda e: e.memset(gqre[:], 0.0), writes=['gqre'])
                    S.op('dve', lambda e: e.memset(gqimn[:], 0.0), writes=['gqimn'])
                for par_ in range(2):
                    pp_ = slice(64 * par_, 64 * par_ + 64)
                    cc_ = slice(128 * par_, 128 * par_ + 128)
                    CP('dve', gqre[pp_, qt * 4:qt * 4 + 4, cc_], fl(H['gre'])[pp_], ['gre'], ['gqre'])
                    TS('dve', gqimn[pp_, qt * 4:qt * 4 + 4, cc_], fl(H['gim'])[pp_], -1.0, None, ALU.mult, None, ['gim'], ['gqimn'])
                if stage < 3.3:
                    return
                for gg in range(8):
                    g = qt * 8 + gg
                    gl, par = gg // 2, gg % 2
                    ps_ = slice(64 * par, 64 * par + 64)
                    b, bk = bank()
                    MM(b[:, 0:128], fl(H['hre'])[ps_, gl, :], fl(H['gre'])[ps_, gl, :], True, False, ['hre', 'gre'], [bk])
                    MM(b[:, 0:128], fl(H['him'])[ps_, gl, :], fl(H['gim'])[ps_, gl, :], False, True, ['him', 'gim'], [bk])
                    MM(b[:, 128:192], fl(H['h2re'])[ps_, gl, :], C['identf'][ps_, ps_], True, True, ['h2re', 'C'], [bk])
                    MM(b[:, 192:256], fl(H['h2im'])[ps_, gl, :], C['identf'][ps_, ps_], True, True, ['h2im', 'C'], [bk])
                    if stage < 3.4:
                        return
                    TT('dve', toepT[:, g, :], b[:, 0:128], C['toepmask'][:], ALU.mult, [bk, 'C'], ['toepT'])
                    if stage < 3.5:
                        return
                    ACT(ptre[:, g, :], b[:, 128:192], AF.Copy, [bk], ['ptre'])
                    ACT(ptim[:, g, :], b[:, 192:256], AF.Copy, [bk], ['ptim'])
            if stage < 4:
                return
            TS('dve', s5['t0'][:], s8['ang'][:, :, 7], 1.0 / (2 * math.pi), 32.0, ALU.mult, ALU.add, ['s8ang'], ['t0'])
            CP('dve', s8i[:, :, 0], s5['t0'][:], ['t0'], ['s8i'])
            CP('dve', s5['t1'][:], s8i[:, :, 0], ['s8i'], ['t1'])
            TT('dve', s5['phr'][:], s5['t0'][:], s5['t1'][:], ALU.subtract, ['t0', 't1'], ['phr'])
            TS('dve', s5['phr'][:], s5['phr'][:], 2 * math.pi, None, ALU.mult, None, ['phr'], ['phr'])
            TT('dve', s17['ang'][:], C['cidx'][:], s5['phr'][:].rearrange("p (g o) -> p g o", o=1).to_broadcast([128, 16, 17]), ALU.mult, ['C', 'phr'], ['s17ang'])
            sincos(s17['ang'], sinT, cosT, s17, s17i, 's17')
            CP('dve', rho[:], s8['mag'][:, :, 7], ['mag'], ['rho'])
            TT('dve', rhoS[:], C['rho0'][:], b16(rho), ALU.mult, ['C', 'rho'], ['rhoS'])
            if stage < 5:
                return
            for tp in range(4):
                for bq in range(12):
                    TS('dve', diag[:, tp, bq, :], C['identf'][:], cwcol[:, tp, bq:bq + 1], None, ALU.mult, None, ['C', 'cwcol'], ['diag'])
            if stage < 6:
                return
            S.end_capture()
            caps_setup['W'] = S.begin_capture()
            si = [0]

            def wscratch(srcap, chunk, kbslot, scale_col=None):
                k2 = si[0] % NSTG; si[0] += 1
                sb, sbb = stg[k2], stgb[k2]
                S.dma('act' if si[0] % 2 else 'sp', lambda e: e.dma_start(out=sb[:], in_=srcap), writes=['stg%d' % k2])
                if si[0] % 2:
                    if scale_col is not None:
                        TS('dve', sbb[:], sb[:], scale_col, None, ALU.mult, None, ['stg%d' % k2, 'gcol'], ['stgb%d' % k2])
                    else:
                        CP('dve', sbb[:], sb[:], ['stg%d' % k2], ['stgb%d' % k2])
                else:
                    if scale_col is not None:
                        ACT(sbb[:], sb[:], AF.Copy, ['stg%d' % k2, 'gcol'], ['stgb%d' % k2], scale=scale_col)
                    else:
                        ACT(sbb[:], sb[:], AF.Copy, ['stg%d' % k2], ['stgb%d' % k2])
                S.dma(['sp', 'act'][(si[0] // 2) % 2], lambda e: e.dma_start(out=wscr[chunk, kbslot], in_=sbb[:]), reads=['stgb%d' % k2], writes=['wscr%d_%d' % (chunk, kbslot)])
            for kb in range(8):
                rs = slice(kb * 128, (kb + 1) * 128)
                for cg in range(6):
                    wscratch(w_in[rs, cg * 512:(cg + 1) * 512], cg, kb, gcol[:, kb:kb + 1])
                for nh in range(2):
                    wscratch(w_out[rs, nh * 512:(nh + 1) * 512], 6 + nh, kb)
                    wscratch(w_pg[rs, nh * 512:(nh + 1) * 512], 8 + nh, kb)
                k2 = si[0] % NSTG; si[0] += 1
                S.dma('sp', lambda e, k2=k2, rs=rs: e.dma_start(out=stg[k2][:, 0:8], in_=w_in[rs, 3072:3080]), writes=['stg%d' % k2])
                TS('dve', wba[:, kb, :], stg[k2][:, 0:8], gcol[:, kb:kb + 1], None, ALU.mult, None, ['stg%d' % k2, 'gcol'], ['wba'])
            for kb in range(2):
                for nh in range(2):
                    wscratch(w_pp[kb * 128:(kb + 1) * 128, nh * 512:(nh + 1) * 512], 10, kb * 2 + nh)
            for kb in range(4):
                k2 = si[0] % NSTG; si[0] += 1
                S.dma('sp', lambda e, k2=k2, kb=kb: e.dma_start(out=stg[k2][:], in_=w_glu[kb * 128:(kb + 1) * 128, :]), writes=['stg%d' % k2])
                CP('dve', wglu[:, kb, :], stg[k2][:], ['stg%d' % k2], ['W'])

        do_setup()
        S.end_capture()
        S.issue_merged([caps_setup['W'], caps_setup['S']], grain=6)
        S.barrier()
        LN128H = -0.5 * math.log(128.0)
        sc = sc_
        gcum = None
        caps = []
        for ti in range(ntiles):
            tl = ti % NT_SEQ
            X = xt; xk = 'xt'
            r0 = ti * 128
            capA = S.begin_capture(); set_banks([3, 4, 5, 6])
            LD(X[:], x[r0:r0 + 128, :], [xk])
            S.op('act', lambda e, X=X: e.activation(out=xn[:], in_=X[:], func=AF.Square, accum_out=st[:, 0:1]), reads=[xk], writes=['xn', 'st0'])
            ACT(st[:, 1:2], st[:, 0:1], AF.Ln, ['st0'], ['st1'], bias=EPS, scale=1.0 / 1024)
            ACT(st[:, 1:2], st[:, 1:2], AF.Exp, ['st1'], ['st1'], scale=-0.5)
            TS('dve', xn[:], X[:], st[:, 1:2], None, ALU.mult, None, [xk, 'st1'], ['xn'])
            b, bk = bank()
            for kb in range(8):
                TR(psb(b, 8, 128)[:, kb, :], xn[:, kb * 128:(kb + 1) * 128], C['identb'][:], ['xn', 'C'], [bk])
            CP('dve', aT[:], psb(b, 8, 128), [bk], ['aT'])
            for grp in range(5):
                b, bk = bank()
                wsl, wk = wget(grp, 'A')
                for q4 in range(4):
                    cb = grp * 4 + q4
                    for kb in range(8):
                        MM(b[:, q4 * 128:(q4 + 1) * 128], wsl[:, kb, q4 * 128:(q4 + 1) * 128], aT[:, kb, :], kb == 0, kb == 7, [wk, 'aT'], [bk])
                v4 = psf(b, 4, 128)
                if grp == 0:
                    ACT(u32[:], v4, AF.Copy, [bk], ['u32'])
                    CP('dve', ubf[:], v4, [bk], ['ubf'])
                elif grp == 1:
                    ACT(zsf[:], v4, AF.Silu, [bk], ['zsf'])
                else:
                    o4 = qkvp[:, (grp - 2) * 4:(grp - 1) * 4, 3:131]
                    if grp % 2:
                        ACT(o4, v4, AF.Copy, [bk], ['qkvp'])
                    else:
                        CP('dve', o4, v4, [bk], ['qkvp'])
            b, bk = bank()
            wsl, wk = wget(5, 'A')
            for kb in range(8):
                MM(b[:, 0:512], aT[:, kb, :], wsl[:, kb, :], kb == 0, kb == 7, ['aT', wk], [bk])
            ACT(zd[:], psf(b, 4, 128), AF.Silu, [bk], ['zd'])
            bS, bSk = PS[7], 'ps7'
            for kb in range(8):
                MM(bS[:, 0:8], aT[:, kb, :], wba[:, kb, :], kb == 0, kb == 7, ['aT', 'wba'], [bSk])
            S.end_capture()
            capB = S.begin_capture(); set_banks([0, 1, 2])
            if tl == 0:
                S.op('pool', lambda e: e.memset(carr[:], 0.0), writes=['carr'])
            b, bk = bank()
            for q4 in range(4):
                TR(psb(b, 4, 128)[:, q4, :], ubf[:, q4, :], C['identb'][:], ['ubf', 'C'], [bk])
            CP('dve', utok[:], psb(b, 512), [bk], ['utok'])
            for j in range(8):
                b, bk = bank()
                MM(b[0:16, 0:512], C['selj'][:, j, :], utok[:], True, True, ['C', 'utok'], [bk])
                srcv = b[0:16, 0:512].rearrange("p (g i) -> p g i", g=32)
                if j % 2:
                    ACT(ucm[:, :, j, :], srcv, AF.Copy, [bk], ['cm'])
                else:
                    CP('dve', ucm[:, :, j, :], srcv, [bk], ['cm'])
            b, bk = bank()
            for g in range(32):
                TR(psb(b, 32, 16)[:, g, :], ucm[:, g].rearrange('p j i -> p (j i)'), C['identb'][0:16, 0:16], ['cm', 'C'], [bk])
            CP('dve', ug[:], psb(b, 32, 16), [bk], ['ug'])
            bR, bRk = bank(); bI, bIk = bank()
            vRr = psf(bR, 16, 2, 16); vRi = psf(bI, 16, 2, 16)
            for gp in range(16):
                rhs_ = ug[:, 2 * gp:2 * gp + 2, :].rearrange("p g c -> p (g c)")
                MM(vRr[:, gp].rearrange("p a c -> p (a c)"), ptre[:, 2 * gp:2 * gp + 2, :].rearrange("p g n -> p (g n)"), rhs_, True, True, ['ptre', 'ug'], [bRk])
                MM(vRi[:, gp].rearrange("p a c -> p (a c)"), ptim[:, 2 * gp:2 * gp + 2, :].rearrange("p g n -> p (g n)"), rhs_, True, True, ['ptim', 'ug'], [bIk])
            c1v, s1v = cosT[:, :, 1:17], sinT[:, :, 1:17]
            for par_ in range(2):
                pp_ = slice(64 * par_, 64 * par_ + 64)
                TT('dve', ztmp[pp_, 0], vRr[pp_, :, par_, :], c1v[pp_], ALU.mult, [bRk, 's17cs'], ['zt0'])
                TT('dve', ztmp[pp_, 1], vRi[pp_, :, par_, :], s1v[pp_], ALU.mult, [bIk, 's17sn'], ['zt1'])
                TT('dve', ztmp[pp_, 2], vRi[pp_, :, par_, :], c1v[pp_], ALU.mult, [bIk, 's17cs'], ['zt2'])
                TT('dve', ztmp[pp_, 3], vRr[pp_, :, par_, :], s1v[pp_], ALU.mult, [bRk, 's17sn'], ['zt3'])
            TT('dve', zin[:, 0], ztmp[:, 0], ztmp[:, 1], ALU.add, ['zt0', 'zt1'], ['zin'])
            TT('dve', zin[:, 1], ztmp[:, 2], ztmp[:, 3], ALU.subtract, ['zt2', 'zt3'], ['zin'])
            TT('dve', ctmp[:], carr[:], rho[:].rearrange("p (o g) -> p o g", o=1).to_broadcast([128, 2, 16]), ALU.mult, ['carr', 'rho'], ['ctmp'])
            TT('dve', zin[:, :, :, 0], zin[:, :, :, 0], ctmp[:], ALU.add, ['zin', 'ctmp'], ['zin'])
            for ri in range(2):
                S.op('dve', lambda e, ri=ri: e.tensor_tensor_scan(out=zs_[:, ri].rearrange("p g c -> p (g c)"), data0=rhoS[:].rearrange("p g c -> p (g c)"), data1=zin[:, ri].rearrange("p g c -> p (g c)"), initial=0.0, op0=ALU.mult, op1=ALU.add), reads=['zin', 'rhoS'], writes=['zs%d' % ri])
            TT('dve', ztmp[:, 0], zs_[:, 0], c1v, ALU.mult, ['zs0', 's17cs'], ['zt0'])
            TT('dve', ztmp[:, 1], zs_[:, 1], s1v, ALU.mult, ['zs1', 's17sn'], ['zt1'])
            TT('dve', ztmp[:, 2], zs_[:, 0], s1v, ALU.mult, ['zs0', 's17sn'], ['zt2'])
            TT('dve', ztmp[:, 3], zs_[:, 1], c1v, ALU.mult, ['zs1', 's17cs'], ['zt3'])
            TT('dve', snx[:, 0], ztmp[:, 0], ztmp[:, 1], ALU.subtract, ['zt0', 'zt1'], ['snx0'])
            TT('dve', snx[:, 1], ztmp[:, 2], ztmp[:, 3], ALU.add, ['zt2', 'zt3'], ['snx1'])
            CP('dve', sbf5[:, :, :, 0], carr[:], ['carr'], ['sbf5'])
            ACT(sbf5[:, :, :, 1:16], snx[:, :, :, 0:15], AF.Copy, ['snx0', 'snx1'], ['sbf5'])
            CP('dve', carr[:], snx[:, :, :, 15], ['snx0', 'snx1'], ['carr'])
            for g4 in range(8):
                b, bk = bank()
                for pi in range(2):
                    gp = g4 * 2 + pi
                    g0, g1 = 2 * gp, 2 * gp + 1
                    o_ = b[0:16, pi * 256:(pi + 1) * 256]
                    MMS(o_[:, 0:128], ug[:, g0, :], toepT[:, g0, :], True, False, ['ug', 'toepT'], [bk])
                    MMS(o_[:, 128:256], ug[:, g1, :], toepT[:, g1, :], False, False, ['ug', 'toepT'], [bk])
                    MMS(o_, sbf5[:, 0, gp, :], gqre[:, gp, :], False, False, ['sbf5', 'gqre'], [bk])
                    MMS(o_, sbf5[:, 1, gp, :], gqimn[:, gp, :], False, True, ['sbf5', 'gqimn'], [bk])
                src = b[0:16, 0:512].rearrange("p (g j o) -> p g j o", g=4, j=8)
                dst = ycm[:, :, g4 * 64:(g4 + 1) * 64].rearrange("p j (g o) -> p g j o", g=4)
                if g4 % 2:
                    ACT(dst, src, AF.Copy, [bk], ['cm'])
                else:
                    CP('dve', dst, src, [bk], ['cm'])
            b, bk = bank()
            vY = psb(b, 4, 8, 16)
            for q4 in range(4):
                for j in range(8):
                    TR(vY[:, q4, j, :], ycm[:, j, q4 * 128:(q4 + 1) * 128], C['identb'][0:16, 0:16], ['cm', 'C'], [bk])
            for q4 in range(4):
                STT(yfm[:, q4, :].rearrange("p (c j) -> p j c", j=8), u32[:, q4, :].rearrange("p (c j) -> p j c", j=8), Dcol[:, q4:q4 + 1], vY[:, q4], ALU.mult, ALU.add, ['u32', 'Dcol', bk], ['yfm'])
            if ti == 0:
                DBG('u32', u32[:], 'u32', [128, 4, 128], F32)
                DBG('yfm', yfm[:], 'yfm', [128, 4, 128], F32)
                DBG('zsf', zsf[:], 'zsf', [128, 4, 128], BF16)
            ACT(y1[:], yfm[:], AF.Gelu_apprx_tanh, ['yfm'], ['y1'])
            b, bk = bank()
            for mb in range(4):
                for kb in range(4):
                    MM(b[:, mb * 128:(mb + 1) * 128], wglu[:, kb, mb * 128:(mb + 1) * 128], y1[:, kb, :], kb == 0, kb == 3, ['W', 'y1'], [bk])
            for mb in range(4):
                ACT(gate[:, mb, :], b[:, mb * 128:(mb + 1) * 128], AF.Tanh, [bk, 'bglu'], ['gate'], bias=bglu[:, mb:mb + 1], scale=0.5)
            STT(gate[:], gate[:], 1.0, zsf[:], ALU.add, ALU.mult, ['gate', 'zsf'], ['gate'])
            STT(ysf[:], gate[:], 0.5, y1[:], ALU.mult, ALU.mult, ['gate', 'y1'], ['ysf'])
            if ti == 0:
                DBG('ysf', ysf[:], 'ysf', [128, 4, 128], BF16)
            S.end_capture()
            capC = S.begin_capture(); set_banks([3, 4, 5, 6])
            if tl == 0:
                S.op('pool', lambda e: e.memset(qkvp[:, :, 0:3], 0.0), writes=['halo'])
                S.op('pool', lambda e: e.memset(S32[:], 0.0), writes=['S32'])
                S.op('pool', lambda e: e.memset(Sbf[:], 0.0), writes=['Sbf'])
            for g3 in range(3):
                b, bk = bank()
                for q4 in range(4):
                    bq = g3 * 4 + q4
                    for tp in range(4):
                        MM(b[:, q4 * 128:(q4 + 1) * 128], diag[:, tp, bq, :], qkvp[:, bq, tp:tp + 128], tp == 0, tp == 3, ['diag', 'qkvp', 'halo'], [bk])
                ACT(qkv[:, g3 * 4:(g3 + 1) * 4, :], psf(b, 4, 128), AF.Silu, [bk], ['qkv'])
            CP('pool', qkvp[:, :, 0:3], qkvp[:, :, 128:131], ['qkvp'], ['halo'])
            ACT(sq[:], qkv[:, 0:8, :], AF.Square, ['qkv'], ['sq'])
            for bq in range(8):
                MM(bS[:, 8 + bq:9 + bq], sq[:, bq, :], C['onesb'][:, 0:1], True, True, ['sq', 'C'], [bSk])
            b, bk = bank()
            for i8 in range(8):
                TR(psb(b, 8, 128)[:, i8, :], qkv[:, 4 + i8, :], C['identb'][:], ['qkv', 'C'], [bk])
            CP('dve', kvt[:], psb(b, 8, 128), [bk], ['kvt'])
            CP('dve', ba[:], bS[:, 0:16], [bSk], ['ba'])
            ACT(sc['sigb'][:], ba[:, 0:4], AF.Sigmoid, ['ba'], ['sigb'])
            TT('dve', sc['tmp'][:], ba[:, 4:8], dtb_b[:], ALU.add, ['ba', 'dtb'], ['tmp'])
            ACT(sc['ex'][:], sc['tmp'][:], AF.Exp, ['tmp'], ['ex'])
            ACT(sc['lnb'][:], sc['sigb'][:], AF.Ln, ['sigb'], ['lnb'])
            ACT(sc['sp'][:], sc['ex'][:], AF.Ln, ['ex'], ['sp'], bias=1.0)
            ACT(lnr[:], ba[:, 8:16], AF.Ln, ['ba'], ['lnr'], bias=EPS)
            TT('dve', sc['g'][:], sc['sp'][:], negA_b[:], ALU.mult, ['sp', 'negA'], ['g'])
            TS('dve', lnr[:], lnr[:], -0.5, None, ALU.mult, None, ['lnr'], ['lnr'])
            for i4, nm in enumerate(['tri', 'blk', 'onesA', 'onesB']):
                MM(bS[:, 16 + 4 * i4:20 + 4 * i4], C[nm][:], sc['g'][:], True, True, ['C', 'g'], [bSk])
            CP('dve', cum[:], bS[:, 16:32], [bSk], ['cum'])
            gc_, gls_, glA_, glB_ = cum[:, 0:4], cum[:, 4:8], cum[:, 8:12], cum[:, 12:16]
            lq, lk = lnr[:, 0:4], lnr[:, 4:8]
            TT('dve', sc['c1'][:], lk, gc_, ALU.subtract, ['lnr', 'cum'], ['c1'])
            STT(sc['r1'][:], gc_, LN128H, lq, ALU.add, ALU.add, ['cum', 'lnr'], ['r1'])
            TT('dve', sc['r2'][:], gc_, sc['lnb'][:], ALU.add, ['cum', 'lnb'], ['r2'])
            TT('dve', sc['r2'][:], sc['r2'][:], lk, ALU.add, ['r2', 'lnr'], ['r2'])
            TT('dve', sc['tmp'][:], sc['c1'][:], gls_, ALU.add, ['c1', 'cum'], ['tmp2'])
            ACT(sc['kbgs'][:], sc['r2'][:], AF.Exp, ['r2'], ['kbgs'])
            ACT(sc['kds'][:], sc['tmp'][:], AF.Exp, ['tmp2'], ['kds'])
            ACT(sc['so'][:], sc['r1'][:], AF.Exp, ['r1'], ['so'])
            ACT(sc['eglA'][:], glA_, AF.Exp, ['cum'], ['eglA'])
            ACT(sc['eglB'][:], glB_, AF.Exp, ['cum'], ['eglB'])
            i2b = C['i2'][:].rearrange("p (o i) -> p o i", o=1).to_broadcast([128, 4, 64])
            hb = lambda t_, n: t_[:].rearrange("p (h o) -> p h o", o=1).to_broadcast([128, 4, n])
            for kd_, nm in enumerate(['r1', 'r2', 'c1']):
                TT('dve', RB[:, :, kd_, :], i2b, hb(sc[nm], 64), ALU.mult, ['C', nm], ['RB'])
            TT('pool', vb[:], kvt[:, 4:8, :], hb(sc['sigb'], 128), ALU.mult, ['kvt', 'sigb'], ['vb'])
            TT('pool', kbg[:], kvt[:, 0:4, :], hb(sc['kbgs'], 128), ALU.mult, ['kvt', 'kbgs'], ['kbg'])
            TT('dve', kd[:], kvt[:, 0:4, :], hb(sc['kds'], 128), ALU.mult, ['kvt', 'kds'], ['kd'])
            for hh in range(2):
                b, bk = bank()
                for h2_ in range(2):
                    h = 2 * hh + h2_
                    o_ = b[:, h2_ * 192:(h2_ + 1) * 192]
                    MM(o_, C['blk'][:], RB[:, h].rearrange("p k i -> p (k i)"), True, True, ['C', 'RB'], [bk])
                TT('dve', Bm[:, 2 * hh:2 * hh + 2].rearrange("p h k i -> p h (k i)"), b[:, 0:384].rearrange("p (h n) -> p h n", h=2), C['maskneg'][:].rearrange("p k i -> p (k i)").rearrange("p (o n) -> p o n", o=1).to_broadcast([128, 2, 192]), ALU.add, [bk, 'C'], ['Bm'])
                v_ = Bm[:, 2 * hh:2 * hh + 2]
                bk = 'Bm'
                for h2_ in range(2):
                    h = 2 * hh + h2_
                    ACT(Mx[:, h, 0:2, :], v_[:, h2_, 0:2, :], AF.Exp, [bk, 'c1'], ['Mx'], bias=sc['c1'][:, h:h + 1])
                    ACT(Mx[:, h, 2, :], v_[:, h2_, 2, :], AF.Exp, [bk, 'r2'], ['Mx'], bias=sc['r2'][:, h:h + 1])
            b, bk = bank()
            vS = psf(b, 4, 2, 64)
            for h in range(4):
                for ch in range(2):
                    ps_ = slice(64 * ch, 64 * ch + 64)
                    MM(vS[ps_, h, :, :], qkv[:, 4 + h, ps_], qkv[:, h:h + 5:4, ps_], True, True, ['qkv'], [bk])
            TT('dve', attnT[:], vS[:, :, 0, :], Mx[:, :, 0, :], ALU.mult, [bk, 'Mx'], ['attnT'])
            STT(NP[0][:, :, 0, :], vS[:, :, 1, :], -1.0, Mx[:, :, 1, :], ALU.mult, ALU.mult, [bk, 'Mx'], ['NP0a'])
            STT(NTt[0][:], vS[:, :, 1, :], -1.0, Mx[:, :, 2, :], ALU.mult, ALU.mult, [bk, 'Mx'], ['NT0'])
            TT('dve', NP[0][:, :, 1, :], NP[0][:, :, 0, :], C['i2'][:].rearrange("p (o i) -> p o i", o=1).to_broadcast([128, 4, 64]), ALU.add, ['NP0a', 'C'], ['NP0b'])
            for s in range(6):
                cur, nxt = s % 2, (s + 1) % 2
                NPc, NPn, NTc, NTn = NP[cur], NP[nxt], NTt[cur], NTt[nxt]
                kNa, kNb, kT = 'NP%da' % cur, 'NP%db' % cur, 'NT%d' % cur
                nNa, nNb, nT = 'NP%da' % nxt, 'NP%db' % nxt, 'NT%d' % nxt
                bA, bAk = bank()
                vA = psf(bA, 4, 128)
                for h in range(4):
                    for ch in range(2):
                        ps_ = slice(64 * ch, 64 * ch + 64)
                        if s == 0:
                            MM(vA[ps_, h, 0:64], NTc[ps_, h, :], NPc[ps_, h, 0, :], True, True, [kT, kNa], [bAk])
                        elif s < 5:
                            MM(vA[ps_, h, :], NTc[ps_, h, :], NPc[ps_, h, :, :].rearrange("p k i -> p (k i)"), True, True, [kT, kNa, kNb], [bAk])
                        else:
                            MM(vA[ps_, h, 64:128], NTc[ps_, h, :], NPc[ps_, h, 1, :], True, True, [kT, kNb], [bAk])
                if s < 5:
                    bB, bBk = bank()
                    vB = psf(bB, 4, 64)
                    for h in range(4):
                        for ch in range(2):
                            ps_ = slice(64 * ch, 64 * ch + 64)
                            MM(vB[ps_, h, :], NPc[ps_, h, 0, :], NTc[ps_, h, :], True, True, [kNa, kT], [bBk])
                    ACT(NPn[:, :, 0, :], vA[:, :, 0:64], AF.Copy, [bAk], [nNa])
                    ACT(NTn[:], vB, AF.Copy, [bBk], [nT])
                if s == 0:
                    CP('dve', NPn[:, :, 1, :], NPc[:, :, 1, :], [kNb], [nNb])
                elif s < 5:
                    TT('dve', NPn[:, :, 1, :], NPc[:, :, 1, :], vA[:, :, 64:128], ALU.add, [kNb, bAk], [nNb])
                else:
                    TT('dve', Pf[:], NPc[:, :, 1, :], vA[:, :, 64:128], ALU.add, [kNb, bAk], ['Pf'])
            bU, bUk = bank(); bW, bWk = bank()
            vU = psf(bU, 4, 128); vW = psf(bW, 4, 2, 64)
            for h in range(4):
                for ch in range(2):
                    ps_ = slice(64 * ch, 64 * ch + 64)
                    MM(vU[ps_, h, :], Pf[ps_, h, :], vb[ps_, h, :], True, True, ['Pf', 'vb'], [bUk])
                    MM(vW[:, h, ch, :], kbg[ps_, h, :], Pf[ps_, h, :], True, True, ['kbg', 'Pf'], [bWk])
            ACT(ug32[:], vU, AF.Copy, [bUk], ['ug32'])
            CP('dve', wT[:], vW, [bWk], ['wT'])
            b1, b1k = bank(); b2, b2k = bank(); b3, b3k = bank()
            v1, v2, v3 = psf(b1, 4, 128), psf(b2, 4, 128), psf(b3, 4, 128)
            for ch in range(2):
                ps_ = slice(64 * ch, 64 * ch + 64)
                egl = sc['eglA'] if ch == 0 else sc['eglB']
                eglk = 'eglA' if ch == 0 else 'eglB'
                for h in range(4):
                    MM(v1[ps_, h, :], wT[:, h, ch, :], Sbf[:, h, :], True, True, ['wT', 'Sbf'], [b1k])
                    MM(v2[ps_, h, :], qkv[:, h, ps_], Sbf[:, h, :], True, True, ['qkv', 'Sbf'], [b2k])
                TT('dve', vnew[ps_], ug32[ps_], v1[ps_], ALU.subtract, ['ug32', b1k], ['vnew'])
                bS2, bS2k = bank()
                vS2 = psf(bS2, 4, 128)
                for h in range(4):
                    MM(v3[ps_, h, :], attnT[ps_, h, :], vnew[ps_, h, :], True, True, ['attnT', 'vnew'], [b3k])
                    MM(vS2[:, h, :], kd[ps_, h, :], vnew[ps_, h, :], True, True, ['kd', 'vnew'], [bS2k])
                TT('dve', S32[:], S32[:], hb(egl, 128), ALU.mult, ['S32', eglk], ['S32'])
                TT('dve', S32[:], S32[:], vS2, ALU.add, ['S32', bS2k], ['S32'])
                ACT(Sbf[:], S32[:], AF.Copy, ['S32'], ['Sbf'])
                TT('dve', otmp[ps_], v2[ps_], sc['so'][ps_].rearrange("p (h o) -> p h o", o=1).to_broadcast([64, 4, 128]), ALU.mult, [b2k, 'so'], ['otmp'])
                TT('dve', otok[ps_], otmp[ps_], v3[ps_], ALU.add, ['otmp', b3k], ['otok'])
            if ti == 0:
                DBG('qkv', qkv[:], 'qkv', [128, 12, 128], BF16)
                DBG('otok', otok[:], 'otok', [128, 4, 128], F32)
            for h in range(4):
                S.op('act', lambda e, h=h: e.activation(out=t1[:, h, :], in_=otok[:, h, :], func=AF.Square, accum_out=ssd[:, h:h + 1]), reads=['otok'], writes=['t1', 'ssd'])
            ACT(rsd[:], ssd[:], AF.Ln, ['ssd'], ['rsd'], bias=EPS, scale=1.0 / 128)
            ACT(rsd[:], rsd[:], AF.Exp, ['rsd'], ['rsd'], scale=-0.5)
            TT('dve', t1[:], otok[:], dng_b[:].rearrange("p (o d) -> p o d", o=1).to_broadcast([128, 4, 128]), ALU.mult, ['otok', 'dng'], ['t1'])
            TT('dve', t1[:], t1[:], zd[:], ALU.mult, ['t1', 'zd'], ['t1'])
            TT('dve', ydt[:], t1[:], hb(rsd, 128), ALU.mult, ['t1', 'rsd'], ['ydt'])
            b, bk = bank()
            for h in range(4):
                TR(psb(b, 4, 128)[:, h, :], ydt[:, h, :], C['identb'][:], ['ydt', 'C'], [bk])
            CP('dve', ydf[:], psb(b, 4, 128), [bk], ['ydf'])
            S.end_capture()
            capD = S.begin_capture(); set_banks([0, 1, 2])
            LD(h1[:], x[r0:r0 + 128, :], ['h1'])
            LD(pt[:], p[r0:r0 + 128, :], ['pt'])
            for nh in range(2):
                b, bk = bank()
                wsl, wk = wget(6 + nh, 'D')
                for kb in range(8):
                    l_ = ysf[:, kb, :] if kb < 4 else ydf[:, kb - 4, :]
                    MM(b[:, 0:512], l_, wsl[:, kb, :], kb == 0, kb == 7, ['ysf', 'ydf', wk], [bk])
                TT('dve', h1[:, nh * 512:(nh + 1) * 512], h1[:, nh * 512:(nh + 1) * 512], b[:, 0:512], ALU.add, ['h1', bk], ['h1'])
            if ti == 0:
                DBG('ydf', ydf[:], 'ydf', [128, 4, 128], BF16)
                DBG('h1', h1[:], 'h1', [128, 1024], F32)
            CP('pool', ptb[:], pt[:], ['pt'], ['ptb'])
            b, bk = bank()
            for kb in range(2):
                TR(psb(b, 2, 128)[:, kb, :], ptb[:, kb * 128:(kb + 1) * 128], C['identb'][:], ['ptb', 'C'], [bk])
            CP('dve', pT[:], psb(b, 2, 128), [bk], ['pT'])
            be = []
            wsl, wk = wget(10, 'D')
            for nh in range(2):
                b, bk = bank()
                be.append((b, bk))
                for kb in range(2):
                    MM(b[:, 0:512], pT[:, kb, :], wsl[:, kb * 2 + nh, :], kb == 0, kb == 1, ['pT', wk], [bk])
                S.op('act', lambda e, b=b, nh=nh: e.activation(out=ee[:, nh * 512:(nh + 1) * 512], in_=b[:, 0:512], func=AF.Square, accum_out=st[:, 2 + nh:3 + nh]), reads=[bk], writes=['ee', 'st2'])
            TT('dve', st[:, 4:5], st[:, 2:3], st[:, 3:4], ALU.add, ['st2'], ['st4'])
            ACT(st[:, 4:5], st[:, 4:5], AF.Ln, ['st4'], ['st4'], bias=EPS, scale=1.0 / 1024)
            ACT(st[:, 4:5], st[:, 4:5], AF.Exp, ['st4'], ['st4'], scale=-0.5)
            for nh in range(2):
                b, bk = be[nh]
                STT(ee[:, nh * 512:(nh + 1) * 512], b[:, 0:512], st[:, 4:5], pleg_b[:, nh * 512:(nh + 1) * 512], ALU.mult, ALU.mult, [bk, 'st4', 'pleg'], ['ee'])
            ACT(h1b[:], h1[:], AF.Copy, ['h1'], ['h1b'])
            b, bk = bank()
            for kb in range(8):
                TR(psb(b, 8, 128)[:, kb, :], h1b[:, kb * 128:(kb + 1) * 128], C['identb'][:], ['h1b', 'C'], [bk])
            CP('dve', h1T[:], psb(b, 8, 128), [bk], ['h1T'])
            for nh in range(2):
                b, bk = bank()
                wsl, wk = wget(8 + nh, 'D')
                for kb in range(8):
                    MM(b[:, 0:512], h1T[:, kb, :], wsl[:, kb, :], kb == 0, kb == 7, ['h1T', wk], [bk])
                ACT(sg[:, nh * 512:(nh + 1) * 512], b[:, 0:512], AF.Sigmoid, [bk], ['sg'])
            TT('dve', sg[:], sg[:], ee[:], ALU.mult, ['sg', 'ee'], ['sg'])
            TT('dve', h1[:], h1[:], sg[:], ALU.add, ['h1', 'sg'], ['h1'])
            S.op('act', lambda e: e.activation(out=sg[:], in_=h1[:], func=AF.Square, accum_out=st[:, 5:6]), reads=['h1'], writes=['sg', 'st5'])
            ACT(st[:, 6:7], st[:, 5:6], AF.Ln, ['st5'], ['st6'], bias=EPS, scale=1.0 / 1024)
            ACT(st[:, 6:7], st[:, 6:7], AF.Exp, ['st6'], ['st6'], scale=-0.5)
            O = ee; ok = 'ee'
            STT(O[:], h1[:], st[:, 6:7], fing_b[:], ALU.mult, ALU.mult, ['h1', 'st6', 'fing'], [ok])
            S.dma('act', lambda e, O=O, r0=r0: e.dma_start(out=out[r0:r0 + 128, :], in_=O[:]), reads=[ok], writes=['out%d' % ti])
            S.end_capture()
            caps.append((capA, capB, capC, capD))
        set_banks([0, 1, 2, 3, 4, 5, 6])
        if caps:
            S.issue_merged([caps[0][0]])
        for ti in range(ntiles):
            S.issue_merged([caps[ti][1], caps[ti][2]], lead=[0.0, 0.2])
            S.issue_merged([caps[ti][3]] + ([caps[ti + 1][0]] if ti + 1 < ntiles else []), lead=[0.25, 0.0])
        S.finish('sp', ['out%d' % ti for ti in range(ntiles)])
        S.finish('act', ['out%d' % ti for ti in range(ntiles)])
        S.finish('sp', dbg_outs)
        S.drain_all('sp')
        S.replay(block)
    return nc


_CACHE = {}


def kernel(**inputs):
    x = np.ascontiguousarray(inputs['x'], dtype=np.float32)
    p = np.ascontiguousarray(inputs['p'], dtype=np.float32)[0]
    B = x.shape[0]
    per = B // NCORES
    consts = host_consts()
    shared = {}
    for k in ['norm_mix_g', 'w_in', 'ssm_A_re', 'ssm_A_im', 'ssm_B_re', 'ssm_B_im', 'ssm_C_re', 'ssm_C_im',
              'ssm_D', 'ssm_log_dt', 'ssm_w_glu', 'ssm_b_glu', 'dn_conv_w', 'dn_A_log', 'dn_dt_bias',
              'dn_norm_g', 'w_out', 'w_ple_proj', 'ple_norm_g', 'w_ple_gate']:
        shared[k] = np.ascontiguousarray(np.asarray(inputs[k], dtype=np.float32)[0])
    shared['final_norm_g'] = np.ascontiguousarray(inputs['final_norm_g'], dtype=np.float32)
    for k, v in consts.items():
        shared['c_' + k] = v
    if 'nc' not in _CACHE:
        _CACHE['nc'] = build_nc()
    nc = _CACHE['nc']
    in_maps = []
    for c in range(NCORES):
        m = dict(shared)
        m['x'] = x[c * per:(c + 1) * per].reshape(per * SEQ, 1024)
        m['p'] = p[c * per:(c + 1) * per].reshape(per * SEQ, 256)
        in_maps.append(m)
    res = run_bass_kernel_spmd(nc, in_maps, core_ids=list(range(NCORES)))
    outs = [np.asarray(r['out']).reshape(per, SEQ, 1024) for r in res.results]
    return np.concatenate(outs, axis=0).astype(np.float32)
```

```python
import math
from contextlib import ExitStack
import numpy as np
import ml_dtypes
import concourse.bass as bass
import concourse.mybir as mybir
from concourse.bass_utils import run_bass_kernel_spmd

F32 = mybir.dt.float32
BF16 = mybir.dt.bfloat16
I32 = mybir.dt.int32
AF = mybir.ActivationFunctionType
ALU = mybir.AluOpType
ENG = ['pe', 'act', 'dve', 'pool', 'sp']
NCORES = 8
SEQ = 2048
NT_SEQ = SEQ // 128
EPS = 1e-6
TWO_PI = 6.283185


class Sched:
    def __init__(self, nc, sems, dma_sems):
        self.nc = nc
        self.sem = sems
        self.cnt = {e: 0 for e in ENG}
        self.waited = {e: {} for e in ENG}
        self.prog = {e: [] for e in ENG}
        self.lastw = {}
        self.readers = {}
        self.dma_sems = dma_sems
        self.dma_cnt = [0] * len(dma_sems)
        self.dma_rr = 0
        self.group_keys = {}
        self.cur_group = None
        self.phase_tokens = {}
        self.phase_seen = set()
        self.capture = None

    def begin_capture(self):
        self.capture = []
        return self.capture

    def end_capture(self):
        self.capture = None

    def issue_merged(self, streams, grain=24, lead=None):
        if lead is None:
            lead = [0.0] * len(streams)
        lead = [l for l, s in zip(lead, streams) if s]
        streams = [s for s in streams if s]
        idx = [0] * len(streams)
        while True:
            best, bf = None, None
            for i, s in enumerate(streams):
                if idx[i] < len(s):
                    f = idx[i] / len(s) - lead[i]
                    if best is None or f < bf:
                        best, bf = i, f
            if best is None:
                break
            for _ in range(grain):
                if idx[best] >= len(streams[best]):
                    break
                kind, a, b, c, d = streams[best][idx[best]]
                idx[best] += 1
                if kind == 'op':
                    self.opn(a, b, c, d)
                else:
                    self.dma(a, b, c, d)

    def barrier(self):
        for E in ENG:
            self.drain_all(E)

    def begin_phase(self, group, aliases):
        toks = {}
        for g in aliases:
            for k in self.group_keys.get(g, ()):
                for tok in [self.lastw.get(k)] + list(self.readers.get(k, ())):
                    if tok is not None and toks.get(tok[0], 0) < tok[1]:
                        toks[tok[0]] = tok[1]
        self.cur_group = group
        self.phase_tokens = toks
        self.phase_seen = set()
        self.group_keys.setdefault(group, set())

    def _semh(self, key):
        if isinstance(key, tuple):
            return self.dma_sems[key[1]]
        return self.sem[key]

    def _deps(self, E, reads, writes):
        deps = {}

        def add(tok, raw):
            if tok is None:
                return
            k, v = tok
            if k == E and E == 'pe':
                return
            if deps.get(k, 0) < v:
                deps[k] = v
        for r in reads:
            add(self.lastw.get(r), True)
            if isinstance(r, str) and r.startswith('ps') and r[2:].isdigit():
                for t in self.readers.get(r, ()):
                    if t[0] != E:
                        add(t, False)
        for w in writes:
            add(self.lastw.get(w), False)
            for t in self.readers.get(w, ()):
                add(t, False)
            if w not in self.phase_seen:
                self.phase_seen.add(w)
                for k, v in self.phase_tokens.items():
                    if not (k == E and E == 'pe'):
                        if deps.get(k, 0) < v:
                            deps[k] = v
        if self.cur_group is not None:
            gk = self.group_keys[self.cur_group]
            gk.update(reads)
            gk.update(writes)
        return deps

    def _emit_waits(self, E, deps):
        for k, v in deps.items():
            if self.waited[E].get(k, 0) >= v:
                continue
            self.waited[E][k] = v
            h = self._semh(k)
            self.prog[E].append(lambda eng, h=h, v=v: eng.wait_ge(h, v))

    def _commit(self, tok, reads, writes):
        for r in reads:
            self.readers.setdefault(r, []).append(tok)
        for w in writes:
            self.lastw[w] = tok
            self.readers[w] = []

    def op(self, E, fn, reads=(), writes=()):
        self.opn(E, [fn], reads, writes)

    def opn(self, E, fns, reads=(), writes=()):
        if self.capture is not None:
            self.capture.append(('op', E, fns, tuple(reads), tuple(writes)))
            return
        deps = self._deps(E, reads, writes)
        self._emit_waits(E, deps)
        self.cnt[E] += 1
        v = self.cnt[E]
        h = self.sem[E]
        for f in fns[:-1]:
            self.prog[E].append(lambda eng, f=f: f(eng))
        self.prog[E].append(lambda eng, fn=fns[-1], h=h: fn(eng).then_inc(h, 1))
        self._commit((E, v), reads, writes)

    def dma(self, Q, fn, reads=(), writes=()):
        if self.capture is not None:
            self.capture.append(('dma', Q, fn, tuple(reads), tuple(writes)))
            return
        i = self.dma_rr
        self.dma_rr = (self.dma_rr + 1) % len(self.dma_sems)
        deps = self._deps(Q, reads, writes)
        if self.dma_cnt[i] > 0:
            k = ('dma', i)
            deps[k] = max(deps.get(k, 0), 16 * self.dma_cnt[i])
        self._emit_waits(Q, deps)
        self.dma_cnt[i] += 1
        v = 16 * self.dma_cnt[i]
        h = self.dma_sems[i]
        self.prog[Q].append(lambda eng, fn=fn, h=h: fn(eng).then_inc(h, 16))
        self._commit((('dma', i), v), reads, writes)

    def finish(self, E, res):
        deps = {}
        for r in res:
            tok = self.lastw.get(r)
            if tok is not None:
                k, v = tok
                if deps.get(k, 0) < v:
                    deps[k] = v
        self._emit_waits(E, deps)

    def drain_all(self, E):
        deps = {}
        for i, c in enumerate(self.dma_cnt):
            if c:
                deps[('dma', i)] = 16 * c
        for e in ENG:
            if e != E and self.cnt[e]:
                deps[e] = self.cnt[e]
        self._emit_waits(E, deps)

    def replay(self, block):
        prog = self.prog

        @block.tensor
        def _(e):
            for f in prog['pe']:
                f(e)

        @block.scalar
        def _(e):
            for f in prog['act']:
                f(e)

        @block.vector
        def _(e):
            for f in prog['dve']:
                f(e)

        @block.gpsimd
        def _(e):
            for f in prog['pool']:
                f(e)

        @block.sync
        def _(e):
            for f in prog['sp']:
                f(e)


def host_consts():
    c = {}
    c['identb'] = np.eye(128, dtype=ml_dtypes.bfloat16)
    c['identf'] = np.eye(128, dtype=np.float32)
    t = np.arange(128)
    same = (t[:, None] // 64) == (t[None, :] // 64)
    c['tri'] = (same & (t[:, None] <= t[None, :])).astype(np.float32)
    c['blk'] = same.astype(np.float32)
    c['onesA'] = np.repeat((t[:, None] < 64), 128, axis=1).astype(np.float32)
    c['onesB'] = np.repeat((t[:, None] >= 64), 128, axis=1).astype(np.float32)
    i = np.arange(64)
    c['i2'] = ((t[:, None] % 64) == i[None, :]).astype(np.float32)
    pj = (t % 64)[:, None]
    m = np.zeros((128, 3, 64), np.float32)
    m[:, 0, :] = np.where(i[None, :] >= pj, 0.0, -30000.0)
    m[:, 1, :] = np.where(i[None, :] > pj, 0.0, -30000.0)
    m[:, 2, :] = np.where(i[None, :] < pj, 0.0, -30000.0)
    c['maskneg'] = m
    jj = t // 16
    c['toepmask'] = (jj[None, :] >= jj[:, None]).astype(np.float32)
    c['tau8'] = np.ascontiguousarray(np.broadcast_to(np.arange(1, 9, dtype=np.float32)[None, None, :], (128, 16, 8)))
    c['cidx'] = np.ascontiguousarray(np.broadcast_to(np.arange(0, 17, dtype=np.float32)[None, None, :], (128, 16, 17)))
    r0 = np.ones((128, 16, 16), np.float32)
    r0[:, :, 0] = 0.0
    c['rho0'] = r0
    sel = np.zeros((128, 8, 16), np.float32)
    for tok in range(128):
        sel[tok, tok % 8, tok // 8] = 1.0
    c['selj'] = sel.astype(ml_dtypes.bfloat16)
    c['onesb'] = np.ones((128, 2), dtype=ml_dtypes.bfloat16)
    return c


CONST_SHAPES = {'identb': ([128, 128], BF16), 'identf': ([128, 128], F32), 'tri': ([128, 128], F32),
                'blk': ([128, 128], F32), 'onesA': ([128, 128], F32), 'onesB': ([128, 128], F32),
                'i2': ([128, 64], F32), 'maskneg': ([128, 3, 64], F32), 'toepmask': ([128, 128], F32),
                'tau8': ([128, 16, 8], F32), 'cidx': ([128, 16, 17], F32), 'rho0': ([128, 16, 16], F32),
                'selj': ([128, 8, 16], BF16), 'onesb': ([128, 2], BF16)}


def build_nc(ntiles=2 * NT_SEQ, dbg=False, stage=99):
    nc = bass.Bass("TRN2", target_bir_lowering=False)
    DI = lambda n, s, d=F32: nc.dram_tensor(n, s, d, kind="ExternalInput").ap()
    ntok = 2 * SEQ
    x = DI("x", [ntok, 1024]); p = DI("p", [ntok, 256])
    norm_mix_g = DI("norm_mix_g", [1024]); w_in = DI("w_in", [1024, 3080])
    A_re = DI("ssm_A_re", [32, 64]); A_im = DI("ssm_A_im", [32, 64])
    B_re = DI("ssm_B_re", [32, 64, 16]); B_im = DI("ssm_B_im", [32, 64, 16])
    C_re = DI("ssm_C_re", [32, 16, 64]); C_im = DI("ssm_C_im", [32, 16, 64])
    ssm_D = DI("ssm_D", [512]); log_dt = DI("ssm_log_dt", [32])
    w_glu = DI("ssm_w_glu", [512, 512]); b_glu = DI("ssm_b_glu", [512])
    conv_w = DI("dn_conv_w", [4, 1536]); A_log = DI("dn_A_log", [4]); dt_bias = DI("dn_dt_bias", [4])
    dn_norm_g = DI("dn_norm_g", [128]); w_out = DI("w_out", [1024, 1024])
    w_pp = DI("w_ple_proj", [256, 1024]); ple_g = DI("ple_norm_g", [1024])
    w_pg = DI("w_ple_gate", [1024, 1024]); fin_g = DI("final_norm_g", [1024])
    cd = {k: DI("c_" + k, s, d) for k, (s, d) in CONST_SHAPES.items()}
    out = nc.dram_tensor("out", [ntok, 1024], F32, kind="ExternalOutput").ap()

    es = ExitStack()
    with es:
        T = lambda n, s, d=F32: es.enter_context(nc.sbuf_tensor(n, s, d))
        wglu = T("wglu", [128, 4, 512], BF16)
        wba = T("wba", [128, 8, 8], BF16)
        NSLOT = 4
        wslot = [T("wslot%d" % i, [128, 8, 512], BF16) for i in range(NSLOT)]
        wscr = nc.dram_tensor("wscr", [11, 8, 128, 512], BF16, kind="Internal").ap()
        C = {k: T("k_" + k, s, d) for k, (s, d) in CONST_SHAPES.items()}
        toepT = T("toepT", [128, 32, 128], BF16)
        ptre = T("ptre", [128, 32, 64], BF16); ptim = T("ptim", [128, 32, 64], BF16)
        gqre = T("gqre", [128, 16, 256], BF16); gqimn = T("gqimn", [128, 16, 256], BF16)
        cosT = T("cosT", [128, 16, 17]); sinT = T("sinT", [128, 16, 17])
        rhoS = T("rhoS", [128, 16, 16]); rho = T("rho", [128, 16])
        diag = T("diag", [128, 4, 12, 128], BF16)
        gcol = T("gcol", [128, 8]); Dcol = T("Dcol", [128, 4]); bglu = T("bglu", [128, 4])
        cwcol = T("cwcol", [128, 4, 12])
        fing_b = T("fing_b", [128, 1024]); pleg_b = T("pleg_b", [128, 1024]); dng_b = T("dng_b", [128, 128])
        negA_b = T("negA_b", [128, 4]); dtb_b = T("dtb_b", [128, 4])
        carr = T("carr", [128, 2, 16])
        qkvp = T("qkvp", [128, 12, 131], BF16)
        S32 = T("S32", [128, 4, 128]); Sbf = T("Sbf", [128, 4, 128], BF16)
        pt = T("pt", [128, 256])
        st = T("st", [128, 8]); ssd = T("ssd", [128, 4]); rsd = T("rsd", [128, 4])
        zsf = T("zsf", [128, 4, 128], BF16); ysf = T("ysf", [128, 4, 128], BF16)
        qkv = T("qkv", [128, 12, 128], BF16); zd = T("zd", [128, 4, 128], BF16)
        ydf = T("ydf", [128, 4, 128], BF16)
        RA, RB_ = 64 * 1024, 26 * 1024
        arena = T("arena", [128, (RA + RB_) // 4])

        class Region:
            def __init__(self, base, size):
                self.base, self.size, self.off = base, size, 0

            def reset(self):
                self.off = 0
                return self

            def alloc(self, shape, dtype=F32):
                esz = 2 if dtype == BF16 else 4
                n = int(np.prod(shape[1:]))
                nb = (n * esz + 31) // 32 * 32
                assert self.off + nb <= self.size, (self.off, nb, self.size)
                b0 = self.base + self.off
                self.off += nb
                a = arena[0:shape[0], b0 // 4:(b0 + nb) // 4]
                if dtype != F32:
                    a = a.bitcast(dtype)
                a = a[:, 0:n]
                fs = list(shape[1:])
                if len(fs) == 2:
                    a = a.rearrange("p (a b) -> p a b", a=fs[0])
                elif len(fs) == 3:
                    a = a.rearrange("p (a b c) -> p a b c", a=fs[0], b=fs[1])
                elif len(fs) == 4:
                    a = a.rearrange("p (a b c d) -> p a b c d", a=fs[0], b=fs[1], c=fs[2])
                return a
        regA = Region(0, RA); regB = Region(RA, RB_); regS = Region(0, RA + RB_)
        regB.reset()
        xt = regB.alloc([128, 1024]); xn = regB.alloc([128, 1024], BF16); aT = regB.alloc([128, 8, 128], BF16)
        h1 = regB.alloc([128, 1024]); h1b = regB.alloc([128, 1024], BF16); h1T = regB.alloc([128, 8, 128], BF16)
        ee = regB.alloc([128, 1024]); sg = regB.alloc([128, 1024]); ot = ee
        ptb = regB.alloc([128, 256], BF16); pT = regB.alloc([128, 2, 128], BF16)
        regA.reset()
        u32 = regA.alloc([128, 4, 128]); ubf = regA.alloc([128, 4, 128], BF16)
        utok = regA.alloc([128, 512], BF16); cmf = regA.alloc([16, 4096], BF16); ucm = cmf.rearrange('p (g j i) -> p g j i', g=32, j=8); ycm = cmf.rearrange('p (j c) -> p j c', j=8)
        ug = regA.alloc([128, 32, 16], BF16)
        zin = regA.alloc([128, 2, 16, 16]); ztmp = regA.alloc([128, 4, 16, 16]); zs_ = regA.alloc([128, 2, 16, 16])
        snx = regA.alloc([128, 2, 16, 16]); sbf5 = regA.alloc([128, 2, 16, 16], BF16); ctmp = regA.alloc([128, 2, 16])
        yfm = regA.alloc([128, 4, 128]); y1 = regA.alloc([128, 4, 128], BF16)
        gate = regA.alloc([128, 4, 128]); y2 = gate
        sq = regA.alloc([128, 8, 128], BF16); kvt = regA.alloc([128, 8, 128], BF16)
        ba = regA.alloc([128, 16]); cum = regA.alloc([128, 16])
        sc_ = {n: regA.alloc([128, 4]) for n in ['sigb', 'lnb', 'ex', 'sp', 'g', 'c1', 'r1', 'r2',
                                                  'kbgs', 'kds', 'so', 'eglA', 'eglB', 'tmp']}
        lnr = regA.alloc([128, 8])
        RB = regA.alloc([128, 4, 3, 64]); Mx = regA.alloc([128, 4, 3, 64]); Bm = regA.alloc([128, 4, 3, 64])
        attnT = regA.alloc([128, 4, 64], BF16)
        NP = [regA.alloc([128, 4, 2, 64]) for i in range(2)]
        NTt = [regA.alloc([128, 4, 64]) for i in range(2)]
        Pf = regA.alloc([128, 4, 64], BF16)
        vb = regA.alloc([128, 4, 128], BF16); kbg = regA.alloc([128, 4, 128], BF16); kd = regA.alloc([128, 4, 128], BF16)
        ug32 = regA.alloc([128, 4, 128]); wT = regA.alloc([128, 4, 2, 64], BF16)
        vnew = regA.alloc([128, 4, 128], BF16); otmp = regA.alloc([128, 4, 128]); otok = regA.alloc([128, 4, 128])
        t1 = regA.alloc([128, 4, 128]); ydt = regA.alloc([128, 4, 128], BF16)
        regS.reset()
        NSTG = 8
        stg = [regS.alloc([128, 512]) for i in range(NSTG)]
        stgb = [regS.alloc([128, 512], BF16) for i in range(NSTG)]
        cst = regS.alloc([128, 128])
        s5 = {n: regS.alloc([128, 16]) for n in ['lr', 'li', 'ldt', 'dt', 'lrdt', 'lidt', 'nr', 'ni', 'den',
                                                  'cr', 'ci', 't0', 't1', 'phr']}
        s8 = {n: regS.alloc([128, 16, 8]) for n in ['magl', 'ang', 'mag', 'imag', 'sn', 'cs', 'pwr', 'pwi',
                                                     'ipr', 'ipi', 'a', 'b', 'c', 'd']}
        s8i = regS.alloc([128, 16, 8], I32)
        s17 = {n: regS.alloc([128, 16, 17]) for n in ['ang', 'a', 'b', 'c', 'd']}
        s17i = regS.alloc([128, 16, 17], I32)
        Bt = {n: regS.alloc([128, 16, 16]) for n in ['bre', 'bim', 'cre', 'cim', 'bbre', 'bbim', 'x', 'y']}
        H = {n: regS.alloc([128, 4, 8, 16]) for n in ['hre', 'him', 'gre', 'gim', 'h2re', 'h2im', 'x', 'y']}
        PS = [es.enter_context(nc.psum_tensor("ps%d" % i, [128, 512], F32)) for i in range(8)]
        sems = {e: es.enter_context(nc.semaphore("s_" + e)) for e in ENG}
        dsems = [es.enter_context(nc.semaphore("d%d" % i)) for i in range(16)]
        block = es.enter_context(nc.Block())
        S = Sched(nc, sems, dsems)
        psrr = [0]
        dbg_outs = []

        def DBG(name, ap, key, shape, dtype):
            if not dbg:
                return
            dt_ = nc.dram_tensor("dbg_" + name, shape, dtype, kind="ExternalOutput").ap()
            S.dma('sp', lambda e: e.dma_start(out=dt_, in_=ap), reads=[key], writes=['dbg_' + name])
            dbg_outs.append('dbg_' + name)

        bankset = {'cur': [0, 1, 2, 3, 4, 5, 6], 'pos': {}}

        def set_banks(lst):
            bankset['cur'] = lst

        def bank():
            lst = bankset['cur']
            k = tuple(lst)
            p_ = bankset['pos'].get(k, 0)
            bankset['pos'][k] = (p_ + 1) % len(lst)
            i = lst[p_]
            return PS[i], 'ps%d' % i

        wrr = {'A': 0, 'D': 0}
        wset = {'A': [0, 1], 'D': [2, 3]}

        def wget(chunk, who):
            i = wset[who][wrr[who]]
            wrr[who] = (wrr[who] + 1) % len(wset[who])
            sl = wslot[i]
            nkb = 4 if chunk == 10 else 8
            S.dma('sp', lambda e, sl=sl, chunk=chunk, nkb=nkb: e.dma_start(out=sl[:, 0:nkb, :], in_=wscr[chunk, 0:nkb].rearrange("kb p n -> p kb n")), reads=['wscr%d_%d' % (chunk, kq) for kq in range(nkb)], writes=['wslot%d' % i])
            return sl, 'wslot%d' % i

        def psf(b, *shape):
            n = int(np.prod(shape))
            a = b[:, 0:n]
            if len(shape) == 2:
                return a.rearrange("p (a b) -> p a b", a=shape[0])
            if len(shape) == 3:
                return a.rearrange("p (a b c) -> p a b c", a=shape[0], b=shape[1])
            return a

        def psb(b, *shape):
            n = int(np.prod(shape))
            a = b[:].bitcast(BF16)[:, 0:n]
            if len(shape) == 2:
                return a.rearrange("p (a b) -> p a b", a=shape[0])
            if len(shape) == 3:
                return a.rearrange("p (a b c) -> p a b c", a=shape[0], b=shape[1])
            return a

        TT = lambda E, o, a, b, op, r, w: S.op(E, lambda e: e.tensor_tensor(out=o, in0=a, in1=b, op=op), reads=r, writes=w)
        TS = lambda E, o, a, s1, s2, o0, o1, r, w: S.op(E, lambda e: e.tensor_scalar(out=o, in0=a, scalar1=s1, scalar2=s2, op0=o0, op1=o1) if o1 is not None else e.tensor_scalar(out=o, in0=a, scalar1=s1, scalar2=None, op0=o0), reads=r, writes=w)
        STT = lambda o, a, s, b, o0, o1, r, w: S.op('dve', lambda e: e.scalar_tensor_tensor(out=o, in0=a, scalar=s, in1=b, op0=o0, op1=o1), reads=r, writes=w)
        ACT = lambda o, a, f, r, w, bias=None, scale=None: S.op('act', lambda e: e.activation(out=o, in_=a, func=f, **({'bias': bias} if bias is not None else {}), **({'scale': scale} if scale is not None else {})), reads=r, writes=w)
        CP = lambda E, o, a, r, w: S.op(E, lambda e: e.tensor_copy(out=o, in_=a), reads=r, writes=w)
        MM = lambda o, l, rh, st_, sp_, r, w: S.op('pe', lambda e: e.matmul(o, lhsT=l, rhs=rh, start=st_, stop=sp_), reads=r, writes=w)
        MMS = lambda o, l, rh, st_, sp_, r, w: S.op('pe', lambda e: e.matmul(o, lhsT=l, rhs=rh, start=st_, stop=sp_, skip_group_check=True), reads=r, writes=w)
        TR = lambda o, a, idt, r, w: S.op('pe', lambda e: e.transpose(out=o, in_=a, identity=idt), reads=r, writes=w)
        LD = lambda o, a, w, q='sp': S.dma(q, lambda e: e.dma_start(out=o, in_=a), writes=w)
        LDS = lambda o, a, w, q='sp': S.dma(q, lambda e: e.dma_start(out=o, in_=a, allow_slow_non_contiguous=True), writes=w)

        caps_setup = {}

        def do_setup():
            S.begin_phase('SETUP', [])
            for k in CONST_SHAPES:
                LD(C[k][:], cd[k], ['C'])
            LDS(gcol[:], norm_mix_g.rearrange("(kb p) -> p kb", p=128), ['gcol'])
            LDS(Dcol[:], ssm_D.rearrange("(kb p) -> p kb", p=128), ['Dcol'])
            LDS(bglu[:], b_glu.rearrange("(kb p) -> p kb", p=128), ['bglu'])
            TS('dve', bglu[:], bglu[:], 0.5, None, ALU.mult, None, ['bglu'], ['bglu'])
            LDS(cwcol[:], conv_w.rearrange("t (b p) -> p t b", p=128), ['cwcol'])
            LD(fing_b[:], fin_g.rearrange("(o n) -> o n", o=1).to_broadcast([128, 1024]), ['fing'])
            LD(pleg_b[:], ple_g.rearrange("(o n) -> o n", o=1).to_broadcast([128, 1024]), ['pleg'])
            LD(dng_b[:], dn_norm_g.rearrange("(o n) -> o n", o=1).to_broadcast([128, 128]), ['dng'])
            LD(negA_b[:], A_log.rearrange("(o n) -> o n", o=1).to_broadcast([128, 4]), ['negA'])
            LD(dtb_b[:], dt_bias.rearrange("(o n) -> o n", o=1).to_broadcast([128, 4]), ['dtb'])
            if stage < 1:
                return
            ACT(negA_b[:], negA_b[:], AF.Exp, ['negA'], ['negA'])
            TS('dve', negA_b[:], negA_b[:], -1.0, None, ALU.mult, None, ['negA'], ['negA'])
            for par in range(2):
                ps_ = slice(64 * par, 64 * par + 64)
                LDS(s5['lr'][ps_, :], A_re.rearrange("(gp par) n -> par n gp", par=2)[par], ['lr'])
                LDS(s5['li'][ps_, :], A_im.rearrange("(gp par) n -> par n gp", par=2)[par], ['li'])
                LDS(s5['ldt'][ps_, :], log_dt.rearrange("(o gp par) -> par o gp", par=2, o=1)[par].to_broadcast([64, 16]), ['ldt'])
                LDS(Bt['bre'][ps_], B_re.rearrange("(gp par) n i -> par n gp i", par=2)[par], ['bre'])
                LDS(Bt['bim'][ps_], B_im.rearrange("(gp par) n i -> par n gp i", par=2)[par], ['bim'])
            caps_setup['S'] = S.begin_capture()
            for (Cs, dst, dk) in ((C_re, Bt['cre'], 'cre'), (C_im, Bt['cim'], 'cim')):
                for hf in range(2):
                    for gl in range(8):
                        gp = 8 * hf + gl
                        LD(cst[16 * gl:16 * gl + 16, :].rearrange("p (par n) -> p par n", par=2), Cs[2 * gp:2 * gp + 2].rearrange("par o n -> o par n"), ['cst'])
                    b, bk = bank()
                    S.op('pe', lambda e, b=b: e.transpose(out=b[:, 0:128], in_=cst[:], identity=C['identf'][:]), reads=['cst', 'C'], writes=[bk])
                    CP('dve', dst[:, 8 * hf:8 * hf + 8, :].rearrange("p g o -> p (g o)"), b[:, 0:128], [bk], [dk])
            if stage < 2:
                return
            ACT(s5['dt'][:], s5['ldt'][:], AF.Exp, ['ldt'], ['dt'])
            TT('dve', s5['lrdt'][:], s5['lr'][:], s5['dt'][:], ALU.mult, ['lr', 'dt'], ['lrdt'])
            TT('dve', s5['lidt'][:], s5['li'][:], s5['dt'][:], ALU.mult, ['li', 'dt'], ['lidt'])
            b8 = lambda t_: t_[:].rearrange("p (g o) -> p g o", o=1).to_broadcast([128, 16, 8])
            TT('dve', s8['magl'][:], C['tau8'][:], b8(s5['lrdt']), ALU.mult, ['C', 'lrdt'], ['magl'])
            TT('dve', s8['ang'][:], C['tau8'][:], b8(s5['lidt']), ALU.mult, ['C', 'lidt'], ['s8ang'])

            def sincos(ang, sn, cs, sc, sci, tag):
                for (dst, off) in ((sn, 32.0), (cs, 32.25)):
                    TS('dve', sc['a'][:], ang[:], 1.0 / (2 * math.pi), off, ALU.mult, ALU.add, [tag + 'ang'], [tag + 'a'])
                    CP('dve', sci[:], sc['a'][:], [tag + 'a'], [tag + 'i'])
                    CP('dve', sc['b'][:], sci[:], [tag + 'i'], [tag + 'b'])
                    TT('dve', sc['c'][:], sc['a'][:], sc['b'][:], ALU.subtract, [tag + 'a', tag + 'b'], [tag + 'c'])
                    TS('dve', sc['d'][:], sc['c'][:], 0.5, None, ALU.is_gt, None, [tag + 'c'], [tag + 'd'])
                    TT('dve', sc['c'][:], sc['c'][:], sc['d'][:], ALU.subtract, [tag + 'c', tag + 'd'], [tag + 'c'])
                    ACT(dst[:], sc['c'][:], AF.Sin, [tag + 'c'], [tag + ('sn' if dst is sn else 'cs')], scale=TWO_PI)

            if stage < 2.1:
                return
            sincos(s8['ang'], s8['sn'], s8['cs'], s8, s8i, 's8')
            if stage < 2.2:
                return
            ACT(s8['mag'][:], s8['magl'][:], AF.Exp, ['magl'], ['mag'])
            ACT(s8['imag'][:], s8['magl'][:], AF.Exp, ['magl'], ['imag'], scale=-1.0)
            TT('dve', s8['pwr'][:], s8['mag'][:], s8['cs'][:], ALU.mult, ['mag', 's8cs'], ['pwr'])
            TT('dve', s8['pwi'][:], s8['mag'][:], s8['sn'][:], ALU.mult, ['mag', 's8sn'], ['pwi'])
            TT('dve', s8['ipr'][:], s8['imag'][:], s8['cs'][:], ALU.mult, ['imag', 's8cs'], ['ipr'])
            STT(s8['ipi'][:], s8['imag'][:], -1.0, s8['sn'][:], ALU.mult, ALU.mult, ['imag', 's8sn'], ['ipi'])
            if stage < 2.3:
                return
            TS('dve', s5['nr'][:], s8['pwr'][:, :, 0], -1.0, None, ALU.add, None, ['pwr'], ['nr'])
            CP('dve', s5['ni'][:], s8['pwi'][:, :, 0], ['pwi'], ['ni'])
            TT('dve', s5['t0'][:], s5['lr'][:], s5['lr'][:], ALU.mult, ['lr'], ['t0'])
            TT('dve', s5['t1'][:], s5['li'][:], s5['li'][:], ALU.mult, ['li'], ['t1'])
            TT('dve', s5['den'][:], s5['t0'][:], s5['t1'][:], ALU.add, ['t0', 't1'], ['den'])
            S.op('dve', lambda e: e.reciprocal(out=s5['den'][:], in_=s5['den'][:]), reads=['den'], writes=['den'])
            TT('dve', s5['t0'][:], s5['nr'][:], s5['lr'][:], ALU.mult, ['nr', 'lr'], ['t0'])
            TT('dve', s5['t1'][:], s5['ni'][:], s5['li'][:], ALU.mult, ['ni', 'li'], ['t1'])
            TT('dve', s5['cr'][:], s5['t0'][:], s5['t1'][:], ALU.add, ['t0', 't1'], ['cr'])
            TT('dve', s5['cr'][:], s5['cr'][:], s5['den'][:], ALU.mult, ['cr', 'den'], ['cr'])
            TT('dve', s5['t0'][:], s5['ni'][:], s5['lr'][:], ALU.mult, ['ni', 'lr'], ['t0'])
            TT('dve', s5['t1'][:], s5['nr'][:], s5['li'][:], ALU.mult, ['nr', 'li'], ['t1'])
            TT('dve', s5['ci'][:], s5['t0'][:], s5['t1'][:], ALU.subtract, ['t0', 't1'], ['ci'])
            TT('dve', s5['ci'][:], s5['ci'][:], s5['den'][:], ALU.mult, ['ci', 'den'], ['ci'])
            b16 = lambda t_: t_[:].rearrange("p (g o) -> p g o", o=1).to_broadcast([128, 16, 16])
            if stage < 2.4:
                return
            TT('dve', Bt['x'][:], Bt['bre'][:], b16(s5['cr']), ALU.mult, ['bre', 'cr'], ['Bx'])
            if stage < 2.5:
                return
            TT('dve', Bt['y'][:], Bt['bim'][:], b16(s5['ci']), ALU.mult, ['bim', 'ci'], ['By'])
            if stage < 2.6:
                return
            TT('dve', Bt['bbre'][:], Bt['x'][:], Bt['y'][:], ALU.subtract, ['Bx', 'By'], ['bbre'])
            if stage < 2.7:
                return
            TT('dve', Bt['x'][:], Bt['bim'][:], b16(s5['cr']), ALU.mult, ['bim', 'cr'], ['Bx'])
            if stage < 2.8:
                return
            TT('dve', Bt['y'][:], Bt['bre'][:], b16(s5['ci']), ALU.mult, ['bre', 'ci'], ['By'])
            if stage < 2.9:
                return
            TT('dve', Bt['bbim'][:], Bt['x'][:], Bt['y'][:], ALU.add, ['Bx', 'By'], ['bbim'])
            if stage < 3:
                return
            fl = lambda t_: t_[:].rearrange("p g j i -> p g (j i)")
            for qt in range(4):
                for gl in range(4):
                    gp = qt * 4 + gl
                    pw_b = lambda t_: t_[:, gp, :].rearrange("p (j o) -> p j o", o=1).to_broadcast([128, 8, 16])
                    bb_b = lambda t_: t_[:, gp, :].rearrange("p (o i) -> p o i", o=1).to_broadcast([128, 8, 16])
                    hx, hy = H['x'][:, gl], H['y'][:, gl]
                    TT('dve', hx, pw_b(s8['ipr']), bb_b(Bt['bbre']), ALU.mult, ['ipr', 'bbre'], ['Hx'])
                    TT('dve', hy, pw_b(s8['ipi']), bb_b(Bt['bbim']), ALU.mult, ['ipi', 'bbim'], ['Hy'])
                    TT('dve', H['hre'][:, gl], hx, hy, ALU.subtract, ['Hx', 'Hy'], ['hre'])
                    TT('dve', hx, pw_b(s8['ipr']), bb_b(Bt['bbim']), ALU.mult, ['ipr', 'bbim'], ['Hx'])
                    TT('dve', hy, pw_b(s8['ipi']), bb_b(Bt['bbre']), ALU.mult, ['ipi', 'bbre'], ['Hy'])
                    TT('dve', H['him'][:, gl], hx, hy, ALU.add, ['Hx', 'Hy'], ['him'])
                    TT('dve', hx, pw_b(s8['pwr']), bb_b(Bt['cre']), ALU.mult, ['pwr', 'cre'], ['Hx'])
                    TT('dve', hy, pw_b(s8['pwi']), bb_b(Bt['cim']), ALU.mult, ['pwi', 'cim'], ['Hy'])
                    TT('dve', H['gre'][:, gl], hx, hy, ALU.subtract, ['Hx', 'Hy'], ['gre'])
                    TT('dve', hx, pw_b(s8['pwi']), bb_b(Bt['cre']), ALU.mult, ['pwi', 'cre'], ['Hx'])
                    TT('dve', hy, pw_b(s8['pwr']), bb_b(Bt['cim']), ALU.mult, ['pwr', 'cim'], ['Hy'])
                    TT('dve', H['gim'][:, gl], hx, hy, ALU.add, ['Hx', 'Hy'], ['gim'])
                if stage < 3.1:
                    return
                p8 = lambda t_: t_[:, qt * 4:qt * 4 + 4, 7:8].to_broadcast([128, 4, 128])
                TT('dve', fl(H['x']), fl(H['hre']), p8(s8['pwr']), ALU.mult, ['hre', 'pwr'], ['Hx'])
                TT('dve', fl(H['y']), fl(H['him']), p8(s8['pwi']), ALU.mult, ['him', 'pwi'], ['Hy'])
                TT('dve', fl(H['h2re']), fl(H['x']), fl(H['y']), ALU.subtract, ['Hx', 'Hy'], ['h2re'])
                TT('dve', fl(H['x']), fl(H['him']), p8(s8['pwr']), ALU.mult, ['him', 'pwr'], ['Hx'])
                TT('dve', fl(H['y']), fl(H['hre']), p8(s8['pwi']), ALU.mult, ['hre', 'pwi'], ['Hy'])
                TT('dve', fl(H['h2im']), fl(H['x']), fl(H['y']), ALU.add, ['Hx', 'Hy'], ['h2im'])
                if stage < 3.2:
                    return
                TS('dve', fl(H['him']), fl(H['him']), -1.0, None, ALU.mult, None, ['him'], ['him'])
                if qt == 0:
                    S.op('dve', lambda e: e.memset(gqre[:], 0.0), writes=['gqre'])
                    S.op('dve', lambda e: e.memset(gqimn[:], 0.0), writes=['gqimn'])
                for par_ in range(2):
                    pp_ = slice(64 * par_, 64 * par_ + 64)
                    cc_ = slice(128 * par_, 128 * par_ + 128)
                    CP('dve', gqre[pp_, qt * 4:qt * 4 + 4, cc_], fl(H['gre'])[pp_], ['gre'], ['gqre'])
                    TS('dve', gqimn[pp_, qt * 4:qt * 4 + 4, cc_], fl(H['gim'])[pp_], -1.0, None, ALU.mult, None, ['gim'], ['gqimn'])
                if stage < 3.3:
                    return
                for gg in range(8):
                    g = qt * 8 + gg
                    gl, par = gg // 2, gg % 2
                    ps_ = slice(64 * par, 64 * par + 64)
                    b, bk = bank()
                    MM(b[:, 0:128], fl(H['hre'])[ps_, gl, :], fl(H['gre'])[ps_, gl, :], True, False, ['hre', 'gre'], [bk])
                    MM(b[:, 0:128], fl(H['him'])[ps_, gl, :], fl(H['gim'])[ps_, gl, :], False, True, ['him', 'gim'], [bk])
                    MM(b[:, 128:192], fl(H['h2re'])[ps_, gl, :], C['identf'][ps_, ps_], True, True, ['h2re', 'C'], [bk])
                    MM(b[:, 192:256], fl(H['h2im'])[ps_, gl, :], C['identf'][ps_, ps_], True, True, ['h2im', 'C'], [bk])
                    if stage < 3.4:
                        return
                    TT('dve', toepT[:, g, :], b[:, 0:128], C['toepmask'][:], ALU.mult, [bk, 'C'], ['toepT'])
                    if stage < 3.5:
                        return
                    ACT(ptre[:, g, :], b[:, 128:192], AF.Copy, [bk], ['ptre'])
                    ACT(ptim[:, g, :], b[:, 192:256], AF.Copy, [bk], ['ptim'])
            if stage < 4:
                return
            TS('dve', s5['t0'][:], s8['ang'][:, :, 7], 1.0 / (2 * math.pi), 32.0, ALU.mult, ALU.add, ['s8ang'], ['t0'])
            CP('dve', s8i[:, :, 0], s5['t0'][:], ['t0'], ['s8i'])
            CP('dve', s5['t1'][:], s8i[:, :, 0], ['s8i'], ['t1'])
            TT('dve', s5['phr'][:], s5['t0'][:], s5['t1'][:], ALU.subtract, ['t0', 't1'], ['phr'])
            TS('dve', s5['phr'][:], s5['phr'][:], 2 * math.pi, None, ALU.mult, None, ['phr'], ['phr'])
            TT('dve', s17['ang'][:], C['cidx'][:], s5['phr'][:].rearrange("p (g o) -> p g o", o=1).to_broadcast([128, 16, 17]), ALU.mult, ['C', 'phr'], ['s17ang'])
            sincos(s17['ang'], sinT, cosT, s17, s17i, 's17')
            CP('dve', rho[:], s8['mag'][:, :, 7], ['mag'], ['rho'])
            TT('dve', rhoS[:], C['rho0'][:], b16(rho), ALU.mult, ['C', 'rho'], ['rhoS'])
            if stage < 5:
                return
            for tp in range(4):
                for bq in range(12):
                    TS('dve', diag[:, tp, bq, :], C['identf'][:], cwcol[:, tp, bq:bq + 1], None, ALU.mult, None, ['C', 'cwcol'], ['diag'])
            if stage < 6:
                return
            S.end_capture()
            caps_setup['W'] = S.begin_capture()
            si = [0]

            def wscratch(srcap, chunk, kbslot, scale_col=None):
                k2 = si[0] % NSTG; si[0] += 1
                sb, sbb = stg[k2], stgb[k2]
                S.dma('act' if si[0] % 2 else 'sp', lambda e: e.dma_start(out=sb[:], in_=srcap), writes=['stg%d' % k2])
                if si[0] % 2:
                    if scale_col is not None:
                        TS('dve', sbb[:], sb[:], scale_col, None, ALU.mult, None, ['stg%d' % k2, 'gcol'], ['stgb%d' % k2])
                    else:
                        CP('dve', sbb[:], sb[:], ['stg%d' % k2], ['stgb%d' % k2])
                else:
                    if scale_col is not None:
                        ACT(sbb[:], sb[:], AF.Copy, ['stg%d' % k2, 'gcol'], ['stgb%d' % k2], scale=scale_col)
                    else:
                        ACT(sbb[:], sb[:], AF.Copy, ['stg%d' % k2], ['stgb%d' % k2])
                S.dma(['sp', 'act'][(si[0] // 2) % 2], lambda e: e.dma_start(out=wscr[chunk, kbslot], in_=sbb[:]), reads=['stgb%d' % k2], writes=['wscr%d_%d' % (chunk, kbslot)])
            for kb in range(8):
                rs = slice(kb * 128, (kb + 1) * 128)
                for cg in range(6):
                    wscratch(w_in[rs, cg * 512:(cg + 1) * 512], cg, kb, gcol[:, kb:kb + 1])
                for nh in range(2):
                    wscratch(w_out[rs, nh * 512:(nh + 1) * 512], 6 + nh, kb)
                    wscratch(w_pg[rs, nh * 512:(nh + 1) * 512], 8 + nh, kb)
                k2 = si[0] % NSTG; si[0] += 1
                S.dma('sp', lambda e, k2=k2, rs=rs: e.dma_start(out=stg[k2][:, 0:8], in_=w_in[rs, 3072:3080]), writes=['stg%d' % k2])
                TS('dve', wba[:, kb, :], stg[k2][:, 0:8], gcol[:, kb:kb + 1], None, ALU.mult, None, ['stg%d' % k2, 'gcol'], ['wba'])
            for kb in range(2):
                for nh in range(2):
                    wscratch(w_pp[kb * 128:(kb + 1) * 128, nh * 512:(nh + 1) * 512], 10, kb * 2 + nh)
            for kb in range(4):
                k2 = si[0] % NSTG; si[0] += 1
                S.dma('sp', lambda e, k2=k2, kb=kb: e.dma_start(out=stg[k2][:], in_=w_glu[kb * 128:(kb + 1) * 128, :]), writes=['stg%d' % k2])
                CP('dve', wglu[:, kb, :], stg[k2][:], ['stg%d' % k2], ['W'])

        do_setup()
        S.end_capture()
        S.issue_merged([caps_setup['W'], caps_setup['S']], grain=6)
        S.barrier()
        LN128H = -0.5 * math.log(128.0)
        sc = sc_
        gcum = None
        caps = []
        for ti in range(ntiles):
            tl = ti % NT_SEQ
            X = xt; xk = 'xt'
            r0 = ti * 128
            capA = S.begin_capture(); set_banks([3, 4, 5, 6])
            LD(X[:], x[r0:r0 + 128, :], [xk])
            S.op('act', lambda e, X=X: e.activation(out=xn[:], in_=X[:], func=AF.Square, accum_out=st[:, 0:1]), reads=[xk], writes=['xn', 'st0'])
            ACT(st[:, 1:2], st[:, 0:1], AF.Ln, ['st0'], ['st1'], bias=EPS, scale=1.0 / 1024)
            ACT(st[:, 1:2], st[:, 1:2], AF.Exp, ['st1'], ['st1'], scale=-0.5)
            TS('dve', xn[:], X[:], st[:, 1:2], None, ALU.mult, None, [xk, 'st1'], ['xn'])
            b, bk = bank()
            for kb in range(8):
                TR(psb(b, 8, 128)[:, kb, :], xn[:, kb * 128:(kb + 1) * 128], C['identb'][:], ['xn', 'C'], [bk])
            CP('dve', aT[:], psb(b, 8, 128), [bk], ['aT'])
            for grp in range(5):
                b, bk = bank()
                wsl, wk = wget(grp, 'A')
                for q4 in range(4):
                    cb = grp * 4 + q4
                    for kb in range(8):
                        MM(b[:, q4 * 128:(q4 + 1) * 128], wsl[:, kb, q4 * 128:(q4 + 1) * 128], aT[:, kb, :], kb == 0, kb == 7, [wk, 'aT'], [bk])
                v4 = psf(b, 4, 128)
                if grp == 0:
                    ACT(u32[:], v4, AF.Copy, [bk], ['u32'])
                    CP('dve', ubf[:], v4, [bk], ['ubf'])
                elif grp == 1:
                    ACT(zsf[:], v4, AF.Silu, [bk], ['zsf'])
                else:
                    o4 = qkvp[:, (grp - 2) * 4:(grp - 1) * 4, 3:131]
                    if grp % 2:
                        ACT(o4, v4, AF.Copy, [bk], ['qkvp'])
                    else:
                        CP('dve', o4, v4, [bk], ['qkvp'])
            b, bk = bank()
            wsl, wk = wget(5, 'A')
            for kb in range(8):
                MM(b[:, 0:512], aT[:, kb, :], wsl[:, kb, :], kb == 0, kb == 7, ['aT', wk], [bk])
            ACT(zd[:], psf(b, 4, 128), AF.Silu, [bk], ['zd'])
            bS, bSk = PS[7], 'ps7'
            for kb in range(8):
                MM(bS[:, 0:8], aT[:, kb, :], wba[:, kb, :], kb == 0, kb == 7, ['aT', 'wba'], [bSk])
            S.end_capture()
            capB = S.begin_capture(); set_banks([0, 1, 2])
            if tl == 0:
                S.op('pool', lambda e: e.memset(carr[:], 0.0), writes=['carr'])
            b, bk = bank()
            for q4 in range(4):
                TR(psb(b, 4, 128)[:, q4, :], ubf[:, q4, :], C['identb'][:], ['ubf', 'C'], [bk])
            CP('dve', utok[:], psb(b, 512), [bk], ['utok'])
            for j in range(8):
                b, bk = bank()
                MM(b[0:16, 0:512], C['selj'][:, j, :], utok[:], True, True, ['C', 'utok'], [bk])
                srcv = b[0:16, 0:512].rearrange("p (g i) -> p g i", g=32)
                if j % 2:
                    ACT(ucm[:, :, j, :], srcv, AF.Copy, [bk], ['cm'])
                else:
                    CP('dve', ucm[:, :, j, :], srcv, [bk], ['cm'])
            b, bk = bank()
            for g in range(32):
                TR(psb(b, 32, 16)[:, g, :], ucm[:, g].rearrange('p j i -> p (j i)'), C['identb'][0:16, 0:16], ['cm', 'C'], [bk])
            CP('dve', ug[:], psb(b, 32, 16), [bk], ['ug'])
            bR, bRk = bank(); bI, bIk = bank()
            vRr = psf(bR, 16, 2, 16); vRi = psf(bI, 16, 2, 16)
            for gp in range(16):
                rhs_ = ug[:, 2 * gp:2 * gp + 2, :].rearrange("p g c -> p (g c)")
                MM(vRr[:, gp].rearrange("p a c -> p (a c)"), ptre[:, 2 * gp:2 * gp + 2, :].rearrange("p g n -> p (g n)"), rhs_, True, True, ['ptre', 'ug'], [bRk])
                MM(vRi[:, gp].rearrange("p a c -> p (a c)"), ptim[:, 2 * gp:2 * gp + 2, :].rearrange("p g n -> p (g n)"), rhs_, True, True, ['ptim', 'ug'], [bIk])
            c1v, s1v = cosT[:, :, 1:17], sinT[:, :, 1:17]
            for par_ in range(2):
                pp_ = slice(64 * par_, 64 * par_ + 64)
                TT('dve', ztmp[pp_, 0], vRr[pp_, :, par_, :], c1v[pp_], ALU.mult, [bRk, 's17cs'], ['zt0'])
                TT('dve', ztmp[pp_, 1], vRi[pp_, :, par_, :], s1v[pp_], ALU.mult, [bIk, 's17sn'], ['zt1'])
                TT('dve', ztmp[pp_, 2], vRi[pp_, :, par_, :], c1v[pp_], ALU.mult, [bIk, 's17cs'], ['zt2'])
                TT('dve', ztmp[pp_, 3], vRr[pp_, :, par_, :], s1v[pp_], ALU.mult, [bRk, 's17sn'], ['zt3'])
            TT('dve', zin[:, 0], ztmp[:, 0], ztmp[:, 1], ALU.add, ['zt0', 'zt1'], ['zin'])
            TT('dve', zin[:, 1], ztmp[:, 2], ztmp[:, 3], ALU.subtract, ['zt2', 'zt3'], ['zin'])
            TT('dve', ctmp[:], carr[:], rho[:].rearrange("p (o g) -> p o g", o=1).to_broadcast([128, 2, 16]), ALU.mult, ['carr', 'rho'], ['ctmp'])
            TT('dve', zin[:, :, :, 0], zin[:, :, :, 0], ctmp[:], ALU.add, ['zin', 'ctmp'], ['zin'])
            for ri in range(2):
                S.op('dve', lambda e, ri=ri: e.tensor_tensor_scan(out=zs_[:, ri].rearrange("p g c -> p (g c)"), data0=rhoS[:].rearrange("p g c -> p (g c)"), data1=zin[:, ri].rearrange("p g c -> p (g c)"), initial=0.0, op0=ALU.mult, op1=ALU.add), reads=['zin', 'rhoS'], writes=['zs%d' % ri])
            TT('dve', ztmp[:, 0], zs_[:, 0], c1v, ALU.mult, ['zs0', 's17cs'], ['zt0'])
            TT('dve', ztmp[:, 1], zs_[:, 1], s1v, ALU.mult, ['zs1', 's17sn'], ['zt1'])
            TT('dve', ztmp[:, 2], zs_[:, 0], s1v, ALU.mult, ['zs0', 's17sn'], ['zt2'])
            TT('dve', ztmp[:, 3], zs_[:, 1], c1v, ALU.mult, ['zs1', 's17cs'], ['zt3'])
            TT('dve', snx[:, 0], ztmp[:, 0], ztmp[:, 1], ALU.subtract, ['zt0', 'zt1'], ['snx0'])
            TT('dve', snx[:, 1], ztmp[:, 2], ztmp[:, 3], ALU.add, ['zt2', 'zt3'], ['snx1'])
            CP('dve', sbf5[:, :, :, 0], carr[:], ['carr'], ['sbf5'])
            ACT(sbf5[:, :, :, 1:16], snx[:, :, :, 0:15], AF.Copy, ['snx0', 'snx1'], ['sbf5'])
            CP('dve', carr[:], snx[:, :, :, 15], ['snx0', 'snx1'], ['carr'])
            for g4 in range(8):
                b, bk = bank()
                for pi in range(2):
                    gp = g4 * 2 + pi
                    g0, g1 = 2 * gp, 2 * gp + 1
                    o_ = b[0:16, pi * 256:(pi + 1) * 256]
                    MMS(o_[:, 0:128], ug[:, g0, :], toepT[:, g0, :], True, False, ['ug', 'toepT'], [bk])
                    MMS(o_[:, 128:256], ug[:, g1, :], toepT[:, g1, :], False, False, ['ug', 'toepT'], [bk])
                    MMS(o_, sbf5[:, 0, gp, :], gqre[:, gp, :], False, False, ['sbf5', 'gqre'], [bk])
                    MMS(o_, sbf5[:, 1, gp, :], gqimn[:, gp, :], False, True, ['sbf5', 'gqimn'], [bk])
                src = b[0:16, 0:512].rearrange("p (g j o) -> p g j o", g=4, j=8)
                dst = ycm[:, :, g4 * 64:(g4 + 1) * 64].rearrange("p j (g o) -> p g j o", g=4)
                if g4 % 2:
                    ACT(dst, src, AF.Copy, [bk], ['cm'])
                else:
                    CP('dve', dst, src, [bk], ['cm'])
            b, bk = bank()
            vY = psb(b, 4, 8, 16)
            for q4 in range(4):
                for j in range(8):
                    TR(vY[:, q4, j, :], ycm[:, j, q4 * 128:(q4 + 1) * 128], C['identb'][0:16, 0:16], ['cm', 'C'], [bk])
            for q4 in range(4):
                STT(yfm[:, q4, :].rearrange("p (c j) -> p j c", j=8), u32[:, q4, :].rearrange("p (c j) -> p j c", j=8), Dcol[:, q4:q4 + 1], vY[:, q4], ALU.mult, ALU.add, ['u32', 'Dcol', bk], ['yfm'])
            if ti == 0:
                DBG('u32', u32[:], 'u32', [128, 4, 128], F32)
                DBG('yfm', yfm[:], 'yfm', [128, 4, 128], F32)
                DBG('zsf', zsf[:], 'zsf', [128, 4, 128], BF16)
            ACT(y1[:], yfm[:], AF.Gelu_apprx_tanh, ['yfm'], ['y1'])
            b, bk = bank()
            for mb in range(4):
                for kb in range(4):
                    MM(b[:, mb * 128:(mb + 1) * 128], wglu[:, kb, mb * 128:(mb + 1) * 128], y1[:, kb, :], kb == 0, kb == 3, ['W', 'y1'], [bk])
            for mb in range(4):
                ACT(gate[:, mb, :], b[:, mb * 128:(mb + 1) * 128], AF.Tanh, [bk, 'bglu'], ['gate'], bias=bglu[:, mb:mb + 1], scale=0.5)
            STT(gate[:], gate[:], 1.0, zsf[:], ALU.add, ALU.mult, ['gate', 'zsf'], ['gate'])
            STT(ysf[:], gate[:], 0.5, y1[:], ALU.mult, ALU.mult, ['gate', 'y1'], ['ysf'])
            if ti == 0:
                DBG('ysf', ysf[:], 'ysf', [128, 4, 128], BF16)
            S.end_capture()
            capC = S.begin_capture(); set_banks([3, 4, 5, 6])
            if tl == 0:
                S.op('pool', lambda e: e.memset(qkvp[:, :, 0:3], 0.0), writes=['halo'])
                S.op('pool', lambda e: e.memset(S32[:], 0.0), writes=['S32'])
                S.op('pool', lambda e: e.memset(Sbf[:], 0.0), writes=['Sbf'])
            for g3 in range(3):
                b, bk = bank()
                for q4 in range(4):
                    bq = g3 * 4 + q4
                    for tp in range(4):
                        MM(b[:, q4 * 128:(q4 + 1) * 128], diag[:, tp, bq, :], qkvp[:, bq, tp:tp + 128], tp == 0, tp == 3, ['diag', 'qkvp', 'halo'], [bk])
                ACT(qkv[:, g3 * 4:(g3 + 1) * 4, :], psf(b, 4, 128), AF.Silu, [bk], ['qkv'])
            CP('pool', qkvp[:, :, 0:3], qkvp[:, :, 128:131], ['qkvp'], ['halo'])
            ACT(sq[:], qkv[:, 0:8, :], AF.Square, ['qkv'], ['sq'])
            for bq in range(8):
                MM(bS[:, 8 + bq:9 + bq], sq[:, bq, :], C['onesb'][:, 0:1], True, True, ['sq', 'C'], [bSk])
            b, bk = bank()
            for i8 in range(8):
                TR(psb(b, 8, 128)[:, i8, :], qkv[:, 4 + i8, :], C['identb'][:], ['qkv', 'C'], [bk])
            CP('dve', kvt[:], psb(b, 8, 128), [bk], ['kvt'])
            CP('dve', ba[:], bS[:, 0:16], [bSk], ['ba'])
            ACT(sc['sigb'][:], ba[:, 0:4], AF.Sigmoid, ['ba'], ['sigb'])
            TT('dve', sc['tmp'][:], ba[:, 4:8], dtb_b[:], ALU.add, ['ba', 'dtb'], ['tmp'])
            ACT(sc['ex'][:], sc['tmp'][:], AF.Exp, ['tmp'], ['ex'])
            ACT(sc['lnb'][:], sc['sigb'][:], AF.Ln, ['sigb'], ['lnb'])
            ACT(sc['sp'][:], sc['ex'][:], AF.Ln, ['ex'], ['sp'], bias=1.0)
            ACT(lnr[:], ba[:, 8:16], AF.Ln, ['ba'], ['lnr'], bias=EPS)
            TT('dve', sc['g'][:], sc['sp'][:], negA_b[:], ALU.mult, ['sp', 'negA'], ['g'])
            TS('dve', lnr[:], lnr[:], -0.5, None, ALU.mult, None, ['lnr'], ['lnr'])
            for i4, nm in enumerate(['tri', 'blk', 'onesA', 'onesB']):
                MM(bS[:, 16 + 4 * i4:20 + 4 * i4], C[nm][:], sc['g'][:], True, True, ['C', 'g'], [bSk])
            CP('dve', cum[:], bS[:, 16:32], [bSk], ['cum'])
            gc_, gls_, glA_, glB_ = cum[:, 0:4], cum[:, 4:8], cum[:, 8:12], cum[:, 12:16]
            lq, lk = lnr[:, 0:4], lnr[:, 4:8]
            TT('dve', sc['c1'][:], lk, gc_, ALU.subtract, ['lnr', 'cum'], ['c1'])
            STT(sc['r1'][:], gc_, LN128H, lq, ALU.add, ALU.add, ['cum', 'lnr'], ['r1'])
            TT('dve', sc['r2'][:], gc_, sc['lnb'][:], ALU.add, ['cum', 'lnb'], ['r2'])
            TT('dve', sc['r2'][:], sc['r2'][:], lk, ALU.add, ['r2', 'lnr'], ['r2'])
            TT('dve', sc['tmp'][:], sc['c1'][:], gls_, ALU.add, ['c1', 'cum'], ['tmp2'])
            ACT(sc['kbgs'][:], sc['r2'][:], AF.Exp, ['r2'], ['kbgs'])
            ACT(sc['kds'][:], sc['tmp'][:], AF.Exp, ['tmp2'], ['kds'])
            ACT(sc['so'][:], sc['r1'][:], AF.Exp, ['r1'], ['so'])
            ACT(sc['eglA'][:], glA_, AF.Exp, ['cum'], ['eglA'])
            ACT(sc['eglB'][:], glB_, AF.Exp, ['cum'], ['eglB'])
            i2b = C['i2'][:].rearrange("p (o i) -> p o i", o=1).to_broadcast([128, 4, 64])
            hb = lambda t_, n: t_[:].rearrange("p (h o) -> p h o", o=1).to_broadcast([128, 4, n])
            for kd_, nm in enumerate(['r1', 'r2', 'c1']):
                TT('dve', RB[:, :, kd_, :], i2b, hb(sc[nm], 64), ALU.mult, ['C', nm], ['RB'])
            TT('pool', vb[:], kvt[:, 4:8, :], hb(sc['sigb'], 128), ALU.mult, ['kvt', 'sigb'], ['vb'])
            TT('pool', kbg[:], kvt[:, 0:4, :], hb(sc['kbgs'], 128), ALU.mult, ['kvt', 'kbgs'], ['kbg'])
            TT('dve', kd[:], kvt[:, 0:4, :], hb(sc['kds'], 128), ALU.mult, ['kvt', 'kds'], ['kd'])
            for hh in range(2):
                b, bk = bank()
                for h2_ in range(2):
                    h = 2 * hh + h2_
                    o_ = b[:, h2_ * 192:(h2_ + 1) * 192]
                    MM(o_, C['blk'][:], RB[:, h].rearrange("p k i -> p (k i)"), True, True, ['C', 'RB'], [bk])
                TT('dve', Bm[:, 2 * hh:2 * hh + 2].rearrange("p h k i -> p h (k i)"), b[:, 0:384].rearrange("p (h n) -> p h n", h=2), C['maskneg'][:].rearrange("p k i -> p (k i)").rearrange("p (o n) -> p o n", o=1).to_broadcast([128, 2, 192]), ALU.add, [bk, 'C'], ['Bm'])
                v_ = Bm[:, 2 * hh:2 * hh + 2]
                bk = 'Bm'
                for h2_ in range(2):
                    h = 2 * hh + h2_
                    ACT(Mx[:, h, 0:2, :], v_[:, h2_, 0:2, :], AF.Exp, [bk, 'c1'], ['Mx'], bias=sc['c1'][:, h:h + 1])
                    ACT(Mx[:, h, 2, :], v_[:, h2_, 2, :], AF.Exp, [bk, 'r2'], ['Mx'], bias=sc['r2'][:, h:h + 1])
            b, bk = bank()
            vS = psf(b, 4, 2, 64)
            for h in range(4):
                for ch in range(2):
                    ps_ = slice(64 * ch, 64 * ch + 64)
                    MM(vS[ps_, h, :, :], qkv[:, 4 + h, ps_], qkv[:, h:h + 5:4, ps_], True, True, ['qkv'], [bk])
            TT('dve', attnT[:], vS[:, :, 0, :], Mx[:, :, 0, :], ALU.mult, [bk, 'Mx'], ['attnT'])
            STT(NP[0][:, :, 0, :], vS[:, :, 1, :], -1.0, Mx[:, :, 1, :], ALU.mult, ALU.mult, [bk, 'Mx'], ['NP0a'])
            STT(NTt[0][:], vS[:, :, 1, :], -1.0, Mx[:, :, 2, :], ALU.mult, ALU.mult, [bk, 'Mx'], ['NT0'])
            TT('dve', NP[0][:, :, 1, :], NP[0][:, :, 0, :], C['i2'][:].rearrange("p (o i) -> p o i", o=1).to_broadcast([128, 4, 64]), ALU.add, ['NP0a', 'C'], ['NP0b'])
            for s in range(6):
                cur, nxt = s % 2, (s + 1) % 2
                NPc, NPn, NTc, NTn = NP[cur], NP[nxt], NTt[cur], NTt[nxt]
                kNa, kNb, kT = 'NP%da' % cur, 'NP%db' % cur, 'NT%d' % cur
                nNa, nNb, nT = 'NP%da' % nxt, 'NP%db' % nxt, 'NT%d' % nxt
                bA, bAk = bank()
                vA = psf(bA, 4, 128)
                for h in range(4):
                    for ch in range(2):
                        ps_ = slice(64 * ch, 64 * ch + 64)
                        if s == 0:
                            MM(vA[ps_, h, 0:64], NTc[ps_, h, :], NPc[ps_, h, 0, :], True, True, [kT, kNa], [bAk])
                        elif s < 5:
                            MM(vA[ps_, h, :], NTc[ps_, h, :], NPc[ps_, h, :, :].rearrange("p k i -> p (k i)"), True, True, [kT, kNa, kNb], [bAk])
                        else:
                            MM(vA[ps_, h, 64:128], NTc[ps_, h, :], NPc[ps_, h, 1, :], True, True, [kT, kNb], [bAk])
                if s < 5:
                    bB, bBk = bank()
                    vB = psf(bB, 4, 64)
                    for h in range(4):
                        for ch in range(2):
                            ps_ = slice(64 * ch, 64 * ch + 64)
                            MM(vB[ps_, h, :], NPc[ps_, h, 0, :], NTc[ps_, h, :], True, True, [kNa, kT], [bBk])
                    ACT(NPn[:, :, 0, :], vA[:, :, 0:64], AF.Copy, [bAk], [nNa])
                    ACT(NTn[:], vB, AF.Copy, [bBk], [nT])
                if s == 0:
                    CP('dve', NPn[:, :, 1, :], NPc[:, :, 1, :], [kNb], [nNb])
                elif s < 5:
                    TT('dve', NPn[:, :, 1, :], NPc[:, :, 1, :], vA[:, :, 64:128], ALU.add, [kNb, bAk], [nNb])
                else:
                    TT('dve', Pf[:], NPc[:, :, 1, :], vA[:, :, 64:128], ALU.add, [kNb, bAk], ['Pf'])
            bU, bUk = bank(); bW, bWk = bank()
            vU = psf(bU, 4, 128); vW = psf(bW, 4, 2, 64)
            for h in range(4):
                for ch in range(2):
                    ps_ = slice(64 * ch, 64 * ch + 64)
                    MM(vU[ps_, h, :], Pf[ps_, h, :], vb[ps_, h, :], True, True, ['Pf', 'vb'], [bUk])
                    MM(vW[:, h, ch, :], kbg[ps_, h, :], Pf[ps_, h, :], True, True, ['kbg', 'Pf'], [bWk])
            ACT(ug32[:], vU, AF.Copy, [bUk], ['ug32'])
            CP('dve', wT[:], vW, [bWk], ['wT'])
            b1, b1k = bank(); b2, b2k = bank(); b3, b3k = bank()
            v1, v2, v3 = psf(b1, 4, 128), psf(b2, 4, 128), psf(b3, 4, 128)
            for ch in range(2):
                ps_ = slice(64 * ch, 64 * ch + 64)
                egl = sc['eglA'] if ch == 0 else sc['eglB']
                eglk = 'eglA' if ch == 0 else 'eglB'
                for h in range(4):
                    MM(v1[ps_, h, :], wT[:, h, ch, :], Sbf[:, h, :], True, True, ['wT', 'Sbf'], [b1k])
                    MM(v2[ps_, h, :], qkv[:, h, ps_], Sbf[:, h, :], True, True, ['qkv', 'Sbf'], [b2k])
                TT('dve', vnew[ps_], ug32[ps_], v1[ps_], ALU.subtract, ['ug32', b1k], ['vnew'])
                bS2, bS2k = bank()
                vS2 = psf(bS2, 4, 128)
                for h in range(4):
                    MM(v3[ps_, h, :], attnT[ps_, h, :], vnew[ps_, h, :], True, True, ['attnT', 'vnew'], [b3k])
                    MM(vS2[:, h, :], kd[ps_, h, :], vnew[ps_, h, :], True, True, ['kd', 'vnew'], [bS2k])
                TT('dve', S32[:], S32[:], hb(egl, 128), ALU.mult, ['S32', eglk], ['S32'])
                TT('dve', S32[:], S32[:], vS2, ALU.add, ['S32', bS2k], ['S32'])
                ACT(Sbf[:], S32[:], AF.Copy, ['S32'], ['Sbf'])
                TT('dve', otmp[ps_], v2[ps_], sc['so'][ps_].rearrange("p (h o) -> p h o", o=1).to_broadcast([64, 4, 128]), ALU.mult, [b2k, 'so'], ['otmp'])
                TT('dve', otok[ps_], otmp[ps_], v3[ps_], ALU.add, ['otmp', b3k], ['otok'])
            if ti == 0:
                DBG('qkv', qkv[:], 'qkv', [128, 12, 128], BF16)
                DBG('otok', otok[:], 'otok', [128, 4, 128], F32)
            for h in range(4):
                S.op('act', lambda e, h=h: e.activation(out=t1[:, h, :], in_=otok[:, h, :], func=AF.Square, accum_out=ssd[:, h:h + 1]), reads=['otok'], writes=['t1', 'ssd'])
            ACT(rsd[:], ssd[:], AF.Ln, ['ssd'], ['rsd'], bias=EPS, scale=1.0 / 128)
            ACT(rsd[:], rsd[:], AF.Exp, ['rsd'], ['rsd'], scale=-0.5)
            TT('dve', t1[:], otok[:], dng_b[:].rearrange("p (o d) -> p o d", o=1).to_broadcast([128, 4, 128]), ALU.mult, ['otok', 'dng'], ['t1'])
            TT('dve', t1[:], t1[:], zd[:], ALU.mult, ['t1', 'zd'], ['t1'])
            TT('dve', ydt[:], t1[:], hb(rsd, 128), ALU.mult, ['t1', 'rsd'], ['ydt'])
            b, bk = bank()
            for h in range(4):
                TR(psb(b, 4, 128)[:, h, :], ydt[:, h, :], C['identb'][:], ['ydt', 'C'], [bk])
            CP('dve', ydf[:], psb(b, 4, 128), [bk], ['ydf'])
            S.end_capture()
            capD = S.begin_capture(); set_banks([0, 1, 2])
            LD(h1[:], x[r0:r0 + 128, :], ['h1'])
            LD(pt[:], p[r0:r0 + 128, :], ['pt'])
            for nh in range(2):
                b, bk = bank()
                wsl, wk = wget(6 + nh, 'D')
                for kb in range(8):
                    l_ = ysf[:, kb, :] if kb < 4 else ydf[:, kb - 4, :]
                    MM(b[:, 0:512], l_, wsl[:, kb, :], kb == 0, kb == 7, ['ysf', 'ydf', wk], [bk])
                TT('dve', h1[:, nh * 512:(nh + 1) * 512], h1[:, nh * 512:(nh + 1) * 512], b[:, 0:512], ALU.add, ['h1', bk], ['h1'])
            if ti == 0:
                DBG('ydf', ydf[:], 'ydf', [128, 4, 128], BF16)
                DBG('h1', h1[:], 'h1', [128, 1024], F32)
            CP('pool', ptb[:], pt[:], ['pt'], ['ptb'])
            b, bk = bank()
            for kb in range(2):
                TR(psb(b, 2, 128)[:, kb, :], ptb[:, kb * 128:(kb + 1) * 128], C['identb'][:], ['ptb', 'C'], [bk])
            CP('dve', pT[:], psb(b, 2, 128), [bk], ['pT'])
            be = []
            wsl, wk = wget(10, 'D')
            for nh in range(2):
                b, bk = bank()
                be.append((b, bk))
                for kb in range(2):
                    MM(b[:, 0:512], pT[:, kb, :], wsl[:, kb * 2 + nh, :], kb == 0, kb == 1, ['pT', wk], [bk])
                S.op('act', lambda e, b=b, nh=nh: e.activation(out=ee[:, nh * 512:(nh + 1) * 512], in_=b[:, 0:512], func=AF.Square, accum_out=st[:, 2 + nh:3 + nh]), reads=[bk], writes=['ee', 'st2'])
            TT('dve', st[:, 4:5], st[:, 2:3], st[:, 3:4], ALU.add, ['st2'], ['st4'])
            ACT(st[:, 4:5], st[:, 4:5], AF.Ln, ['st4'], ['st4'], bias=EPS, scale=1.0 / 1024)
            ACT(st[:, 4:5], st[:, 4:5], AF.Exp, ['st4'], ['st4'], scale=-0.5)
            for nh in range(2):
                b, bk = be[nh]
                STT(ee[:, nh * 512:(nh + 1) * 512], b[:, 0:512], st[:, 4:5], pleg_b[:, nh * 512:(nh + 1) * 512], ALU.mult, ALU.mult, [bk, 'st4', 'pleg'], ['ee'])
            ACT(h1b[:], h1[:], AF.Copy, ['h1'], ['h1b'])
            b, bk = bank()
            for kb in range(8):
                TR(psb(b, 8, 128)[:, kb, :], h1b[:, kb * 128:(kb + 1) * 128], C['identb'][:], ['h1b', 'C'], [bk])
            CP('dve', h1T[:], psb(b, 8, 128), [bk], ['h1T'])
            for nh in range(2):
                b, bk = bank()
                wsl, wk = wget(8 + nh, 'D')
                for kb in range(8):
                    MM(b[:, 0:512], h1T[:, kb, :], wsl[:, kb, :], kb == 0, kb == 7, ['h1T', wk], [bk])
                ACT(sg[:, nh * 512:(nh + 1) * 512], b[:, 0:512], AF.Sigmoid, [bk], ['sg'])
            TT('dve', sg[:], sg[:], ee[:], ALU.mult, ['sg', 'ee'], ['sg'])
            TT('dve', h1[:], h1[:], sg[:], ALU.add, ['h1', 'sg'], ['h1'])
            S.op('act', lambda e: e.activation(out=sg[:], in_=h1[:], func=AF.Square, accum_out=st[:, 5:6]), reads=['h1'], writes=['sg', 'st5'])
            ACT(st[:, 6:7], st[:, 5:6], AF.Ln, ['st5'], ['st6'], bias=EPS, scale=1.0 / 1024)
            ACT(st[:, 6:7], st[:, 6:7], AF.Exp, ['st6'], ['st6'], scale=-0.5)
            O = ee; ok = 'ee'
            STT(O[:], h1[:], st[:, 6:7], fing_b[:], ALU.mult, ALU.mult, ['h1', 'st6', 'fing'], [ok])
            S.dma('act', lambda e, O=O, r0=r0: e.dma_start(out=out[r0:r0 + 128, :], in_=O[:]), reads=[ok], writes=['out%d' % ti])
            S.end_capture()
            caps.append((capA, capB, capC, capD))
        set_banks([0, 1, 2, 3, 4, 5, 6])
        if caps:
            S.issue_merged([caps[0][0]])
        for ti in range(ntiles):
            S.issue_merged([caps[ti][1], caps[ti][2]], lead=[0.0, 0.2])
            S.issue_merged([caps[ti][3]] + ([caps[ti + 1][0]] if ti + 1 < ntiles else []), lead=[0.15, 0.0])
        S.finish('sp', ['out%d' % ti for ti in range(ntiles)])
        S.finish('act', ['out%d' % ti for ti in range(ntiles)])
        S.finish('sp', dbg_outs)
        S.drain_all('sp')
        S.replay(block)
    return nc


_CACHE = {}


def kernel(**inputs):
    x = np.ascontiguousarray(inputs['x'], dtype=np.float32)
    p = np.ascontiguousarray(inputs['p'], dtype=np.float32)[0]
    B = x.shape[0]
    per = B // NCORES
    consts = host_consts()
    shared = {}
    for k in ['norm_mix_g', 'w_in', 'ssm_A_re', 'ssm_A_im', 'ssm_B_re', 'ssm_B_im', 'ssm_C_re', 'ssm_C_im',
              'ssm_D', 'ssm_log_dt', 'ssm_w_glu', 'ssm_b_glu', 'dn_conv_w', 'dn_A_log', 'dn_dt_bias',
              'dn_norm_g', 'w_out', 'w_ple_proj', 'ple_norm_g', 'w_ple_gate']:
        shared[k] = np.ascontiguousarray(np.asarray(inputs[k], dtype=np.float32)[0])
    shared['final_norm_g'] = np.ascontiguousarray(inputs['final_norm_g'], dtype=np.float32)
    for k, v in consts.items():
        shared['c_' + k] = v
    if 'nc' not in _CACHE:
        _CACHE['nc'] = build_nc()
    nc = _CACHE['nc']
    in_maps = []
    for c in range(NCORES):
        m = dict(shared)
        m['x'] = x[c * per:(c + 1) * per].reshape(per * SEQ, 1024)
        m['p'] = p[c * per:(c + 1) * per].reshape(per * SEQ, 256)
        in_maps.append(m)
    res = run_bass_kernel_spmd(nc, in_maps, core_ids=list(range(NCORES)))
    outs = [np.asarray(r['out']).reshape(per, SEQ, 1024) for r in res.results]
    return np.concatenate(outs, axis=0).astype(np.float32)
```

```python
import math
from contextlib import ExitStack
import numpy as np
import ml_dtypes
import concourse.bass as bass
import concourse.mybir as mybir
from concourse.bass_utils import run_bass_kernel_spmd

F32 = mybir.dt.float32
BF16 = mybir.dt.bfloat16
I32 = mybir.dt.int32
AF = mybir.ActivationFunctionType
ALU = mybir.AluOpType
ENG = ['pe', 'act', 'dve', 'pool', 'sp']
NCORES = 8
SEQ = 2048
NT_SEQ = SEQ // 128
EPS = 1e-6
TWO_PI = 6.283185


class Sched:
    def __init__(self, nc, sems, dma_sems):
        self.nc = nc
        self.sem = sems
        self.cnt = {e: 0 for e in ENG}
        self.waited = {e: {} for e in ENG}
        self.prog = {e: [] for e in ENG}
        self.lastw = {}
        self.readers = {}
        self.dma_sems = dma_sems
        self.dma_cnt = [0] * len(dma_sems)
        self.dma_rr = 0
        self.group_keys = {}
        self.cur_group = None
        self.phase_tokens = {}
        self.phase_seen = set()
        self.capture = None

    def begin_capture(self):
        self.capture = []
        return self.capture

    def end_capture(self):
        self.capture = None

    def issue_merged(self, streams, grain=24, lead=None):
        if lead is None:
            lead = [0.0] * len(streams)
        lead = [l for l, s in zip(lead, streams) if s]
        streams = [s for s in streams if s]
        idx = [0] * len(streams)
        while True:
            best, bf = None, None
            for i, s in enumerate(streams):
                if idx[i] < len(s):
                    f = idx[i] / len(s) - lead[i]
                    if best is None or f < bf:
                        best, bf = i, f
            if best is None:
                break
            for _ in range(grain):
                if idx[best] >= len(streams[best]):
                    break
                kind, a, b, c, d = streams[best][idx[best]]
                idx[best] += 1
                if kind == 'op':
                    self.opn(a, b, c, d)
                else:
                    self.dma(a, b, c, d)

    def barrier(self):
        for E in ENG:
            self.drain_all(E)

    def begin_phase(self, group, aliases):
        toks = {}
        for g in aliases:
            for k in self.group_keys.get(g, ()):
                for tok in [self.lastw.get(k)] + list(self.readers.get(k, ())):
                    if tok is not None and toks.get(tok[0], 0) < tok[1]:
                        toks[tok[0]] = tok[1]
        self.cur_group = group
        self.phase_tokens = toks
        self.phase_seen = set()
        self.group_keys.setdefault(group, set())

    def _semh(self, key):
        if isinstance(key, tuple):
            return self.dma_sems[key[1]]
        return self.sem[key]

    def _deps(self, E, reads, writes):
        deps = {}

        def add(tok, raw):
            if tok is None:
                return
            k, v = tok
            if k == E and E == 'pe':
                return
            if deps.get(k, 0) < v:
                deps[k] = v
        for r in reads:
            add(self.lastw.get(r), True)
            if isinstance(r, str) and r.startswith('ps') and r[2:].isdigit():
                for t in self.readers.get(r, ()):
                    if t[0] != E:
                        add(t, False)
        for w in writes:
            add(self.lastw.get(w), False)
            for t in self.readers.get(w, ()):
                add(t, False)
            if w not in self.phase_seen:
                self.phase_seen.add(w)
                for k, v in self.phase_tokens.items():
                    if not (k == E and E == 'pe'):
                        if deps.get(k, 0) < v:
                            deps[k] = v
        if self.cur_group is not None:
            gk = self.group_keys[self.cur_group]
            gk.update(reads)
            gk.update(writes)
        return deps

    def _emit_waits(self, E, deps):
        for k, v in deps.items():
            if self.waited[E].get(k, 0) >= v:
                continue
            self.waited[E][k] = v
            h = self._semh(k)
            self.prog[E].append(lambda eng, h=h, v=v: eng.wait_ge(h, v))

    def _commit(self, tok, reads, writes):
        for r in reads:
            self.readers.setdefault(r, []).append(tok)
        for w in writes:
            self.lastw[w] = tok
            self.readers[w] = []

    def op(self, E, fn, reads=(), writes=()):
        self.opn(E, [fn], reads, writes)

    def opn(self, E, fns, reads=(), writes=()):
        if self.capture is not None:
            self.capture.append(('op', E, fns, tuple(reads), tuple(writes)))
            return
        deps = self._deps(E, reads, writes)
        self._emit_waits(E, deps)
        self.cnt[E] += 1
        v = self.cnt[E]
        h = self.sem[E]
        for f in fns[:-1]:
            self.prog[E].append(lambda eng, f=f: f(eng))
        self.prog[E].append(lambda eng, fn=fns[-1], h=h: fn(eng).then_inc(h, 1))
        self._commit((E, v), reads, writes)

    def dma(self, Q, fn, reads=(), writes=()):
        if self.capture is not None:
            self.capture.append(('dma', Q, fn, tuple(reads), tuple(writes)))
            return
        i = self.dma_rr
        self.dma_rr = (self.dma_rr + 1) % len(self.dma_sems)
        deps = self._deps(Q, reads, writes)
        if self.dma_cnt[i] > 0:
            k = ('dma', i)
            deps[k] = max(deps.get(k, 0), 16 * self.dma_cnt[i])
        self._emit_waits(Q, deps)
        self.dma_cnt[i] += 1
        v = 16 * self.dma_cnt[i]
        h = self.dma_sems[i]
        self.prog[Q].append(lambda eng, fn=fn, h=h: fn(eng).then_inc(h, 16))
        self._commit((('dma', i), v), reads, writes)

    def finish(self, E, res):
        deps = {}
        for r in res:
            tok = self.lastw.get(r)
            if tok is not None:
                k, v = tok
                if deps.get(k, 0) < v:
                    deps[k] = v
        self._emit_waits(E, deps)

    def drain_all(self, E):
        deps = {}
        for i, c in enumerate(self.dma_cnt):
            if c:
                deps[('dma', i)] = 16 * c
        for e in ENG:
            if e != E and self.cnt[e]:
                deps[e] = self.cnt[e]
        self._emit_waits(E, deps)

    def replay(self, block):
        prog = self.prog

        @block.tensor
        def _(e):
            for f in prog['pe']:
                f(e)

        @block.scalar
        def _(e):
            for f in prog['act']:
                f(e)

        @block.vector
        def _(e):
            for f in prog['dve']:
                f(e)

        @block.gpsimd
        def _(e):
            for f in prog['pool']:
                f(e)

        @block.sync
        def _(e):
            for f in prog['sp']:
                f(e)


def host_consts():
    c = {}
    c['identb'] = np.eye(128, dtype=ml_dtypes.bfloat16)
    c['identf'] = np.eye(128, dtype=np.float32)
    t = np.arange(128)
    same = (t[:, None] // 64) == (t[None, :] // 64)
    c['tri'] = (same & (t[:, None] <= t[None, :])).astype(np.float32)
    c['blk'] = same.astype(np.float32)
    c['onesA'] = np.repeat((t[:, None] < 64), 128, axis=1).astype(np.float32)
    c['onesB'] = np.repeat((t[:, None] >= 64), 128, axis=1).astype(np.float32)
    i = np.arange(64)
    c['i2'] = ((t[:, None] % 64) == i[None, :]).astype(np.float32)
    pj = (t % 64)[:, None]
    m = np.zeros((128, 3, 64), np.float32)
    m[:, 0, :] = np.where(i[None, :] >= pj, 0.0, -30000.0)
    m[:, 1, :] = np.where(i[None, :] > pj, 0.0, -30000.0)
    m[:, 2, :] = np.where(i[None, :] < pj, 0.0, -30000.0)
    c['maskneg'] = m
    jj = t // 16
    c['toepmask'] = (jj[None, :] >= jj[:, None]).astype(np.float32)
    c['tau8'] = np.ascontiguousarray(np.broadcast_to(np.arange(1, 9, dtype=np.float32)[None, None, :], (128, 16, 8)))
    c['cidx'] = np.ascontiguousarray(np.broadcast_to(np.arange(0, 17, dtype=np.float32)[None, None, :], (128, 16, 17)))
    r0 = np.ones((128, 16, 16), np.float32)
    r0[:, :, 0] = 0.0
    c['rho0'] = r0
    sel = np.zeros((128, 8, 16), np.float32)
    for tok in range(128):
        sel[tok, tok % 8, tok // 8] = 1.0
    c['selj'] = sel.astype(ml_dtypes.bfloat16)
    c['onesb'] = np.ones((128, 2), dtype=ml_dtypes.bfloat16)
    return c


CONST_SHAPES = {'identb': ([128, 128], BF16), 'identf': ([128, 128], F32), 'tri': ([128, 128], F32),
                'blk': ([128, 128], F32), 'onesA': ([128, 128], F32), 'onesB': ([128, 128], F32),
                'i2': ([128, 64], F32), 'maskneg': ([128, 3, 64], F32), 'toepmask': ([128, 128], F32),
                'tau8': ([128, 16, 8], F32), 'cidx': ([128, 16, 17], F32), 'rho0': ([128, 16, 16], F32),
                'selj': ([128, 8, 16], BF16), 'onesb': ([128, 2], BF16)}


def build_nc(ntiles=2 * NT_SEQ, dbg=False, stage=99):
    nc = bass.Bass("TRN2", target_bir_lowering=False)
    DI = lambda n, s, d=F32: nc.dram_tensor(n, s, d, kind="ExternalInput").ap()
    ntok = 2 * SEQ
    x = DI("x", [ntok, 1024]); p = DI("p", [ntok, 256])
    norm_mix_g = DI("norm_mix_g", [1024]); w_in = DI("w_in", [1024, 3080])
    A_re = DI("ssm_A_re", [32, 64]); A_im = DI("ssm_A_im", [32, 64])
    B_re = DI("ssm_B_re", [32, 64, 16]); B_im = DI("ssm_B_im", [32, 64, 16])
    C_re = DI("ssm_C_re", [32, 16, 64]); C_im = DI("ssm_C_im", [32, 16, 64])
    ssm_D = DI("ssm_D", [512]); log_dt = DI("ssm_log_dt", [32])
    w_glu = DI("ssm_w_glu", [512, 512]); b_glu = DI("ssm_b_glu", [512])
    conv_w = DI("dn_conv_w", [4, 1536]); A_log = DI("dn_A_log", [4]); dt_bias = DI("dn_dt_bias", [4])
    dn_norm_g = DI("dn_norm_g", [128]); w_out = DI("w_out", [1024, 1024])
    w_pp = DI("w_ple_proj", [256, 1024]); ple_g = DI("ple_norm_g", [1024])
    w_pg = DI("w_ple_gate", [1024, 1024]); fin_g = DI("final_norm_g", [1024])
    cd = {k: DI("c_" + k, s, d) for k, (s, d) in CONST_SHAPES.items()}
    out = nc.dram_tensor("out", [ntok, 1024], F32, kind="ExternalOutput").ap()

    es = ExitStack()
    with es:
        T = lambda n, s, d=F32: es.enter_context(nc.sbuf_tensor(n, s, d))
        wglu = T("wglu", [128, 4, 512], BF16)
        wba = T("wba", [128, 8, 8], BF16)
        NSLOT = 4
        wslot = [T("wslot%d" % i, [128, 8, 512], BF16) for i in range(NSLOT)]
        wscr = nc.dram_tensor("wscr", [11, 8, 128, 512], BF16, kind="Internal").ap()
        C = {k: T("k_" + k, s, d) for k, (s, d) in CONST_SHAPES.items()}
        toepT = T("toepT", [128, 32, 128], BF16)
        ptre = T("ptre", [128, 32, 64], BF16); ptim = T("ptim", [128, 32, 64], BF16)
        gqre = T("gqre", [128, 16, 256], BF16); gqimn = T("gqimn", [128, 16, 256], BF16)
        cosT = T("cosT", [128, 16, 17]); sinT = T("sinT", [128, 16, 17])
        rhoS = T("rhoS", [128, 16, 16]); rho = T("rho", [128, 16])
        diag = T("diag", [128, 4, 12, 128], BF16)
        gcol = T("gcol", [128, 8]); Dcol = T("Dcol", [128, 4]); bglu = T("bglu", [128, 4])
        cwcol = T("cwcol", [128, 4, 12])
        fing_b = T("fing_b", [128, 1024]); pleg_b = T("pleg_b", [128, 1024]); dng_b = T("dng_b", [128, 128])
        negA_b = T("negA_b", [128, 4]); dtb_b = T("dtb_b", [128, 4])
        carr = T("carr", [128, 2, 16])
        qkvp = T("qkvp", [128, 12, 131], BF16)
        S32 = T("S32", [128, 4, 128]); Sbf = T("Sbf", [128, 4, 128], BF16)
        pt = T("pt", [128, 256])
        st = T("st", [128, 8]); ssd = T("ssd", [128, 4]); rsd = T("rsd", [128, 4])
        zsf = T("zsf", [128, 4, 128], BF16); ysf = T("ysf", [128, 4, 128], BF16)
        qkv = T("qkv", [128, 12, 128], BF16); zd = T("zd", [128, 4, 128], BF16)
        ydf = T("ydf", [128, 4, 128], BF16)
        RA, RB_ = 64 * 1024, 26 * 1024
        arena = T("arena", [128, (RA + RB_) // 4])

        class Region:
            def __init__(self, base, size):
                self.base, self.size, self.off = base, size, 0

            def reset(self):
                self.off = 0
                return self

            def alloc(self, shape, dtype=F32):
                esz = 2 if dtype == BF16 else 4
                n = int(np.prod(shape[1:]))
                nb = (n * esz + 31) // 32 * 32
                assert self.off + nb <= self.size, (self.off, nb, self.size)
                b0 = self.base + self.off
                self.off += nb
                a = arena[0:shape[0], b0 // 4:(b0 + nb) // 4]
                if dtype != F32:
                    a = a.bitcast(dtype)
                a = a[:, 0:n]
                fs = list(shape[1:])
                if len(fs) == 2:
                    a = a.rearrange("p (a b) -> p a b", a=fs[0])
                elif len(fs) == 3:
                    a = a.rearrange("p (a b c) -> p a b c", a=fs[0], b=fs[1])
                elif len(fs) == 4:
                    a = a.rearrange("p (a b c d) -> p a b c d", a=fs[0], b=fs[1], c=fs[2])
                return a
        regA = Region(0, RA); regB = Region(RA, RB_); regS = Region(0, RA + RB_)
        regB.reset()
        xt = regB.alloc([128, 1024]); xn = regB.alloc([128, 1024], BF16); aT = regB.alloc([128, 8, 128], BF16)
        h1 = regB.alloc([128, 1024]); h1b = regB.alloc([128, 1024], BF16); h1T = regB.alloc([128, 8, 128], BF16)
        ee = regB.alloc([128, 1024]); sg = regB.alloc([128, 1024]); ot = ee
        ptb = regB.alloc([128, 256], BF16); pT = regB.alloc([128, 2, 128], BF16)
        regA.reset()
        u32 = regA.alloc([128, 4, 128]); ubf = regA.alloc([128, 4, 128], BF16)
        utok = regA.alloc([128, 512], BF16); cmf = regA.alloc([16, 4096], BF16); ucm = cmf.rearrange('p (g j i) -> p g j i', g=32, j=8); ycm = cmf.rearrange('p (j c) -> p j c', j=8)
        ug = regA.alloc([128, 32, 16], BF16)
        zin = regA.alloc([128, 2, 16, 16]); ztmp = regA.alloc([128, 4, 16, 16]); zs_ = regA.alloc([128, 2, 16, 16])
        snx = regA.alloc([128, 2, 16, 16]); sbf5 = regA.alloc([128, 2, 16, 16], BF16); ctmp = regA.alloc([128, 2, 16])
        yfm = regA.alloc([128, 4, 128]); y1 = regA.alloc([128, 4, 128], BF16)
        gate = regA.alloc([128, 4, 128]); y2 = gate
        sq = regA.alloc([128, 8, 128], BF16); kvt = regA.alloc([128, 8, 128], BF16)
        ba = regA.alloc([128, 16]); cum = regA.alloc([128, 16])
        sc_ = {n: regA.alloc([128, 4]) for n in ['sigb', 'lnb', 'ex', 'sp', 'g', 'c1', 'r1', 'r2',
                                                  'kbgs', 'kds', 'so', 'eglA', 'eglB', 'tmp']}
        lnr = regA.alloc([128, 8])
        RB = regA.alloc([128, 4, 3, 64]); Mx = regA.alloc([128, 4, 3, 64]); Bm = regA.alloc([128, 4, 3, 64])
        attnT = regA.alloc([128, 4, 64], BF16)
        NP = [regA.alloc([128, 4, 2, 64]) for i in range(2)]
        NTt = [regA.alloc([128, 4, 64]) for i in range(2)]
        Pf = regA.alloc([128, 4, 64], BF16)
        vb = regA.alloc([128, 4, 128], BF16); kbg = regA.alloc([128, 4, 128], BF16); kd = regA.alloc([128, 4, 128], BF16)
        ug32 = regA.alloc([128, 4, 128]); wT = regA.alloc([128, 4, 2, 64], BF16)
        vnew = regA.alloc([128, 4, 128], BF16); otmp = regA.alloc([128, 4, 128]); otok = regA.alloc([128, 4, 128])
        t1 = regA.alloc([128, 4, 128]); ydt = regA.alloc([128, 4, 128], BF16)
        regS.reset()
        NSTG = 8
        stg = [regS.alloc([128, 512]) for i in range(NSTG)]
        stgb = [regS.alloc([128, 512], BF16) for i in range(NSTG)]
        cst = regS.alloc([128, 128])
        s5 = {n: regS.alloc([128, 16]) for n in ['lr', 'li', 'ldt', 'dt', 'lrdt', 'lidt', 'nr', 'ni', 'den',
                                                  'cr', 'ci', 't0', 't1', 'phr']}
        s8 = {n: regS.alloc([128, 16, 8]) for n in ['magl', 'ang', 'mag', 'imag', 'sn', 'cs', 'pwr', 'pwi',
                                                     'ipr', 'ipi', 'a', 'b', 'c', 'd']}
        s8i = regS.alloc([128, 16, 8], I32)
        s17 = {n: regS.alloc([128, 16, 17]) for n in ['ang', 'a', 'b', 'c', 'd']}
        s17i = regS.alloc([128, 16, 17], I32)
        Bt = {n: regS.alloc([128, 16, 16]) for n in ['bre', 'bim', 'cre', 'cim', 'bbre', 'bbim', 'x', 'y']}
        H = {n: regS.alloc([128, 4, 8, 16]) for n in ['hre', 'him', 'gre', 'gim', 'h2re', 'h2im', 'x', 'y']}
        PS = [es.enter_context(nc.psum_tensor("ps%d" % i, [128, 512], F32)) for i in range(8)]
        sems = {e: es.enter_context(nc.semaphore("s_" + e)) for e in ENG}
        dsems = [es.enter_context(nc.semaphore("d%d" % i)) for i in range(16)]
        block = es.enter_context(nc.Block())
        S = Sched(nc, sems, dsems)
        psrr = [0]
        dbg_outs = []

        def DBG(name, ap, key, shape, dtype):
            if not dbg:
                return
            dt_ = nc.dram_tensor("dbg_" + name, shape, dtype, kind="ExternalOutput").ap()
            S.dma('sp', lambda e: e.dma_start(out=dt_, in_=ap), reads=[key], writes=['dbg_' + name])
            dbg_outs.append('dbg_' + name)

        bankset = {'cur': [0, 1, 2, 3, 4, 5, 6], 'pos': {}}

        def set_banks(lst):
            bankset['cur'] = lst

        def bank():
            lst = bankset['cur']
            k = tuple(lst)
            p_ = bankset['pos'].get(k, 0)
            bankset['pos'][k] = (p_ + 1) % len(lst)
            i = lst[p_]
            return PS[i], 'ps%d' % i

        wrr = {'A': 0, 'D': 0}
        wset = {'A': [0, 1], 'D': [2, 3]}

        def wget(chunk, who):
            i = wset[who][wrr[who]]
            wrr[who] = (wrr[who] + 1) % len(wset[who])
            sl = wslot[i]
            nkb = 4 if chunk == 10 else 8
            S.dma('sp', lambda e, sl=sl, chunk=chunk, nkb=nkb: e.dma_start(out=sl[:, 0:nkb, :], in_=wscr[chunk, 0:nkb].rearrange("kb p n -> p kb n")), reads=['wscr%d_%d' % (chunk, kq) for kq in range(nkb)], writes=['wslot%d' % i])
            return sl, 'wslot%d' % i

        def psf(b, *shape):
            n = int(np.prod(shape))
            a = b[:, 0:n]
            if len(shape) == 2:
                return a.rearrange("p (a b) -> p a b", a=shape[0])
            if len(shape) == 3:
                return a.rearrange("p (a b c) -> p a b c", a=shape[0], b=shape[1])
            return a

        def psb(b, *shape):
            n = int(np.prod(shape))
            a = b[:].bitcast(BF16)[:, 0:n]
            if len(shape) == 2:
                return a.rearrange("p (a b) -> p a b", a=shape[0])
            if len(shape) == 3:
                return a.rearrange("p (a b c) -> p a b c", a=shape[0], b=shape[1])
            return a

        TT = lambda E, o, a, b, op, r, w: S.op(E, lambda e: e.tensor_tensor(out=o, in0=a, in1=b, op=op), reads=r, writes=w)
        TS = lambda E, o, a, s1, s2, o0, o1, r, w: S.op(E, lambda e: e.tensor_scalar(out=o, in0=a, scalar1=s1, scalar2=s2, op0=o0, op1=o1) if o1 is not None else e.tensor_scalar(out=o, in0=a, scalar1=s1, scalar2=None, op0=o0), reads=r, writes=w)
        STT = lambda o, a, s, b, o0, o1, r, w: S.op('dve', lambda e: e.scalar_tensor_tensor(out=o, in0=a, scalar=s, in1=b, op0=o0, op1=o1), reads=r, writes=w)
        ACT = lambda o, a, f, r, w, bias=None, scale=None: S.op('act', lambda e: e.activation(out=o, in_=a, func=f, **({'bias': bias} if bias is not None else {}), **({'scale': scale} if scale is not None else {})), reads=r, writes=w)
        CP = lambda E, o, a, r, w: S.op(E, lambda e: e.tensor_copy(out=o, in_=a), reads=r, writes=w)
        MM = lambda o, l, rh, st_, sp_, r, w: S.op('pe', lambda e: e.matmul(o, lhsT=l, rhs=rh, start=st_, stop=sp_), reads=r, writes=w)
        MMS = lambda o, l, rh, st_, sp_, r, w: S.op('pe', lambda e: e.matmul(o, lhsT=l, rhs=rh, start=st_, stop=sp_, skip_group_check=True), reads=r, writes=w)
        TR = lambda o, a, idt, r, w: S.op('pe', lambda e: e.transpose(out=o, in_=a, identity=idt), reads=r, writes=w)
        LD = lambda o, a, w, q='sp': S.dma(q, lambda e: e.dma_start(out=o, in_=a), writes=w)
        LDS = lambda o, a, w, q='sp': S.dma(q, lambda e: e.dma_start(out=o, in_=a, allow_slow_non_contiguous=True), writes=w)

        caps_setup = {}

        def do_setup():
            S.begin_phase('SETUP', [])
            for k in CONST_SHAPES:
                LD(C[k][:], cd[k], ['C'])
            LDS(gcol[:], norm_mix_g.rearrange("(kb p) -> p kb", p=128), ['gcol'])
            LDS(Dcol[:], ssm_D.rearrange("(kb p) -> p kb", p=128), ['Dcol'])
            LDS(bglu[:], b_glu.rearrange("(kb p) -> p kb", p=128), ['bglu'])
            TS('dve', bglu[:], bglu[:], 0.5, None, ALU.mult, None, ['bglu'], ['bglu'])
            LDS(cwcol[:], conv_w.rearrange("t (b p) -> p t b", p=128), ['cwcol'])
            LD(fing_b[:], fin_g.rearrange("(o n) -> o n", o=1).to_broadcast([128, 1024]), ['fing'])
            LD(pleg_b[:], ple_g.rearrange("(o n) -> o n", o=1).to_broadcast([128, 1024]), ['pleg'])
            LD(dng_b[:], dn_norm_g.rearrange("(o n) -> o n", o=1).to_broadcast([128, 128]), ['dng'])
            LD(negA_b[:], A_log.rearrange("(o n) -> o n", o=1).to_broadcast([128, 4]), ['negA'])
            LD(dtb_b[:], dt_bias.rearrange("(o n) -> o n", o=1).to_broadcast([128, 4]), ['dtb'])
            if stage < 1:
                return
            ACT(negA_b[:], negA_b[:], AF.Exp, ['negA'], ['negA'])
            TS('dve', negA_b[:], negA_b[:], -1.0, None, ALU.mult, None, ['negA'], ['negA'])
            for par in range(2):
                ps_ = slice(64 * par, 64 * par + 64)
                LDS(s5['lr'][ps_, :], A_re.rearrange("(gp par) n -> par n gp", par=2)[par], ['lr'])
                LDS(s5['li'][ps_, :], A_im.rearrange("(gp par) n -> par n gp", par=2)[par], ['li'])
                LDS(s5['ldt'][ps_, :], log_dt.rearrange("(o gp par) -> par o gp", par=2, o=1)[par].to_broadcast([64, 16]), ['ldt'])
                LDS(Bt['bre'][ps_], B_re.rearrange("(gp par) n i -> par n gp i", par=2)[par], ['bre'])
                LDS(Bt['bim'][ps_], B_im.rearrange("(gp par) n i -> par n gp i", par=2)[par], ['bim'])
            caps_setup['S'] = S.begin_capture()
            for (Cs, dst, dk) in ((C_re, Bt['cre'], 'cre'), (C_im, Bt['cim'], 'cim')):
                for hf in range(2):
                    for gl in range(8):
                        gp = 8 * hf + gl
                        LD(cst[16 * gl:16 * gl + 16, :].rearrange("p (par n) -> p par n", par=2), Cs[2 * gp:2 * gp + 2].rearrange("par o n -> o par n"), ['cst'])
                    b, bk = bank()
                    S.op('pe', lambda e, b=b: e.transpose(out=b[:, 0:128], in_=cst[:], identity=C['identf'][:]), reads=['cst', 'C'], writes=[bk])
                    CP('dve', dst[:, 8 * hf:8 * hf + 8, :].rearrange("p g o -> p (g o)"), b[:, 0:128], [bk], [dk])
            if stage < 2:
                return
            ACT(s5['dt'][:], s5['ldt'][:], AF.Exp, ['ldt'], ['dt'])
            TT('dve', s5['lrdt'][:], s5['lr'][:], s5['dt'][:], ALU.mult, ['lr', 'dt'], ['lrdt'])
            TT('dve', s5['lidt'][:], s5['li'][:], s5['dt'][:], ALU.mult, ['li', 'dt'], ['lidt'])
            b8 = lambda t_: t_[:].rearrange("p (g o) -> p g o", o=1).to_broadcast([128, 16, 8])
            TT('dve', s8['magl'][:], C['tau8'][:], b8(s5['lrdt']), ALU.mult, ['C', 'lrdt'], ['magl'])
            TT('dve', s8['ang'][:], C['tau8'][:], b8(s5['lidt']), ALU.mult, ['C', 'lidt'], ['s8ang'])

            def sincos(ang, sn, cs, sc, sci, tag):
                for (dst, off) in ((sn, 32.0), (cs, 32.25)):
                    TS('dve', sc['a'][:], ang[:], 1.0 / (2 * math.pi), off, ALU.mult, ALU.add, [tag + 'ang'], [tag + 'a'])
                    CP('dve', sci[:], sc['a'][:], [tag + 'a'], [tag + 'i'])
                    CP('dve', sc['b'][:], sci[:], [tag + 'i'], [tag + 'b'])
                    TT('dve', sc['c'][:], sc['a'][:], sc['b'][:], ALU.subtract, [tag + 'a', tag + 'b'], [tag + 'c'])
                    TS('dve', sc['d'][:], sc['c'][:], 0.5, None, ALU.is_gt, None, [tag + 'c'], [tag + 'd'])
                    TT('dve', sc['c'][:], sc['c'][:], sc['d'][:], ALU.subtract, [tag + 'c', tag + 'd'], [tag + 'c'])
                    ACT(dst[:], sc['c'][:], AF.Sin, [tag + 'c'], [tag + ('sn' if dst is sn else 'cs')], scale=TWO_PI)

            if stage < 2.1:
                return
            sincos(s8['ang'], s8['sn'], s8['cs'], s8, s8i, 's8')
            if stage < 2.2:
                return
            ACT(s8['mag'][:], s8['magl'][:], AF.Exp, ['magl'], ['mag'])
            ACT(s8['imag'][:], s8['magl'][:], AF.Exp, ['magl'], ['imag'], scale=-1.0)
            TT('dve', s8['pwr'][:], s8['mag'][:], s8['cs'][:], ALU.mult, ['mag', 's8cs'], ['pwr'])
            TT('dve', s8['pwi'][:], s8['mag'][:], s8['sn'][:], ALU.mult, ['mag', 's8sn'], ['pwi'])
            TT('dve', s8['ipr'][:], s8['imag'][:], s8['cs'][:], ALU.mult, ['imag', 's8cs'], ['ipr'])
            STT(s8['ipi'][:], s8['imag'][:], -1.0, s8['sn'][:], ALU.mult, ALU.mult, ['imag', 's8sn'], ['ipi'])
            if stage < 2.3:
                return
            TS('dve', s5['nr'][:], s8['pwr'][:, :, 0], -1.0, None, ALU.add, None, ['pwr'], ['nr'])
            CP('dve', s5['ni'][:], s8['pwi'][:, :, 0], ['pwi'], ['ni'])
            TT('dve', s5['t0'][:], s5['lr'][:], s5['lr'][:], ALU.mult, ['lr'], ['t0'])
            TT('dve', s5['t1'][:], s5['li'][:], s5['li'][:], ALU.mult, ['li'], ['t1'])
            TT('dve', s5['den'][:], s5['t0'][:], s5['t1'][:], ALU.add, ['t0', 't1'], ['den'])
            S.op('dve', lambda e: e.reciprocal(out=s5['den'][:], in_=s5['den'][:]), reads=['den'], writes=['den'])
            TT('dve', s5['t0'][:], s5['nr'][:], s5['lr'][:], ALU.mult, ['nr', 'lr'], ['t0'])
            TT('dve', s5['t1'][:], s5['ni'][:], s5['li'][:], ALU.mult, ['ni', 'li'], ['t1'])
            TT('dve', s5['cr'][:], s5['t0'][:], s5['t1'][:], ALU.add, ['t0', 't1'], ['cr'])
            TT('dve', s5['cr'][:], s5['cr'][:], s5['den'][:], ALU.mult, ['cr', 'den'], ['cr'])
            TT('dve', s5['t0'][:], s5['ni'][:], s5['lr'][:], ALU.mult, ['ni', 'lr'], ['t0'])
            TT('dve', s5['t1'][:], s5['nr'][:], s5['li'][:], ALU.mult, ['nr', 'li'], ['t1'])
            TT('dve', s5['ci'][:], s5['t0'][:], s5['t1'][:], ALU.subtract, ['t0', 't1'], ['ci'])
            TT('dve', s5['ci'][:], s5['ci'][:], s5['den'][:], ALU.mult, ['ci', 'den'], ['ci'])
            b16 = lambda t_: t_[:].rearrange("p (g o) -> p g o", o=1).to_broadcast([128, 16, 16])
            if stage < 2.4:
                return
            TT('dve', Bt['x'][:], Bt['bre'][:], b16(s5['cr']), ALU.mult, ['bre', 'cr'], ['Bx'])
            if stage < 2.5:
                return
            TT('dve', Bt['y'][:], Bt['bim'][:], b16(s5['ci']), ALU.mult, ['bim', 'ci'], ['By'])
            if stage < 2.6:
                return
            TT('dve', Bt['bbre'][:], Bt['x'][:], Bt['y'][:], ALU.subtract, ['Bx', 'By'], ['bbre'])
            if stage < 2.7:
                return
            TT('dve', Bt['x'][:], Bt['bim'][:], b16(s5['cr']), ALU.mult, ['bim', 'cr'], ['Bx'])
            if stage < 2.8:
                return
            TT('dve', Bt['y'][:], Bt['bre'][:], b16(s5['ci']), ALU.mult, ['bre', 'ci'], ['By'])
            if stage < 2.9:
                return
            TT('dve', Bt['bbim'][:], Bt['x'][:], Bt['y'][:], ALU.add, ['Bx', 'By'], ['bbim'])
            if stage < 3:
                return
            fl = lambda t_: t_[:].rearrange("p g j i -> p g (j i)")
            for qt in range(4):
                for gl in range(4):
                    gp = qt * 4 + gl
                    pw_b = lambda t_: t_[:, gp, :].rearrange("p (j o) -> p j o", o=1).to_broadcast([128, 8, 16])
                    bb_b = lambda t_: t_[:, gp, :].rearrange("p (o i) -> p o i", o=1).to_broadcast([128, 8, 16])
                    hx, hy = H['x'][:, gl], H['y'][:, gl]
                    TT('dve', hx, pw_b(s8['ipr']), bb_b(Bt['bbre']), ALU.mult, ['ipr', 'bbre'], ['Hx'])
                    TT('dve', hy, pw_b(s8['ipi']), bb_b(Bt['bbim']), ALU.mult, ['ipi', 'bbim'], ['Hy'])
                    TT('dve', H['hre'][:, gl], hx, hy, ALU.subtract, ['Hx', 'Hy'], ['hre'])
                    TT('dve', hx, pw_b(s8['ipr']), bb_b(Bt['bbim']), ALU.mult, ['ipr', 'bbim'], ['Hx'])
                    TT('dve', hy, pw_b(s8['ipi']), bb_b(Bt['bbre']), ALU.mult, ['ipi', 'bbre'], ['Hy'])
                    TT('dve', H['him'][:, gl], hx, hy, ALU.add, ['Hx', 'Hy'], ['him'])
                    TT('dve', hx, pw_b(s8['pwr']), bb_b(Bt['cre']), ALU.mult, ['pwr', 'cre'], ['Hx'])
                    TT('dve', hy, pw_b(s8['pwi']), bb_b(Bt['cim']), ALU.mult, ['pwi', 'cim'], ['Hy'])
                    TT('dve', H['gre'][:, gl], hx, hy, ALU.subtract, ['Hx', 'Hy'], ['gre'])
                    TT('dve', hx, pw_b(s8['pwi']), bb_b(Bt['cre']), ALU.mult, ['pwi', 'cre'], ['Hx'])
                    TT('dve', hy, pw_b(s8['pwr']), bb_b(Bt['cim']), ALU.mult, ['pwr', 'cim'], ['Hy'])
                    TT('dve', H['gim'][:, gl], hx, hy, ALU.add, ['Hx', 'Hy'], ['gim'])
                if stage < 3.1:
                    return
                p8 = lambda t_: t_[:, qt * 4:qt * 4 + 4, 7:8].to_broadcast([128, 4, 128])
                TT('dve', fl(H['x']), fl(H['hre']), p8(s8['pwr']), ALU.mult, ['hre', 'pwr'], ['Hx'])
                TT('dve', fl(H['y']), fl(H['him']), p8(s8['pwi']), ALU.mult, ['him', 'pwi'], ['Hy'])
                TT('dve', fl(H['h2re']), fl(H['x']), fl(H['y']), ALU.subtract, ['Hx', 'Hy'], ['h2re'])
                TT('dve', fl(H['x']), fl(H['him']), p8(s8['pwr']), ALU.mult, ['him', 'pwr'], ['Hx'])
                TT('dve', fl(H['y']), fl(H['hre']), p8(s8['pwi']), ALU.mult, ['hre', 'pwi'], ['Hy'])
                TT('dve', fl(H['h2im']), fl(H['x']), fl(H['y']), ALU.add, ['Hx', 'Hy'], ['h2im'])
                if stage < 3.2:
                    return
                TS('dve', fl(H['him']), fl(H['him']), -1.0, None, ALU.mult, None, ['him'], ['him'])
                if qt == 0:
                    S.op('dve', lambda e: e.memset(gqre[:], 0.0), writes=['gqre'])
                    S.op('dve', lambda e: e.memset(gqimn[:], 0.0), writes=['gqimn'])
                for par_ in range(2):
                    pp_ = slice(64 * par_, 64 * par_ + 64)
                    cc_ = slice(128 * par_, 128 * par_ + 128)
                    CP('dve', gqre[pp_, qt * 4:qt * 4 + 4, cc_], fl(H['gre'])[pp_], ['gre'], ['gqre'])
                    TS('dve', gqimn[pp_, qt * 4:qt * 4 + 4, cc_], fl(H['gim'])[pp_], -1.0, None, ALU.mult, None, ['gim'], ['gqimn'])
                if stage < 3.3:
                    return
                for gg in range(8):
                    g = qt * 8 + gg
                    gl, par = gg // 2, gg % 2
                    ps_ = slice(64 * par, 64 * par + 64)
                    b, bk = bank()
                    MM(b[:, 0:128], fl(H['hre'])[ps_, gl, :], fl(H['gre'])[ps_, gl, :], True, False, ['hre', 'gre'], [bk])
                    MM(b[:, 0:128], fl(H['him'])[ps_, gl, :], fl(H['gim'])[ps_, gl, :], False, True, ['him', 'gim'], [bk])
                    MM(b[:, 128:192], fl(H['h2re'])[ps_, gl, :], C['identf'][ps_, ps_], True, True, ['h2re', 'C'], [bk])
                    MM(b[:, 192:256], fl(H['h2im'])[ps_, gl, :], C['identf'][ps_, ps_], True, True, ['h2im', 'C'], [bk])
                    if stage < 3.4:
                        return
                    TT('dve', toepT[:, g, :], b[:, 0:128], C['toepmask'][:], ALU.mult, [bk, 'C'], ['toepT'])
                    if stage < 3.5:
                        return
                    ACT(ptre[:, g, :], b[:, 128:192], AF.Copy, [bk], ['ptre'])
                    ACT(ptim[:, g, :], b[:, 192:256], AF.Copy, [bk], ['ptim'])
            if stage < 4:
                return
            TS('dve', s5['t0'][:], s8['ang'][:, :, 7], 1.0 / (2 * math.pi), 32.0, ALU.mult, ALU.add, ['s8ang'], ['t0'])
            CP('dve', s8i[:, :, 0], s5['t0'][:], ['t0'], ['s8i'])
            CP('dve', s5['t1'][:], s8i[:, :, 0], ['s8i'], ['t1'])
            TT('dve', s5['phr'][:], s5['t0'][:], s5['t1'][:], ALU.subtract, ['t0', 't1'], ['phr'])
            TS('dve', s5['phr'][:], s5['phr'][:], 2 * math.pi, None, ALU.mult, None, ['phr'], ['phr'])
            TT('dve', s17['ang'][:], C['cidx'][:], s5['phr'][:].rearrange("p (g o) -> p g o", o=1).to_broadcast([128, 16, 17]), ALU.mult, ['C', 'phr'], ['s17ang'])
            sincos(s17['ang'], sinT, cosT, s17, s17i, 's17')
            CP('dve', rho[:], s8['mag'][:, :, 7], ['mag'], ['rho'])
            TT('dve', rhoS[:], C['rho0'][:], b16(rho), ALU.mult, ['C', 'rho'], ['rhoS'])
            if stage < 5:
                return
            for tp in range(4):
                for bq in range(12):
                    TS('dve', diag[:, tp, bq, :], C['identf'][:], cwcol[:, tp, bq:bq + 1], None, ALU.mult, None, ['C', 'cwcol'], ['diag'])
            if stage < 6:
                return
            S.end_capture()
            caps_setup['W'] = S.begin_capture()
            si = [0]

            def wscratch(srcap, chunk, kbslot, scale_col=None):
                k2 = si[0] % NSTG; si[0] += 1
                sb, sbb = stg[k2], stgb[k2]
                S.dma('act' if si[0] % 2 else 'sp', lambda e: e.dma_start(out=sb[:], in_=srcap), writes=['stg%d' % k2])
                if si[0] % 2:
                    if scale_col is not None:
                        TS('dve', sbb[:], sb[:], scale_col, None, ALU.mult, None, ['stg%d' % k2, 'gcol'], ['stgb%d' % k2])
                    else:
                        CP('dve', sbb[:], sb[:], ['stg%d' % k2], ['stgb%d' % k2])
                else:
                    if scale_col is not None:
                        ACT(sbb[:], sb[:], AF.Copy, ['stg%d' % k2, 'gcol'], ['stgb%d' % k2], scale=scale_col)
                    else:
                        ACT(sbb[:], sb[:], AF.Copy, ['stg%d' % k2], ['stgb%d' % k2])
                S.dma(['sp', 'act'][(si[0] // 2) % 2], lambda e: e.dma_start(out=wscr[chunk, kbslot], in_=sbb[:]), reads=['stgb%d' % k2], writes=['wscr%d_%d' % (chunk, kbslot)])
            for kb in range(8):
                rs = slice(kb * 128, (kb + 1) * 128)
                for cg in range(6):
                    wscratch(w_in[rs, cg * 512:(cg + 1) * 512], cg, kb, gcol[:, kb:kb + 1])
                for nh in range(2):
                    wscratch(w_out[rs, nh * 512:(nh + 1) * 512], 6 + nh, kb)
                    wscratch(w_pg[rs, nh * 512:(nh + 1) * 512], 8 + nh, kb)
                k2 = si[0] % NSTG; si[0] += 1
                S.dma('sp', lambda e, k2=k2, rs=rs: e.dma_start(out=stg[k2][:, 0:8], in_=w_in[rs, 3072:3080]), writes=['stg%d' % k2])
                TS('dve', wba[:, kb, :], stg[k2][:, 0:8], gcol[:, kb:kb + 1], None, ALU.mult, None, ['stg%d' % k2, 'gcol'], ['wba'])
            for kb in range(2):
                for nh in range(2):
                    wscratch(w_pp[kb * 128:(kb + 1) * 128, nh * 512:(nh + 1) * 512], 10, kb * 2 + nh)
            for kb in range(4):
                k2 = si[0] % NSTG; si[0] += 1
                S.dma('sp', lambda e, k2=k2, kb=kb: e.dma_start(out=stg[k2][:], in_=w_glu[kb * 128:(kb + 1) * 128, :]), writes=['stg%d' % k2])
                CP('dve', wglu[:, kb, :], stg[k2][:], ['stg%d' % k2], ['W'])

        do_setup()
        S.end_capture()
        S.issue_merged([caps_setup['W'], caps_setup['S']], grain=6)
        S.barrier()
        LN128H = -0.5 * math.log(128.0)
        sc = sc_
        gcum = None
        caps = []
        for ti in range(ntiles):
            tl = ti % NT_SEQ
            X = xt; xk = 'xt'
            r0 = ti * 128
            capA = S.begin_capture(); set_banks([3, 4, 5, 6])
            LD(X[:], x[r0:r0 + 128, :], [xk])
            S.op('act', lambda e, X=X: e.activation(out=xn[:], in_=X[:], func=AF.Square, accum_out=st[:, 0:1]), reads=[xk], writes=['xn', 'st0'])
            ACT(st[:, 1:2], st[:, 0:1], AF.Ln, ['st0'], ['st1'], bias=EPS, scale=1.0 / 1024)
            ACT(st[:, 1:2], st[:, 1:2], AF.Exp, ['st1'], ['st1'], scale=-0.5)
            TS('dve', xn[:], X[:], st[:, 1:2], None, ALU.mult, None, [xk, 'st1'], ['xn'])
            b, bk = bank()
            for kb in range(8):
                TR(psb(b, 8, 128)[:, kb, :], xn[:, kb * 128:(kb + 1) * 128], C['identb'][:], ['xn', 'C'], [bk])
            CP('dve', aT[:], psb(b, 8, 128), [bk], ['aT'])
            for grp in range(5):
                b, bk = bank()
                wsl, wk = wget(grp, 'A')
                for q4 in range(4):
                    cb = grp * 4 + q4
                    for kb in range(8):
                        MM(b[:, q4 * 128:(q4 + 1) * 128], wsl[:, kb, q4 * 128:(q4 + 1) * 128], aT[:, kb, :], kb == 0, kb == 7, [wk, 'aT'], [bk])
                v4 = psf(b, 4, 128)
                if grp == 0:
                    ACT(u32[:], v4, AF.Copy, [bk], ['u32'])
                    CP('dve', ubf[:], v4, [bk], ['ubf'])
                elif grp == 1:
                    ACT(zsf[:], v4, AF.Silu, [bk], ['zsf'])
                else:
                    o4 = qkvp[:, (grp - 2) * 4:(grp - 1) * 4, 3:131]
                    if grp % 2:
                        ACT(o4, v4, AF.Copy, [bk], ['qkvp'])
                    else:
                        CP('dve', o4, v4, [bk], ['qkvp'])
            b, bk = bank()
            wsl, wk = wget(5, 'A')
            for kb in range(8):
                MM(b[:, 0:512], aT[:, kb, :], wsl[:, kb, :], kb == 0, kb == 7, ['aT', wk], [bk])
            ACT(zd[:], psf(b, 4, 128), AF.Silu, [bk], ['zd'])
            bS, bSk = PS[7], 'ps7'
            for kb in range(8):
                MM(bS[:, 0:8], aT[:, kb, :], wba[:, kb, :], kb == 0, kb == 7, ['aT', 'wba'], [bSk])
            S.end_capture()
            capB = S.begin_capture(); set_banks([0, 1, 2])
            if tl == 0:
                S.op('pool', lambda e: e.memset(carr[:], 0.0), writes=['carr'])
            b, bk = bank()
            for q4 in range(4):
                TR(psb(b, 4, 128)[:, q4, :], ubf[:, q4, :], C['identb'][:], ['ubf', 'C'], [bk])
            CP('dve', utok[:], psb(b, 512), [bk], ['utok'])
            for j in range(8):
                b, bk = bank()
                MM(b[0:16, 0:512], C['selj'][:, j, :], utok[:], True, True, ['C', 'utok'], [bk])
                srcv = b[0:16, 0:512].rearrange("p (g i) -> p g i", g=32)
                if j % 2:
                    ACT(ucm[:, :, j, :], srcv, AF.Copy, [bk], ['cm'])
                else:
                    CP('dve', ucm[:, :, j, :], srcv, [bk], ['cm'])
            b, bk = bank()
            for g in range(32):
                TR(psb(b, 32, 16)[:, g, :], ucm[:, g].rearrange('p j i -> p (j i)'), C['identb'][0:16, 0:16], ['cm', 'C'], [bk])
            CP('dve', ug[:], psb(b, 32, 16), [bk], ['ug'])
            bR, bRk = bank(); bI, bIk = bank()
            vRr = psf(bR, 16, 2, 16); vRi = psf(bI, 16, 2, 16)
            for gp in range(16):
                rhs_ = ug[:, 2 * gp:2 * gp + 2, :].rearrange("p g c -> p (g c)")
                MM(vRr[:, gp].rearrange("p a c -> p (a c)"), ptre[:, 2 * gp:2 * gp + 2, :].rearrange("p g n -> p (g n)"), rhs_, True, True, ['ptre', 'ug'], [bRk])
                MM(vRi[:, gp].rearrange("p a c -> p (a c)"), ptim[:, 2 * gp:2 * gp + 2, :].rearrange("p g n -> p (g n)"), rhs_, True, True, ['ptim', 'ug'], [bIk])
            c1v, s1v = cosT[:, :, 1:17], sinT[:, :, 1:17]
            for par_ in range(2):
                pp_ = slice(64 * par_, 64 * par_ + 64)
                TT('dve', ztmp[pp_, 0], vRr[pp_, :, par_, :], c1v[pp_], ALU.mult, [bRk, 's17cs'], ['zt0'])
                TT('dve', ztmp[pp_, 1], vRi[pp_, :, par_, :], s1v[pp_], ALU.mult, [bIk, 's17sn'], ['zt1'])
                TT('dve', ztmp[pp_, 2], vRi[pp_, :, par_, :], c1v[pp_], ALU.mult, [bIk, 's17cs'], ['zt2'])
                TT('dve', ztmp[pp_, 3], vRr[pp_, :, par_, :], s1v[pp_], ALU.mult, [bRk, 's17sn'], ['zt3'])
            TT('dve', zin[:, 0], ztmp[:, 0], ztmp[:, 1], ALU.add, ['zt0', 'zt1'], ['zin'])
            TT('dve', zin[:, 1], ztmp[:, 2], ztmp[:, 3], ALU.subtract, ['zt2', 'zt3'], ['zin'])
            TT('dve', ctmp[:], carr[:], rho[:].rearrange("p (o g) -> p o g", o=1).to_broadcast([128, 2, 16]), ALU.mult, ['carr', 'rho'], ['ctmp'])
            TT('dve', zin[:, :, :, 0], zin[:, :, :, 0], ctmp[:], ALU.add, ['zin', 'ctmp'], ['zin'])
            for ri in range(2):
                S.op('dve', lambda e, ri=ri: e.tensor_tensor_scan(out=zs_[:, ri].rearrange("p g c -> p (g c)"), data0=rhoS[:].rearrange("p g c -> p (g c)"), data1=zin[:, ri].rearrange("p g c -> p (g c)"), initial=0.0, op0=ALU.mult, op1=ALU.add), reads=['zin', 'rhoS'], writes=['zs%d' % ri])
            TT('dve', ztmp[:, 0], zs_[:, 0], c1v, ALU.mult, ['zs0', 's17cs'], ['zt0'])
            TT('dve', ztmp[:, 1], zs_[:, 1], s1v, ALU.mult, ['zs1', 's17sn'], ['zt1'])
            TT('dve', ztmp[:, 2], zs_[:, 0], s1v, ALU.mult, ['zs0', 's17sn'], ['zt2'])
            TT('dve', ztmp[:, 3], zs_[:, 1], c1v, ALU.mult, ['zs1', 's17cs'], ['zt3'])
            TT('dve', snx[:, 0], ztmp[:, 0], ztmp[:, 1], ALU.subtract, ['zt0', 'zt1'], ['snx0'])
            TT('dve', snx[:, 1], ztmp[:, 2], ztmp[:, 3], ALU.add, ['zt2', 'zt3'], ['snx1'])
            CP('dve', sbf5[:, :, :, 0], carr[:], ['carr'], ['sbf5'])
            ACT(sbf5[:, :, :, 1:16], snx[:, :, :, 0:15], AF.Copy, ['snx0', 'snx1'], ['sbf5'])
            CP('dve', carr[:], snx[:, :, :, 15], ['snx0', 'snx1'], ['carr'])
            for g4 in range(8):
                b, bk = bank()
                for pi in range(2):
                    gp = g4 * 2 + pi
                    g0, g1 = 2 * gp, 2 * gp + 1
                    o_ = b[0:16, pi * 256:(pi + 1) * 256]
                    MMS(o_[:, 0:128], ug[:, g0, :], toepT[:, g0, :], True, False, ['ug', 'toepT'], [bk])
                    MMS(o_[:, 128:256], ug[:, g1, :], toepT[:, g1, :], False, False, ['ug', 'toepT'], [bk])
                    MMS(o_, sbf5[:, 0, gp, :], gqre[:, gp, :], False, False, ['sbf5', 'gqre'], [bk])
                    MMS(o_, sbf5[:, 1, gp, :], gqimn[:, gp, :], False, True, ['sbf5', 'gqimn'], [bk])
                src = b[0:16, 0:512].rearrange("p (g j o) -> p g j o", g=4, j=8)
                dst = ycm[:, :, g4 * 64:(g4 + 1) * 64].rearrange("p j (g o) -> p g j o", g=4)
                if g4 % 2:
                    ACT(dst, src, AF.Copy, [bk], ['cm'])
                else:
                    CP('dve', dst, src, [bk], ['cm'])
            b, bk = bank()
            vY = psb(b, 4, 8, 16)
            for q4 in range(4):
                for j in range(8):
                    TR(vY[:, q4, j, :], ycm[:, j, q4 * 128:(q4 + 1) * 128], C['identb'][0:16, 0:16], ['cm', 'C'], [bk])
            for q4 in range(4):
                STT(yfm[:, q4, :].rearrange("p (c j) -> p j c", j=8), u32[:, q4, :].rearrange("p (c j) -> p j c", j=8), Dcol[:, q4:q4 + 1], vY[:, q4], ALU.mult, ALU.add, ['u32', 'Dcol', bk], ['yfm'])
            if ti == 0:
                DBG('u32', u32[:], 'u32', [128, 4, 128], F32)
                DBG('yfm', yfm[:], 'yfm', [128, 4, 128], F32)
                DBG('zsf', zsf[:], 'zsf', [128, 4, 128], BF16)
            ACT(y1[:], yfm[:], AF.Gelu_apprx_tanh, ['yfm'], ['y1'])
            b, bk = bank()
            for mb in range(4):
                for kb in range(4):
                    MM(b[:, mb * 128:(mb + 1) * 128], wglu[:, kb, mb * 128:(mb + 1) * 128], y1[:, kb, :], kb == 0, kb == 3, ['W', 'y1'], [bk])
            for mb in range(4):
                ACT(gate[:, mb, :], b[:, mb * 128:(mb + 1) * 128], AF.Tanh, [bk, 'bglu'], ['gate'], bias=bglu[:, mb:mb + 1], scale=0.5)
            STT(gate[:], gate[:], 1.0, zsf[:], ALU.add, ALU.mult, ['gate', 'zsf'], ['gate'])
            STT(ysf[:], gate[:], 0.5, y1[:], ALU.mult, ALU.mult, ['gate', 'y1'], ['ysf'])
            if ti == 0:
                DBG('ysf', ysf[:], 'ysf', [128, 4, 128], BF16)
            S.end_capture()
            capC = S.begin_capture(); set_banks([3, 4, 5, 6])
            if tl == 0:
                S.op('pool', lambda e: e.memset(qkvp[:, :, 0:3], 0.0), writes=['halo'])
                S.op('pool', lambda e: e.memset(S32[:], 0.0), writes=['S32'])
                S.op('pool', lambda e: e.memset(Sbf[:], 0.0), writes=['Sbf'])
            for g3 in range(3):
                b, bk = bank()
                for q4 in range(4):
                    bq = g3 * 4 + q4
                    for tp in range(4):
                        MM(b[:, q4 * 128:(q4 + 1) * 128], diag[:, tp, bq, :], qkvp[:, bq, tp:tp + 128], tp == 0, tp == 3, ['diag', 'qkvp', 'halo'], [bk])
                ACT(qkv[:, g3 * 4:(g3 + 1) * 4, :], psf(b, 4, 128), AF.Silu, [bk], ['qkv'])
            CP('pool', qkvp[:, :, 0:3], qkvp[:, :, 128:131], ['qkvp'], ['halo'])
            ACT(sq[:], qkv[:, 0:8, :], AF.Square, ['qkv'], ['sq'])
            for bq in range(8):
                MM(bS[:, 8 + bq:9 + bq], sq[:, bq, :], C['onesb'][:, 0:1], True, True, ['sq', 'C'], [bSk])
            b, bk = bank()
            for i8 in range(8):
                TR(psb(b, 8, 128)[:, i8, :], qkv[:, 4 + i8, :], C['identb'][:], ['qkv', 'C'], [bk])
            CP('dve', kvt[:], psb(b, 8, 128), [bk], ['kvt'])
            CP('dve', ba[:], bS[:, 0:16], [bSk], ['ba'])
            ACT(sc['sigb'][:], ba[:, 0:4], AF.Sigmoid, ['ba'], ['sigb'])
            TT('dve', sc['tmp'][:], ba[:, 4:8], dtb_b[:], ALU.add, ['ba', 'dtb'], ['tmp'])
            ACT(sc['ex'][:], sc['tmp'][:], AF.Exp, ['tmp'], ['ex'])
            ACT(sc['lnb'][:], sc['sigb'][:], AF.Ln, ['sigb'], ['lnb'])
            ACT(sc['sp'][:], sc['ex'][:], AF.Ln, ['ex'], ['sp'], bias=1.0)
            ACT(lnr[:], ba[:, 8:16], AF.Ln, ['ba'], ['lnr'], bias=EPS)
            TT('dve', sc['g'][:], sc['sp'][:], negA_b[:], ALU.mult, ['sp', 'negA'], ['g'])
            TS('dve', lnr[:], lnr[:], -0.5, None, ALU.mult, None, ['lnr'], ['lnr'])
            for i4, nm in enumerate(['tri', 'blk', 'onesA', 'onesB']):
                MM(bS[:, 16 + 4 * i4:20 + 4 * i4], C[nm][:], sc['g'][:], True, True, ['C', 'g'], [bSk])
            CP('dve', cum[:], bS[:, 16:32], [bSk], ['cum'])
            gc_, gls_, glA_, glB_ = cum[:, 0:4], cum[:, 4:8], cum[:, 8:12], cum[:, 12:16]
            lq, lk = lnr[:, 0:4], lnr[:, 4:8]
            TT('dve', sc['c1'][:], lk, gc_, ALU.subtract, ['lnr', 'cum'], ['c1'])
            STT(sc['r1'][:], gc_, LN128H, lq, ALU.add, ALU.add, ['cum', 'lnr'], ['r1'])
            TT('dve', sc['r2'][:], gc_, sc['lnb'][:], ALU.add, ['cum', 'lnb'], ['r2'])
            TT('dve', sc['r2'][:], sc['r2'][:], lk, ALU.add, ['r2', 'lnr'], ['r2'])
            TT('dve', sc['tmp'][:], sc['c1'][:], gls_, ALU.add, ['c1', 'cum'], ['tmp2'])
            ACT(sc['kbgs'][:], sc['r2'][:], AF.Exp, ['r2'], ['kbgs'])
            ACT(sc['kds'][:], sc['tmp'][:], AF.Exp, ['tmp2'], ['kds'])
            ACT(sc['so'][:], sc['r1'][:], AF.Exp, ['r1'], ['so'])
            ACT(sc['eglA'][:], glA_, AF.Exp, ['cum'], ['eglA'])
            ACT(sc['eglB'][:], glB_, AF.Exp, ['cum'], ['eglB'])
            i2b = C['i2'][:].rearrange("p (o i) -> p o i", o=1).to_broadcast([128, 4, 64])
            hb = lambda t_, n: t_[:].rearrange("p (h o) -> p h o", o=1).to_broadcast([128, 4, n])
            for kd_, nm in enumerate(['r1', 'r2', 'c1']):
                TT('dve', RB[:, :, kd_, :], i2b, hb(sc[nm], 64), ALU.mult, ['C', nm], ['RB'])
            TT('pool', vb[:], kvt[:, 4:8, :], hb(sc['sigb'], 128), ALU.mult, ['kvt', 'sigb'], ['vb'])
            TT('pool', kbg[:], kvt[:, 0:4, :], hb(sc['kbgs'], 128), ALU.mult, ['kvt', 'kbgs'], ['kbg'])
            TT('dve', kd[:], kvt[:, 0:4, :], hb(sc['kds'], 128), ALU.mult, ['kvt', 'kds'], ['kd'])
            for hh in range(2):
                b, bk = bank()
                for h2_ in range(2):
                    h = 2 * hh + h2_
                    o_ = b[:, h2_ * 192:(h2_ + 1) * 192]
                    MM(o_, C['blk'][:], RB[:, h].rearrange("p k i -> p (k i)"), True, True, ['C', 'RB'], [bk])
                TT('dve', Bm[:, 2 * hh:2 * hh + 2].rearrange("p h k i -> p h (k i)"), b[:, 0:384].rearrange("p (h n) -> p h n", h=2), C['maskneg'][:].rearrange("p k i -> p (k i)").rearrange("p (o n) -> p o n", o=1).to_broadcast([128, 2, 192]), ALU.add, [bk, 'C'], ['Bm'])
                v_ = Bm[:, 2 * hh:2 * hh + 2]
                bk = 'Bm'
                for h2_ in range(2):
                    h = 2 * hh + h2_
                    ACT(Mx[:, h, 0:2, :], v_[:, h2_, 0:2, :], AF.Exp, [bk, 'c1'], ['Mx'], bias=sc['c1'][:, h:h + 1])
                    ACT(Mx[:, h, 2, :], v_[:, h2_, 2, :], AF.Exp, [bk, 'r2'], ['Mx'], bias=sc['r2'][:, h:h + 1])
            b, bk = bank()
            vS = psf(b, 4, 2, 64)
            for h in range(4):
                for ch in range(2):
                    ps_ = slice(64 * ch, 64 * ch + 64)
                    MM(vS[ps_, h, :, :], qkv[:, 4 + h, ps_], qkv[:, h:h + 5:4, ps_], True, True, ['qkv'], [bk])
            TT('dve', attnT[:], vS[:, :, 0, :], Mx[:, :, 0, :], ALU.mult, [bk, 'Mx'], ['attnT'])
            STT(NP[0][:, :, 0, :], vS[:, :, 1, :], -1.0, Mx[:, :, 1, :], ALU.mult, ALU.mult, [bk, 'Mx'], ['NP0a'])
            STT(NTt[0][:], vS[:, :, 1, :], -1.0, Mx[:, :, 2, :], ALU.mult, ALU.mult, [bk, 'Mx'], ['NT0'])
            TT('dve', NP[0][:, :, 1, :], NP[0][:, :, 0, :], C['i2'][:].rearrange("p (o i) -> p o i", o=1).to_broadcast([128, 4, 64]), ALU.add, ['NP0a', 'C'], ['NP0b'])
            for s in range(6):
                cur, nxt = s % 2, (s + 1) % 2
                NPc, NPn, NTc, NTn = NP[cur], NP[nxt], NTt[cur], NTt[nxt]
                kNa, kNb, kT = 'NP%da' % cur, 'NP%db' % cur, 'NT%d' % cur
                nNa, nNb, nT = 'NP%da' % nxt, 'NP%db' % nxt, 'NT%d' % nxt
                bA, bAk = bank()
                vA = psf(bA, 4, 128)
                for h in range(4):
                    for ch in range(2):
                        ps_ = slice(64 * ch, 64 * ch + 64)
                        if s == 0:
                            MM(vA[ps_, h, 0:64], NTc[ps_, h, :], NPc[ps_, h, 0, :], True, True, [kT, kNa], [bAk])
                        elif s < 5:
                            MM(vA[ps_, h, :], NTc[ps_, h, :], NPc[ps_, h, :, :].rearrange("p k i -> p (k i)"), True, True, [kT, kNa, kNb], [bAk])
                        else:
                            MM(vA[ps_, h, 64:128], NTc[ps_, h, :], NPc[ps_, h, 1, :], True, True, [kT, kNb], [bAk])
                if s < 5:
                    bB, bBk = bank()
                    vB = psf(bB, 4, 64)
                    for h in range(4):
                        for ch in range(2):
                            ps_ = slice(64 * ch, 64 * ch + 64)
                            MM(vB[ps_, h, :], NPc[ps_, h, 0, :], NTc[ps_, h, :], True, True, [kNa, kT], [bBk])
                    ACT(NPn[:, :, 0, :], vA[:, :, 0:64], AF.Copy, [bAk], [nNa])
                    ACT(NTn[:], vB, AF.Copy, [bBk], [nT])
                if s == 0:
                    CP('dve', NPn[:, :, 1, :], NPc[:, :, 1, :], [kNb], [nNb])
                elif s < 5:
                    TT('dve', NPn[:, :, 1, :], NPc[:, :, 1, :], vA[:, :, 64:128], ALU.add, [kNb, bAk], [nNb])
                else:
                    TT('dve', Pf[:], NPc[:, :, 1, :], vA[:, :, 64:128], ALU.add, [kNb, bAk], ['Pf'])
            bU, bUk = bank(); bW, bWk = bank()
            vU = psf(bU, 4, 128); vW = psf(bW, 4, 2, 64)
            for h in range(4):
                for ch in range(2):
                    ps_ = slice(64 * ch, 64 * ch + 64)
                    MM(vU[ps_, h, :], Pf[ps_, h, :], vb[ps_, h, :], True, True, ['Pf', 'vb'], [bUk])
                    MM(vW[:, h, ch, :], kbg[ps_, h, :], Pf[ps_, h, :], True, True, ['kbg', 'Pf'], [bWk])
            ACT(ug32[:], vU, AF.Copy, [bUk], ['ug32'])
            CP('dve', wT[:], vW, [bWk], ['wT'])
            b1, b1k = bank(); b2, b2k = bank(); b3, b3k = bank()
            v1, v2, v3 = psf(b1, 4, 128), psf(b2, 4, 128), psf(b3, 4, 128)
            for ch in range(2):
                ps_ = slice(64 * ch, 64 * ch + 64)
                egl = sc['eglA'] if ch == 0 else sc['eglB']
                eglk = 'eglA' if ch == 0 else 'eglB'
                for h in range(4):
                    MM(v1[ps_, h, :], wT[:, h, ch, :], Sbf[:, h, :], True, True, ['wT', 'Sbf'], [b1k])
                    MM(v2[ps_, h, :], qkv[:, h, ps_], Sbf[:, h, :], True, True, ['qkv', 'Sbf'], [b2k])
                TT('dve', vnew[ps_], ug32[ps_], v1[ps_], ALU.subtract, ['ug32', b1k], ['vnew'])
                bS2, bS2k = bank()
                vS2 = psf(bS2, 4, 128)
                for h in range(4):
                    MM(v3[ps_, h, :], attnT[ps_, h, :], vnew[ps_, h, :], True, True, ['attnT', 'vnew'], [b3k])
                    MM(vS2[:, h, :], kd[ps_, h, :], vnew[ps_, h, :], True, True, ['kd', 'vnew'], [bS2k])
                TT('dve', S32[:], S32[:], hb(egl, 128), ALU.mult, ['S32', eglk], ['S32'])
                TT('dve', S32[:], S32[:], vS2, ALU.add, ['S32', bS2k], ['S32'])
                ACT(Sbf[:], S32[:], AF.Copy, ['S32'], ['Sbf'])
                TT('dve', otmp[ps_], v2[ps_], sc['so'][ps_].rearrange("p (h o) -> p h o", o=1).to_broadcast([64, 4, 128]), ALU.mult, [b2k, 'so'], ['otmp'])
                TT('dve', otok[ps_], otmp[ps_], v3[ps_], ALU.add, ['otmp', b3k], ['otok'])
            if ti == 0:
                DBG('qkv', qkv[:], 'qkv', [128, 12, 128], BF16)
                DBG('otok', otok[:], 'otok', [128, 4, 128], F32)
            for h in range(4):
                S.op('act', lambda e, h=h: e.activation(out=t1[:, h, :], in_=otok[:, h, :], func=AF.Square, accum_out=ssd[:, h:h + 1]), reads=['otok'], writes=['t1', 'ssd'])
            ACT(rsd[:], ssd[:], AF.Ln, ['ssd'], ['rsd'], bias=EPS, scale=1.0 / 128)
            ACT(rsd[:], rsd[:], AF.Exp, ['rsd'], ['rsd'], scale=-0.5)
            TT('dve', t1[:], otok[:], dng_b[:].rearrange("p (o d) -> p o d", o=1).to_broadcast([128, 4, 128]), ALU.mult, ['otok', 'dng'], ['t1'])
            TT('dve', t1[:], t1[:], zd[:], ALU.mult, ['t1', 'zd'], ['t1'])
            TT('dve', ydt[:], t1[:], hb(rsd, 128), ALU.mult, ['t1', 'rsd'], ['ydt'])
            b, bk = bank()
            for h in range(4):
                TR(psb(b, 4, 128)[:, h, :], ydt[:, h, :], C['identb'][:], ['ydt', 'C'], [bk])
            CP('dve', ydf[:], psb(b, 4, 128), [bk], ['ydf'])
            S.end_capture()
            capD = S.begin_capture(); set_banks([0, 1, 2])
            LD(h1[:], x[r0:r0 + 128, :], ['h1'])
            LD(pt[:], p[r0:r0 + 128, :], ['pt'])
            for nh in range(2):
                b, bk = bank()
                wsl, wk = wget(6 + nh, 'D')
                for kb in range(8):
                    l_ = ysf[:, kb, :] if kb < 4 else ydf[:, kb - 4, :]
                    MM(b[:, 0:512], l_, wsl[:, kb, :], kb == 0, kb == 7, ['ysf', 'ydf', wk], [bk])
                TT('dve', h1[:, nh * 512:(nh + 1) * 512], h1[:, nh * 512:(nh + 1) * 512], b[:, 0:512], ALU.add, ['h1', bk], ['h1'])
            if ti == 0:
                DBG('ydf', ydf[:], 'ydf', [128, 4, 128], BF16)
                DBG('h1', h1[:], 'h1', [128, 1024], F32)
            CP('pool', ptb[:], pt[:], ['pt'], ['ptb'])
            b, bk = bank()
            for kb in range(2):
                TR(psb(b, 2, 128)[:, kb, :], ptb[:, kb * 128:(kb + 1) * 128], C['identb'][:], ['ptb', 'C'], [bk])
            CP('dve', pT[:], psb(b, 2, 128), [bk], ['pT'])
            be = []
            wsl, wk = wget(10, 'D')
            for nh in range(2):
                b, bk = bank()
                be.append((b, bk))
                for kb in range(2):
                    MM(b[:, 0:512], pT[:, kb, :], wsl[:, kb * 2 + nh, :], kb == 0, kb == 1, ['pT', wk], [bk])
                S.op('act', lambda e, b=b, nh=nh: e.activation(out=ee[:, nh * 512:(nh + 1) * 512], in_=b[:, 0:512], func=AF.Square, accum_out=st[:, 2 + nh:3 + nh]), reads=[bk], writes=['ee', 'st2'])
            TT('dve', st[:, 4:5], st[:, 2:3], st[:, 3:4], ALU.add, ['st2'], ['st4'])
            ACT(st[:, 4:5], st[:, 4:5], AF.Ln, ['st4'], ['st4'], bias=EPS, scale=1.0 / 1024)
            ACT(st[:, 4:5], st[:, 4:5], AF.Exp, ['st4'], ['st4'], scale=-0.5)
            for nh in range(2):
                b, bk = be[nh]
                STT(ee[:, nh * 512:(nh + 1) * 512], b[:, 0:512], st[:, 4:5], pleg_b[:, nh * 512:(nh + 1) * 512], ALU.mult, ALU.mult, [bk, 'st4', 'pleg'], ['ee'])
            ACT(h1b[:], h1[:], AF.Copy, ['h1'], ['h1b'])
            b, bk = bank()
            for kb in range(8):
                TR(psb(b, 8, 128)[:, kb, :], h1b[:, kb * 128:(kb + 1) * 128], C['identb'][:], ['h1b', 'C'], [bk])
            CP('dve', h1T[:], psb(b, 8, 128), [bk], ['h1T'])
            for nh in range(2):
                b, bk = bank()
                wsl, wk = wget(8 + nh, 'D')
                for kb in range(8):
                    MM(b[:, 0:512], h1T[:, kb, :], wsl[:, kb, :], kb == 0, kb == 7, ['h1T', wk], [bk])
                ACT(sg[:, nh * 512:(nh + 1) * 512], b[:, 0:512], AF.Sigmoid, [bk], ['sg'])
            TT('dve', sg[:], sg[:], ee[:], ALU.mult, ['sg', 'ee'], ['sg'])
            TT('dve', h1[:], h1[:], sg[:], ALU.add, ['h1', 'sg'], ['h1'])
            S.op('act', lambda e: e.activation(out=sg[:], in_=h1[:], func=AF.Square, accum_out=st[:, 5:6]), reads=['h1'], writes=['sg', 'st5'])
            ACT(st[:, 6:7], st[:, 5:6], AF.Ln, ['st5'], ['st6'], bias=EPS, scale=1.0 / 1024)
            ACT(st[:, 6:7], st[:, 6:7], AF.Exp, ['st6'], ['st6'], scale=-0.5)
            O = ee; ok = 'ee'
            STT(O[:], h1[:], st[:, 6:7], fing_b[:], ALU.mult, ALU.mult, ['h1', 'st6', 'fing'], [ok])
            S.dma('act', lambda e, O=O, r0=r0: e.dma_start(out=out[r0:r0 + 128, :], in_=O[:]), reads=[ok], writes=['out%d' % ti])
            S.end_capture()
            caps.append((capA, capB, capC, capD))
        set_banks([0, 1, 2, 3, 4, 5, 6])
        if caps:
            S.issue_merged([caps[0][0]])
        for ti in range(ntiles):
            S.issue_merged([caps[ti][1], caps[ti][2]], grain=20, lead=[0.0, 0.2])
            S.issue_merged([caps[ti][3]] + ([caps[ti + 1][0]] if ti + 1 < ntiles else []), lead=[0.2, 0.0])
        S.finish('sp', ['out%d' % ti for ti in range(ntiles)])
        S.finish('act', ['out%d' % ti for ti in range(ntiles)])
        S.finish('sp', dbg_outs)
        S.drain_all('sp')
        S.replay(block)
    return nc


_CACHE = {}


def kernel(**inputs):
    x = np.ascontiguousarray(inputs['x'], dtype=np.float32)
    p = np.ascontiguousarray(inputs['p'], dtype=np.float32)[0]
    B = x.shape[0]
    per = B // NCORES
    consts = host_consts()
    shared = {}
    for k in ['norm_mix_g', 'w_in', 'ssm_A_re', 'ssm_A_im', 'ssm_B_re', 'ssm_B_im', 'ssm_C_re', 'ssm_C_im',
              'ssm_D', 'ssm_log_dt', 'ssm_w_glu', 'ssm_b_glu', 'dn_conv_w', 'dn_A_log', 'dn_dt_bias',
              'dn_norm_g', 'w_out', 'w_ple_proj', 'ple_norm_g', 'w_ple_gate']:
        shared[k] = np.ascontiguousarray(np.asarray(inputs[k], dtype=np.float32)[0])
    shared['final_norm_g'] = np.ascontiguousarray(inputs['final_norm_g'], dtype=np.float32)
    for k, v in consts.items():
        shared['c_' + k] = v
    if 'nc' not in _CACHE:
        _CACHE['nc'] = build_nc()
    nc = _CACHE['nc']
    in_maps = []
    for c in range(NCORES):
        m = dict(shared)
        m['x'] = x[c * per:(c + 1) * per].reshape(per * SEQ, 1024)
        m['p'] = p[c * per:(c + 1) * per].reshape(per * SEQ, 256)
        in_maps.append(m)
    res = run_bass_kernel_spmd(nc, in_maps, core_ids=list(range(NCORES)))
    outs = [np.asarray(r['out']).reshape(per, SEQ, 1024) for r in res.results]
    return np.concatenate(outs, axis=0).astype(np.float32)
```
